# Optimizing a Trainium2 kernel written in Bass

```python
import jax
import jax.numpy as jnp
from jax import lax
import numpy as np

D_MODEL = 1024
BATCH = 2
SEQ = 8192
DEPTH = 4

CTX_LEN = 256
GRID_W = 64
EPS = 1e-6
F32 = jnp.float32

ATT_HEADS = 8
ATT_KV_HEADS = 2
ATT_GROUP = ATT_HEADS // ATT_KV_HEADS
HEAD_DIM = 64
WINDOW = 128
WBLK = 128
ROPE_BASE = 10000.0
ROPE_AXIS_DIM = HEAD_DIM // 2
ROPE_FREQS = ROPE_AXIS_DIM // 2

GLA_HEADS = 4
GLA_DK = 64
GLA_DV = 128
GLA_RANK = 16
GLA_TAU = 16.0

RET_HEADS = 4
RET_DK = 64
RET_DV = 128

CHUNK = 64
N_BRANCH = 3
BRANCH_W = ATT_HEADS * HEAD_DIM

D_FF = 2816
CONV_W = 3

IN_SIZES = (
    ATT_HEADS * HEAD_DIM, ATT_KV_HEADS * HEAD_DIM, ATT_KV_HEADS * HEAD_DIM,
    GLA_HEADS * GLA_DK, GLA_HEADS * GLA_DK, GLA_HEADS * GLA_DV, GLA_HEADS * GLA_DV,
    2 * GLA_RANK,
    RET_HEADS * RET_DK, RET_HEADS * RET_DK, RET_HEADS * RET_DV, RET_HEADS * RET_DV,
    N_BRANCH * D_MODEL,
)
D_IN = sum(IN_SIZES)

kernel_name = 'hybrid_parallel_mixer_dit'


def heads(a, n):
    return a.reshape(*a.shape[:-1], n, a.shape[-1] // n)


def rms_norm(a, g):
    af = a.astype(F32)
    y = af * lax.rsqrt(jnp.mean(af * af, axis=-1, keepdims=True) + EPS)
    return (y * g.astype(F32)).astype(a.dtype)


def split_in(z):
    idx = [int(i) for i in np.cumsum(IN_SIZES)[:-1]]
    return jnp.split(z, idx, axis=-1)


def axial_rope_tables(n_tokens):
    rows = n_tokens // GRID_W
    row = jnp.repeat(jnp.arange(rows, dtype=F32), GRID_W)
    col = jnp.tile(jnp.arange(GRID_W, dtype=F32), rows)
    inv = ROPE_BASE ** (-jnp.arange(ROPE_FREQS, dtype=F32) * 2.0 / ROPE_AXIS_DIM)
    ang = jnp.stack([row[:, None] * inv, col[:, None] * inv], axis=1)
    return jnp.cos(ang), jnp.sin(ang)


def axial_rope(a, cos, sin):
    af = a.astype(F32).reshape(*a.shape[:-1], 2, 2, ROPE_FREQS)
    a1, a2 = af[..., 0, :], af[..., 1, :]
    c = cos[None, :, None]
    s = sin[None, :, None]
    out = jnp.stack([a1 * c - a2 * s, a2 * c + a1 * s], axis=-2)
    return out.reshape(a.shape).astype(a.dtype)


def window_attention(q, k, v, kc, vc, sink):
    B, T, H, D = q.shape
    L = kc.shape[1]
    nb = T // WBLK
    scale = D ** -0.5
    qb = q.reshape(B, nb, WBLK, ATT_KV_HEADS, ATT_GROUP, D)

    def band(a):
        ap = jnp.pad(a, ((0, 0), (WBLK, WBLK), (0, 0), (0, 0)))
        return jnp.concatenate(
            [ap[:, j * WBLK:j * WBLK + T].reshape(B, nb, WBLK, ATT_KV_HEADS, D) for j in range(3)], axis=2)

    kb, vb = band(k), band(v)
    slot = jnp.arange(3 * WBLK)
    rel = slot[None, :] - WBLK - jnp.arange(WBLK)[:, None]
    kpos = jnp.arange(nb)[:, None] * WBLK - WBLK + slot[None, :]
    mask = (jnp.abs(rel) <= WINDOW)[None] & ((kpos >= 0) & (kpos < T))[:, None, :]

    s_loc = jnp.einsum('bnihgd,bnshd->bnhgis', qb, kb).astype(F32) * scale
    s_loc = jnp.where(mask[None, :, None, None], s_loc, -jnp.inf)
    s_ctx = jnp.einsum('bnihgd,bchd->bnhgic', qb, kc).astype(F32) * scale
    s_sink = jnp.broadcast_to(sink.astype(F32).reshape(1, 1, ATT_KV_HEADS, ATT_GROUP, 1, 1),
                              s_loc.shape[:-1] + (1,))
    p = jax.nn.softmax(jnp.concatenate([s_loc, s_ctx, s_sink], axis=-1), axis=-1).astype(v.dtype)
    o = (jnp.einsum('bnhgis,bnshd->bnihgd', p[..., :3 * WBLK], vb)
         + jnp.einsum('bnhgic,bchd->bnihgd', p[..., 3 * WBLK:3 * WBLK + L], vc))
    return o.reshape(B, T, H * D)


def context_attention(qc, kc, vc, sink):
    B, L, H, D = qc.shape
    qg = qc.reshape(B, L, ATT_KV_HEADS, ATT_GROUP, D)
    s = jnp.einsum('bihgd,bjhd->bhgij', qg, kc).astype(F32) * D ** -0.5
    s_sink = jnp.broadcast_to(sink.astype(F32).reshape(1, ATT_KV_HEADS, ATT_GROUP, 1, 1), s.shape[:-1] + (1,))
    p = jax.nn.softmax(jnp.concatenate([s, s_sink], axis=-1), axis=-1).astype(vc.dtype)
    o = jnp.einsum('bhgij,bjhd->bihgd', p[..., :L], vc)
    return o.reshape(B, L, H * D)


def chunk_scan(q, k, v, g, s0):
    B, T, H, K = q.shape
    V = v.shape[-1]
    n = T // CHUNK
    q, k, v, g = (a.astype(F32).reshape(B, n, CHUNK, H, a.shape[-1]) for a in (q, k, v, g))
    b = jnp.cumsum(g, axis=2)
    b_last = b[:, :, -1]
    q_in = q * jnp.exp(b)
    k_in = k * jnp.exp(-b)
    lower = jnp.tril(jnp.ones((CHUNK, CHUNK), dtype=bool))
    att = jnp.where(lower, jnp.einsum('bnihk,bnjhk->bnhij', q_in, k_in), 0.0)
    o_intra = jnp.einsum('bnhij,bnjhv->bnihv', att, v)
    k_end = k * jnp.exp(b_last[:, :, None] - b)
    ds = jnp.einsum('bnjhk,bnjhv->bnhkv', k_end, v)

    def step(S, inp):
        dec, d = inp
        return S * jnp.exp(dec)[..., None] + d, S

    s_fin, s_start = lax.scan(step, s0.astype(F32), (jnp.moveaxis(b_last, 1, 0), jnp.moveaxis(ds, 1, 0)))
    s_start = jnp.moveaxis(s_start, 0, 1)
    o = o_intra + jnp.einsum('bnihk,bnhkv->bnihv', q_in, s_start)
    return o.reshape(B, T, H, V), s_fin


def bidir_scan(lat, ctx_in):
    q, k, v, gf, gb = lat
    qc, kc, vc, gcf, gcb = ctx_in
    B, _, H, K = q.shape
    zero = jnp.zeros((B, H, K, v.shape[-1]), F32)
    oc_f, sc_f = chunk_scan(qc, kc, vc, gcf, zero)
    oc_b, sc_b = chunk_scan(*(jnp.flip(a, 1) for a in (qc, kc, vc, gcb)), zero)
    o_f, _ = chunk_scan(q, k, v, gf, sc_f)
    o_b, _ = chunk_scan(*(jnp.flip(a, 1) for a in (q, k, v, gb)), sc_b)
    return o_f + jnp.flip(o_b, 1), oc_f + jnp.flip(oc_b, 1)


def conv_ffn(h, w_up, conv_w, conv_b, w_down):
    u = h @ w_up
    u = lax.conv_general_dilated(u, conv_w[:, None, :], window_strides=(1,), padding=((1, 1),),
                                 dimension_numbers=('NWC', 'WIO', 'NWC'),
                                 feature_group_count=u.shape[-1]) + conv_b
    a, bv = jnp.split(u, 2, axis=-1)
    return (jax.nn.silu(a) * bv) @ w_down


def token_mixers(h, hc, w_in, q_norm_g, k_norm_g, sink, gla_gate_w, gla_gate_b, gla_norm_g,
                 ret_log_decay, ret_norm_g, w_branch, w_out, cos, sin, with_ctx):
    (aq, ak, av, gq, gk, gv, gr, ga, rq, rk, rv, rg, mg) = split_in(h @ w_in)
    (caq, cak, cav, cgq, cgk, cgv, cgr, cga, crq, crk, crv, crg, cmg) = split_in(hc @ w_in)

    qa = axial_rope(rms_norm(heads(aq, ATT_HEADS), q_norm_g), cos, sin)
    ka = axial_rope(rms_norm(heads(ak, ATT_KV_HEADS), k_norm_g), cos, sin)
    kca = rms_norm(heads(cak, ATT_KV_HEADS), k_norm_g)
    vca = heads(cav, ATT_KV_HEADS)
    o_a = window_attention(qa, ka, heads(av, ATT_KV_HEADS), kca, vca, sink)

    def gla_inputs(q, k, v, a):
        lr = heads(a, 2)
        log_a = jax.nn.log_sigmoid(
            (jnp.einsum('btzr,zrk->btzk', lr, gla_gate_w) + gla_gate_b).astype(F32)) / GLA_TAU
        return (heads(q, GLA_HEADS) * GLA_DK ** -0.5, heads(k, GLA_HEADS), heads(v, GLA_HEADS),
                heads(log_a[:, :, 0], GLA_HEADS), heads(log_a[:, :, 1], GLA_HEADS))

    o_g, oc_g = bidir_scan(gla_inputs(gq, gk, gv, ga), gla_inputs(cgq, cgk, cgv, cga))

    log_gamma = -jnp.exp(ret_log_decay.astype(F32))

    def ret_inputs(q, k, v, rotate):
        q = heads(q, RET_HEADS)
        k = heads(k, RET_HEADS) * RET_DK ** -0.5
        if rotate:
            q, k = axial_rope(q, cos, sin), axial_rope(k, cos, sin)
        return (q, k, heads(v, RET_HEADS),
                jnp.broadcast_to(log_gamma[0][:, None], q.shape),
                jnp.broadcast_to(log_gamma[1][:, None], q.shape))

    o_r, oc_r = bidir_scan(ret_inputs(rq, rk, rv, True), ret_inputs(crq, crk, crv, False))

    def gated_out(o, g, norm_g, n):
        y = rms_norm(o, norm_g) * jax.nn.silu(heads(g, n)).astype(F32)
        return y.reshape(g.shape).astype(g.dtype)

    def merge(oa, og, orr, m):
        ys = jnp.einsum('btzw,zwd->btzd', jnp.stack([oa, og, orr], axis=2), w_branch)
        return jnp.sum(jax.nn.sigmoid(heads(m, N_BRANCH)) * ys, axis=2) @ w_out

    y = merge(o_a, gated_out(o_g, gr, gla_norm_g, GLA_HEADS), gated_out(o_r, rg, ret_norm_g, RET_HEADS), mg)
    if not with_ctx:
        return y, None
    qca = rms_norm(heads(caq, ATT_HEADS), q_norm_g)
    yc = merge(context_attention(qca, kca, vca, sink),
               gated_out(oc_g, cgr, gla_norm_g, GLA_HEADS),
               gated_out(oc_r, crg, ret_norm_g, RET_HEADS), cmg)
    return y, yc


def setup_inputs(seed: int = 0) -> dict:
    key = jax.random.key(seed)
    ks = jax.random.split(key, 24)
    D = D_MODEL

    def nrm(k, shape, scale):
        return jax.random.normal(k, shape, F32) * scale

    ret_base = jnp.log(-jnp.log(1.0 - 2.0 ** (-5.0 - jnp.arange(RET_HEADS, dtype=F32))))
    return {
        'x': nrm(ks[0], (BATCH, SEQ, D), 1.0),
        'c': nrm(ks[1], (BATCH, D), 1.0),
        'ctx': nrm(ks[2], (BATCH, CTX_LEN, D), 1.0),
        'c_ctx': nrm(ks[3], (D,), 1.0),
        'ada_w': nrm(ks[4], (DEPTH, D, 6 * D), 0.5 * D ** -0.5),
        'ada_b': nrm(ks[5], (DEPTH, 6 * D), 0.02),
        'norm1_g': 1.0 + nrm(ks[6], (DEPTH, D), 0.05),
        'norm2_g': 1.0 + nrm(ks[7], (DEPTH, D), 0.05),
        'w_in': nrm(ks[8], (DEPTH, D, D_IN), D ** -0.5),
        'attn_q_norm_g': 1.0 + nrm(ks[9], (DEPTH, HEAD_DIM), 0.05),
        'attn_k_norm_g': 1.0 + nrm(ks[10], (DEPTH, HEAD_DIM), 0.05),
        'attn_sink': nrm(ks[11], (DEPTH, ATT_HEADS), 0.5),
        'gla_gate_w': nrm(ks[12], (DEPTH, 2, GLA_RANK, GLA_HEADS * GLA_DK), GLA_RANK ** -0.5),
        'gla_gate_b': nrm(ks[13], (DEPTH, 2, GLA_HEADS * GLA_DK), 0.1),
        'gla_out_norm_g': 1.0 + nrm(ks[14], (DEPTH, GLA_DV), 0.05),
        'ret_log_decay': ret_base[None, None, :] + nrm(ks[15], (DEPTH, 2, RET_HEADS), 0.1),
        'ret_out_norm_g': 1.0 + nrm(ks[16], (DEPTH, RET_DV), 0.05),
        'w_branch': nrm(ks[17], (DEPTH, N_BRANCH, BRANCH_W, D), BRANCH_W ** -0.5),
        'w_out': nrm(ks[18], (DEPTH, D, D), D ** -0.5),
        'ffn_up': nrm(ks[19], (DEPTH, D, 2 * D_FF), D ** -0.5),
        'ffn_conv_w': nrm(ks[20], (DEPTH, CONV_W, 2 * D_FF), CONV_W ** -0.5),
        'ffn_conv_b': nrm(ks[21], (DEPTH, 2 * D_FF), 0.02),
        'ffn_down': nrm(ks[22], (DEPTH, D_FF, D), D_FF ** -0.5),
    }


def reference(x, c, ctx, c_ctx, ada_w, ada_b, norm1_g, norm2_g, w_in, attn_q_norm_g, attn_k_norm_g,
              attn_sink, gla_gate_w, gla_gate_b, gla_out_norm_g, ret_log_decay, ret_out_norm_g,
              w_branch, w_out, ffn_up, ffn_conv_w, ffn_conv_b, ffn_down):
    cos, sin = axial_rope_tables(x.shape[1])
    for l in range(DEPTH):
        with_ctx = l < DEPTH - 1
        mod = (jax.nn.silu(c) @ ada_w[l] + ada_b[l])[:, None, :]
        mod_c = jax.nn.silu(c_ctx) @ ada_w[l] + ada_b[l]
        sh1, sc1, gt1, sh2, sc2, gt2 = jnp.split(mod, 6, axis=-1)
        csh1, csc1, cgt1, csh2, csc2, cgt2 = jnp.split(mod_c, 6, axis=-1)

        h = rms_norm(x, norm1_g[l]) * (1.0 + sc1) + sh1
        hc = rms_norm(ctx, norm1_g[l]) * (1.0 + csc1) + csh1
        y, yc = token_mixers(h, hc, w_in[l], attn_q_norm_g[l], attn_k_norm_g[l], attn_sink[l],
                             gla_gate_w[l], gla_gate_b[l], gla_out_norm_g[l], ret_log_decay[l],
                             ret_out_norm_g[l], w_branch[l], w_out[l], cos, sin, with_ctx)
        x = x + gt1 * y
        h2 = rms_norm(x, norm2_g[l]) * (1.0 + sc2) + sh2
        x = x + gt2 * conv_ffn(h2, ffn_up[l], ffn_conv_w[l], ffn_conv_b[l], ffn_down[l])
        if with_ctx:
            ctx = ctx + cgt1 * yc
            hc2 = rms_norm(ctx, norm2_g[l]) * (1.0 + csc2) + csh2
            ctx = ctx + cgt2 * conv_ffn(hc2, ffn_up[l], ffn_conv_w[l], ffn_conv_b[l], ffn_down[l])
    return x
```

```python
import numpy as np
import ml_dtypes
import concourse.bass as bass
import concourse.mybir as mybir
from concourse.bass_utils import run_bass_kernel_spmd

F32 = mybir.dt.float32
BF16 = mybir.dt.bfloat16
AF = mybir.ActivationFunctionType
ALU = mybir.AluOpType
AX = mybir.AxisListType

D = 1024
B = 2
T = 8192
L = 256
DEPTH = 4
NCORE = 8
TS = T // 4
LS = L // 4
NT = TS + LS
TB = T + L
D_FF = 2816
D_IN = 6944
EPS = 1e-6
GRID_W = 64

OFF = {}
_o = 0
for _n, _s in (("aq", 512), ("ak", 128), ("av", 128), ("gq", 256), ("gk", 256), ("gv", 512), ("gr", 512),
               ("ga", 32), ("rq", 256), ("rk", 256), ("rv", 512), ("rg", 512), ("mg", 3072)):
    OFF[_n] = (_o, _s)
    _o += _s
assert _o == D_IN


class Buf:
    __slots__ = ("name", "w", "r")

    def __init__(self, name):
        self.name = name
        self.w = None
        self.r = {}


class Tile:
    def __init__(self, t, name):
        self.t = t
        self.b = Buf(name)

    def __getitem__(self, idx):
        return self.t[idx]


class Sched:
    def __init__(self, nc):
        self.nc = nc
        self.eng = {"pe": nc.tensor, "act": nc.scalar, "dve": nc.vector, "pool": nc.gpsimd, "sp": nc.sync}
        self.sem = {}
        self.cnt = {}
        self.seen = {k: {} for k in self.eng}
        self.nwait = 0
        self.nins = 0
        for k in ("pe", "act", "dve", "pool"):
            self.sem[k] = nc.alloc_semaphore("s_" + k)
            self.cnt[k] = 0
        self._uid = 0

    def sbuf(self, name, shape, dtype):
        return Tile(self.nc.alloc_sbuf_tensor("sb_" + name, list(shape), dtype), name)

    def psum(self, name, shape, dtype=F32):
        return Tile(self.nc.alloc_psum_tensor("pp_" + name, list(shape), dtype), name)

    @staticmethod
    def _b(x):
        return x.b if isinstance(x, Tile) else x

    def _deps(self, ek, reads, writes):
        deps = {}

        def add(m):
            if m is None:
                return
            k, v = m
            if deps.get(k, 0) < v:
                deps[k] = v
        for b in reads:
            add(self._b(b).w)
        for b in writes:
            b = self._b(b)
            add(b.w)
            for k, v in b.r.items():
                add((k, v))
        if ek == "pe":
            deps.pop("pe", None)
        e = self.eng[ek]
        seen = self.seen[ek]
        for k, v in deps.items():
            if seen.get(k, 0) < v:
                e.wait_ge(self.sem[k], v)
                seen[k] = v
                self.nwait += 1

    def _mark(self, mark, reads, writes):
        k, v = mark
        for b in writes:
            b = self._b(b)
            b.w = mark
            b.r = {}
        for b in reads:
            b = self._b(b)
            if b.r.get(k, 0) < v:
                b.r[k] = v

    def op(self, ek, fn, r=(), w=(), signal=True):
        self._deps(ek, r, w)
        ins = fn(self.eng[ek])
        self.nins += 1
        if signal:
            self.cnt[ek] += 1
            ins.then_inc(self.sem[ek], 1)
            mark = (ek, self.cnt[ek])
        else:
            assert ek == "pe"
            mark = (ek, self.cnt[ek] + 1)
        self._mark(mark, r, w)
        return ins

    def dma(self, qk, out, in_, r=(), w=(), key=None):
        self._deps(qk, r, w)
        tok = self._b(w[0]) if len(w) else self._b(r[0])
        k = ("dma", key or tok.name)
        if k not in self.sem:
            self._uid += 1
            self.sem[k] = self.nc.alloc_semaphore("sd%d" % self._uid)
            self.cnt[k] = 0
        self.cnt[k] += 16
        self.eng[qk].dma_start(out=out, in_=in_).then_inc(self.sem[k], 16)
        self.nins += 1
        self._mark((k, self.cnt[k]), r, w)

    def finish(self, toks, ek="sp"):
        self._deps(ek, [], toks)
        e = self.eng[ek]
        for k, v in self.cnt.items():
            if v > 0 and self.seen[ek].get(k, 0) < v:
                e.wait_ge(self.sem[k], v)
                self.seen[ek][k] = v


def token_blocks():
    blks = [(i * 512, 512, 0) for i in range(TS // 512)]
    blks.append((TS, LS, 1))
    return blks


def token_tiles():
    tl = [(i * 128, 128) for i in range(TS // 128)]
    tl.append((TS, LS))
    return tl


FM_GROUPS = [
    (0, 512), (512, 128), (768, 512), (1792, 512), (2304, 32), (2336, 512), (3360, 512),
    (3872, 512), (4384, 512), (4896, 512), (5408, 512), (5920, 512), (6432, 512)]
FM_ROWS = sum(n for _, n in FM_GROUPS)
FMO = {"aq": 0, "ak": 512, "gq": 640, "gk": 896, "gr": 1152, "ga": 1664, "rq": 1696, "rk": 1952,
       "rg": 2208, "mg": 2720}
TM_GROUPS = [(640, 128), (1280, 512), (2848, 512), (1024, 256), (2592, 256)]
TM_COLS = sum(n for _, n in TM_GROUPS)
TMO = {"av": 0, "gv": 128, "rv": 640, "gk": 1152, "rk": 1408}


class Consts:
    def __init__(self, S):
        self.ones_bf = S.sbuf("ones_bf", [128, 128], BF16)
        S.op("pool", lambda e: e.memset(self.ones_bf[:], 1.0), w=[self.ones_bf])
        self.deps = S.sbuf("c_deps", [128, 1], F32)
        S.op("pool", lambda e: e.memset(self.deps[:], float(D * EPS)), w=[self.deps])


def emit_mod(S, cs_d, ada_w_d, ada_b_d, modT, ps):
    nc = S.nc
    cs = S.sbuf("cs", [128, 8, 2], F32)
    sg = S.sbuf("cs_sg", [128, 8, 2], F32)
    ab = S.sbuf("ada_b", [128, 48], F32)
    S.dma("sp", cs[:], cs_d, w=[cs])
    S.dma("sp", ab[:], ada_b_d, w=[ab])
    S.op("act", lambda e: e.activation(out=sg[:], in_=cs[:], func=AF.Sigmoid), r=[cs], w=[sg])
    S.op("dve", lambda e: e.tensor_tensor(out=cs[:], in0=cs[:], in1=sg[:], op=ALU.mult), r=[sg, cs], w=[cs])
    GW = 384
    wb = [S.sbuf("adaw%d" % i, [128, 8, GW], F32) for i in range(2)]
    for g in range(6144 // GW):
        wt = wb[g % 2]
        S.dma("sp", wt[:], ada_w_d[:, g * GW:(g + 1) * GW].rearrange("(c p) n -> p c n", p=128), w=[wt])
        for fc in range(GW // 128):
            ch = g * (GW // 128) + fc
            for kc in range(8):
                S.op("pe", lambda e: e.matmul(ps[:, ch * 2:ch * 2 + 2], wt[:, kc, fc * 128:(fc + 1) * 128],
                                              cs[:, kc, :], start=(kc == 0), stop=(kc == 7)),
                     r=[wt, cs], w=[ps], signal=(kc == 7))
    for j in range(2):
        S.op("dve", lambda e: e.tensor_tensor(out=modT[:, :, j], in0=ps[:, j:96:2], in1=ab[:], op=ALU.add),
             r=[ps, ab], w=[modT])


def emit_norm_mod(S, C, xT, hT, htoks, gmod, modT, part_sh, ps_ss, name):
    sq = S.sbuf(name + "_sq", [128, 8, 512], BF16)
    rstd = S.sbuf(name + "_rstd", [128, 512], F32)
    tmp = [S.sbuf(name + "_tmp%d" % i, [128, 512], F32) for i in range(2)]
    for bi, (t0, nb, j) in enumerate(token_blocks()):
        S.op("act", lambda e: e.activation(out=sq[:, :, :nb], in_=xT[:, :, t0:t0 + nb], func=AF.Square),
             r=[xT], w=[sq])
        for c in range(8):
            S.op("pe", lambda e: e.matmul(ps_ss[:, :nb], C.ones_bf[:], sq[:, c, :nb], start=(c == 0), stop=(c == 7)),
                 r=[C.ones_bf, sq], w=[ps_ss], signal=(c == 7))
        S.op("act", lambda e: e.activation(out=rstd[:, :nb], in_=ps_ss[:, :nb], func=AF.Sqrt,
                                           bias=C.deps[:, 0:1], scale=1.0), r=[ps_ss, C.deps], w=[rstd])
        S.op("dve", lambda e: e.reciprocal(out=rstd[:, :nb], in_=rstd[:, :nb]), r=[rstd], w=[rstd])
        for c in range(8):
            tt = tmp[c % 2]
            S.op("dve", lambda e: e.scalar_tensor_tensor(out=tt[:, :nb], in0=xT[:, c, t0:t0 + nb],
                                                         scalar=gmod[:, c, j:j + 1], in1=rstd[:, :nb],
                                                         op0=ALU.mult, op1=ALU.mult),
                 r=[xT, gmod, rstd], w=[tt])
            S.op("act", lambda e: e.activation(out=hT[:, c, t0:t0 + nb], in_=tt[:, :nb], func=AF.Identity,
                                               bias=modT[:, part_sh * 8 + c, j:j + 1], scale=1.0),
                 r=[tt, modT], w=[htoks[bi]])


def emit_gmod(S, gmod, g_d, modT, part_sc, name):
    g = S.sbuf(name + "_g", [128, 8], F32)
    S.dma("sp", g[:], g_d, w=[g])
    for j in range(2):
        S.op("dve", lambda e: e.tensor_scalar(out=gmod[:, :, j], in0=modT[:, part_sc * 8:(part_sc + 1) * 8, j],
                                              scalar1=1.0, scalar2=float(np.sqrt(D)), op0=ALU.add, op1=ALU.mult),
             r=[modT], w=[gmod])
        S.op("dve", lambda e: e.tensor_tensor(out=gmod[:, :, j], in0=gmod[:, :, j], in1=g[:], op=ALU.mult),
             r=[gmod, g], w=[gmod])


def build_p1():
    nc = bass.Bass("TRN2", target_bir_lowering=False)
    S = Sched(nc)
    xT_d = nc.dram_tensor("xT", [D, NT], F32, kind="ExternalInput").ap()
    cs_d = nc.dram_tensor("cs", [128, 8, 2], F32, kind="ExternalInput").ap()
    ada_w_d = nc.dram_tensor("ada_w", [D, 6 * D], F32, kind="ExternalInput").ap()
    ada_b_d = nc.dram_tensor("ada_b", [128, 48], F32, kind="ExternalInput").ap()
    g1_d = nc.dram_tensor("norm1_g", [128, 8], F32, kind="ExternalInput").ap()
    w_in_d = nc.dram_tensor("w_in", [D, D_IN], F32, kind="ExternalInput").ap()
    zfm_d = nc.dram_tensor("zfm", [FM_ROWS, NT], BF16, kind="ExternalOutput").ap()
    ztm_d = nc.dram_tensor("ztm", [NT, TM_COLS], BF16, kind="ExternalOutput").ap()
    mod_d = nc.dram_tensor("modT", [128, 96], F32, kind="ExternalOutput").ap()

    C = Consts(S)
    xT = S.sbuf("xT", [128, 8, NT], F32)
    hT = S.sbuf("hT", [128, 8, NT], BF16)
    htoks = [Buf("hT%d" % i) for i in range(len(token_blocks()))]
    modT = S.sbuf("modT", [128, 48, 2], F32)
    gmod = S.sbuf("gmod1", [128, 8, 2], F32)
    ps_mod = S.psum("ps_mod", [128, 512])
    ps_ss = S.psum("ps_ss", [128, 512])
    ps_o = [S.psum("ps_o%d" % i, [128, 512]) for i in range(4)]

    S.dma("sp", xT[:], xT_d.rearrange("(c p) t -> p c t", p=128), w=[xT])
    emit_mod(S, cs_d, ada_w_d, ada_b_d, modT, ps_mod)
    S.dma("sp", mod_d.rearrange("p (c j) -> p c j", j=2), modT[:], r=[modT])
    emit_gmod(S, gmod, g1_d, modT, 1, "n1")
    emit_norm_mod(S, C, xT, hT, htoks, gmod, modT, 0, ps_ss, "n1")

    wbufs = [S.sbuf("w%d" % i, [128, 8, 512], BF16) for i in range(3)]
    stg_fm = [S.sbuf("stgfm%d" % i, [128, NT], BF16) for i in range(3)]
    stg_tm = [S.sbuf("stgtm%d" % i, [128, 17, 512], BF16) for i in range(1)]
    gi = 0
    pi = 0
    ei = 0
    si = 0
    row0 = 0
    blks = token_blocks()
    for (c0, n) in FM_GROUPS:
        wt = wbufs[gi % 3]
        gi += 1
        S.dma("pool", wt[:, :, :n], w_in_d[:, c0:c0 + n].rearrange("(c p) n -> p c n", p=128), w=[wt])
        for f0 in range(0, n, 128):
            m = min(128, n - f0)
            stg = stg_fm[si % 3]
            si += 1
            for bi, (t0, nb, j) in enumerate(blks):
                ps = ps_o[pi % 4]
                pi += 1
                for kc in range(8):
                    S.op("pe", lambda e: e.matmul(ps[:m, :nb], wt[:, kc, f0:f0 + m], hT[:, kc, t0:t0 + nb],
                                                  start=(kc == 0), stop=(kc == 7)),
                         r=[wt, htoks[bi]], w=[ps], signal=(kc == 7))
                if ei % 2 == 0:
                    S.op("act", lambda e: e.activation(out=stg[:m, t0:t0 + nb], in_=ps[:m, :nb], func=AF.Copy),
                         r=[ps], w=[stg])
                else:
                    S.op("dve", lambda e: e.tensor_copy(out=stg[:m, t0:t0 + nb], in_=ps[:m, :nb]), r=[ps], w=[stg])
                ei += 1
            S.dma("sp", zfm_d[row0 + f0:row0 + f0 + m, :], stg[:m, :], r=[stg])
        row0 += n
    col0 = 0
    tiles = token_tiles()
    for gidx, (c0, n) in enumerate(TM_GROUPS):
        wt = wbufs[gi % 3]
        gi += 1
        S.dma("pool", wt[:, :, :n], w_in_d[:, c0:c0 + n].rearrange("(c p) n -> p c n", p=128), w=[wt])
        stg = stg_tm[0]
        for ti, (t0, nt) in enumerate(tiles):
            ps = ps_o[pi % 4]
            pi += 1
            bi = min(t0 // 512, len(blks) - 1)
            for kc in range(8):
                S.op("pe", lambda e: e.matmul(ps[:nt, :n], hT[:, kc, t0:t0 + nt], wt[:, kc, :n],
                                              start=(kc == 0), stop=(kc == 7)),
                     r=[wt, htoks[bi]], w=[ps], signal=(kc == 7))
            if ei % 2 == 0:
                S.op("act", lambda e: e.activation(out=stg[:nt, ti, :n], in_=ps[:nt, :n], func=AF.Copy),
                     r=[ps], w=[stg])
            else:
                S.op("dve", lambda e: e.tensor_copy(out=stg[:nt, ti, :n], in_=ps[:nt, :n]), r=[ps], w=[stg])
            ei += 1
        S.dma("sp", ztm_d[0:TS, col0:col0 + n].rearrange("(i p) n -> p i n", p=128), stg[:, 0:16, :n], r=[stg])
        S.dma("sp", ztm_d[TS:NT, col0:col0 + n], stg[:LS, 16, :n], r=[stg])
        col0 += n
    S.finish(stg_fm + stg_tm + [modT], "sp")
    return nc, S


def pc(v, nchunk):
    return np.ascontiguousarray(np.asarray(v).reshape(nchunk, 128).T)


def p1_inmaps(inp, l, x, ctx):
    maps = []
    for c in range(NCORE):
        b, seg = c // 4, c % 4
        xt = np.concatenate([x[b, seg * TS:(seg + 1) * TS], ctx[b, seg * LS:(seg + 1) * LS]], axis=0).T
        cs = np.stack([inp["c"][b], inp["c_ctx"]], axis=1)
        cs = np.ascontiguousarray(cs.reshape(8, 128, 2).transpose(1, 0, 2))
        maps.append({
            "xT": np.ascontiguousarray(xt, dtype=np.float32),
            "cs": cs.astype(np.float32),
            "ada_w": inp["ada_w"][l],
            "ada_b": pc(inp["ada_b"][l], 48),
            "norm1_g": pc(inp["norm1_g"][l], 8),
            "w_in": inp["w_in"][l],
        })
    return maps


NL3 = TS + 2
NC3 = LS + 2
NT3 = NL3 + NC3
BLK3 = [(0, 512, 0), (512, 512, 0), (1024, 512, 0), (1536, 512, 0), (2048, 2, 0), (NL3, NC3, 1)]
FFN_GROUPS = [[(1, 352, 0), (353, 352, 0)], [(705, 352, 0), (1057, 352, 0)],
              [(1409, 352, 0), (1761, 288, 0), (NL3 + 1, LS, 1)]]
GW3 = 770


def build_p3():
    nc = bass.Bass("TRN2", target_bir_lowering=False)
    S = Sched(nc)
    xT_d = nc.dram_tensor("xT", [D, NT3], F32, kind="ExternalInput").ap()
    mod_d = nc.dram_tensor("modT", [128, 96], F32, kind="ExternalInput").ap()
    g2_d = nc.dram_tensor("norm2_g", [128, 8], F32, kind="ExternalInput").ap()
    og_d = nc.dram_tensor("onorm_g", [128, 2], F32, kind="ExternalInput").ap()
    hm_d = nc.dram_tensor("hmask", [128, 4], F32, kind="ExternalInput").ap()
    oT_d = nc.dram_tensor("oT", [3 * 512, NT3], BF16, kind="ExternalInput").ap()
    gT_d = nc.dram_tensor("gT", [4096, NT3], BF16, kind="ExternalInput").ap()
    wb_d = nc.dram_tensor("w_branch", [3, 512, D], F32, kind="ExternalInput").ap()
    wo_d = nc.dram_tensor("w_out", [D, D], F32, kind="ExternalInput").ap()
    up_d = nc.dram_tensor("ffn_up", [D, 2 * D_FF], F32, kind="ExternalInput").ap()
    cw_d = nc.dram_tensor("conv_w", [128, 3, 44], F32, kind="ExternalInput").ap()
    cb_d = nc.dram_tensor("conv_b", [128, 44], F32, kind="ExternalInput").ap()
    dn_d = nc.dram_tensor("ffn_down", [D_FF, D], F32, kind="ExternalInput").ap()
    xo_d = nc.dram_tensor("xo", [D, NT], F32, kind="ExternalOutput").ap()

    C = Consts(S)
    xT = S.sbuf("xT", [128, 8, NT3], F32)
    arenaA = S.sbuf("arenaA", [128, 12 * NT3], BF16)
    arenaB = S.sbuf("arenaB", [128, 22 * GW3], BF16)
    yb = arenaA.t[:, :].rearrange("p (c t) -> p c t", c=12)
    h2T = arenaA.t[:, 0:8 * NT3].rearrange("p (c t) -> p c t", c=8)
    mT = arenaB.t[:, 0:8 * NT3].rearrange("p (c t) -> p c t", c=8)
    gF = arenaB.t[:, :].rearrange("p (c t) -> p c t", c=22)
    modT = S.sbuf("modT", [128, 48, 2], F32)
    gmod = S.sbuf("gmod2", [128, 8, 2], F32)
    ong = S.sbuf("ong", [128, 2], F32)
    hm = S.sbuf("hm", [128, 4], F32)
    cw = S.sbuf("cw", [128, 3, 44], F32)
    cb = S.sbuf("cb", [128, 44], F32)
    c128 = S.sbuf("c_128eps", [128, 1], F32)
    S.op("pool", lambda e: e.memset(c128[:], float(128 * EPS)), w=[c128])
    ps = [S.psum("ps%d" % i, [128, 512]) for i in range(8)]

    S.dma("sp", xT[:], xT_d.rearrange("(c p) t -> p c t", p=128), w=[xT])
    S.dma("sp", modT[:], mod_d.rearrange("p (c j) -> p c j", j=2), w=[modT])
    S.dma("sp", ong[:], og_d, w=[ong])
    S.dma("sp", hm[:], hm_d, w=[hm])
    S.dma("sp", cw[:], cw_d, w=[cw])
    S.dma("sp", cb[:], cb_d, w=[cb])
    S.op("dve", lambda e: e.tensor_scalar(out=ong[:], in0=ong[:], scalar1=float(np.sqrt(128.0)), scalar2=None,
                                          op0=ALU.mult), r=[ong], w=[ong])
    S.dma("sp", yb, oT_d.rearrange("(c p) t -> p c t", p=128), w=[arenaA])
    mgt = [S.sbuf("mgt%d" % i, [128, NT3], BF16) for i in range(3)]
    gate = mgt[0:2]
    sq = S.sbuf("sq3", [128, 512], BF16)
    rstd = S.sbuf("rstd3", [128, 512], F32)
    tmpf = [S.sbuf("tmpf%d" % i, [128, 512], F32) for i in range(2)]
    sgt = [S.sbuf("sgt%d" % i, [128, 512], F32) for i in range(2)]
    pi = 0
    k = 0
    for z in (1, 2):
        for hc in range(4):
            ci = z * 4 + hc
            gt = gate[k % 2]
            S.dma("sp", gt[:], gT_d[(z - 1) * 512 + hc * 128:(z - 1) * 512 + (hc + 1) * 128, :], w=[gt])
            for (t0, nb, j) in BLK3:
                if nb < 8:
                    pass
                p_ = ps[pi % 2]
                pi += 1
                S.op("act", lambda e: e.activation(out=sq[:, :nb], in_=yb[:, ci, t0:t0 + nb], func=AF.Square),
                     r=[arenaA], w=[sq])
                S.op("pe", lambda e: e.matmul(p_[:, :nb], C.ones_bf[:], sq[:, :nb], start=True, stop=True),
                     r=[sq, C.ones_bf], w=[p_])
                S.op("act", lambda e: e.activation(out=rstd[:, :nb], in_=p_[:, :nb], func=AF.Sqrt,
                                                   bias=c128[:, 0:1], scale=1.0), r=[p_, c128], w=[rstd])
                S.op("dve", lambda e: e.reciprocal(out=rstd[:, :nb], in_=rstd[:, :nb]), r=[rstd], w=[rstd])
                tf = tmpf[k % 2]
                sg = sgt[k % 2]
                S.op("act", lambda e: e.activation(out=sg[:, :nb], in_=gt[:, t0:t0 + nb], func=AF.Silu),
                     r=[gt], w=[sg])
                S.op("dve", lambda e: e.scalar_tensor_tensor(out=tf[:, :nb], in0=yb[:, ci, t0:t0 + nb],
                                                             scalar=ong[:, z - 1:z], in1=rstd[:, :nb],
                                                             op0=ALU.mult, op1=ALU.mult),
                     r=[arenaA, ong, rstd], w=[tf])
                S.op("pool", lambda e: e.tensor_tensor(out=yb[:, ci, t0:t0 + nb], in0=tf[:, :nb], in1=sg[:, :nb],
                                                       op=ALU.mult), r=[tf, sg], w=[arenaA])
                k += 1
    wbr = [S.sbuf("wbr%d" % i, [128, 4, 128], BF16) for i in range(6)]
    wi = 0
    for oc in range(8):
        wts = []
        for z in range(3):
            wt = wbr[wi % 6]
            wi += 1
            mg = mgt[z]
            S.dma("pool", wt[:], wb_d[z, :, oc * 128:(oc + 1) * 128].rearrange("(c p) n -> p c n", p=128), w=[wt])
            S.dma("sp", mg[:], gT_d[1024 + z * 1024 + oc * 128:1024 + z * 1024 + (oc + 1) * 128, :], w=[mg])
            S.op("act", lambda e: e.activation(out=mg[:], in_=mg[:], func=AF.Sigmoid), r=[mg], w=[mg])
            wts.append(wt)
        for (t0, nb, j) in BLK3:
            pz = []
            for z in range(3):
                p_ = ps[2 + pi % 6]
                pi += 1
                for kc in range(4):
                    S.op("pe", lambda e: e.matmul(p_[:, :nb], wts[z][:, kc, :], yb[:, z * 4 + kc, t0:t0 + nb],
                                                  start=(kc == 0), stop=(kc == 3)),
                         r=[wts[z], arenaA], w=[p_], signal=(kc == 3))
                pz.append(p_)
            tA, tB = tmpf[0], tmpf[1]
            S.op("dve", lambda e: e.tensor_tensor(out=tA[:, :nb], in0=pz[0][:, :nb], in1=mgt[0][:, t0:t0 + nb],
                                                  op=ALU.mult), r=[pz[0], mgt[0]], w=[tA])
            S.op("dve", lambda e: e.tensor_tensor(out=tB[:, :nb], in0=pz[1][:, :nb], in1=mgt[1][:, t0:t0 + nb],
                                                  op=ALU.mult), r=[pz[1], mgt[1]], w=[tB])
            S.op("pool", lambda e: e.tensor_tensor(out=tA[:, :nb], in0=tA[:, :nb], in1=tB[:, :nb], op=ALU.add),
                 r=[tA, tB], w=[tA])
            S.op("dve", lambda e: e.tensor_tensor(out=tB[:, :nb], in0=pz[2][:, :nb], in1=mgt[2][:, t0:t0 + nb],
                                                  op=ALU.mult), r=[pz[2], mgt[2]], w=[tB])
            S.op("pool", lambda e: e.tensor_tensor(out=mT[:, oc, t0:t0 + nb], in0=tA[:, :nb], in1=tB[:, :nb],
                                                   op=ALU.add), r=[tA, tB], w=[arenaB])
    wo = [S.sbuf("wo%d" % i, [128, 8, 128], BF16) for i in range(2)]
    for oc in range(8):
        wt = wo[oc % 2]
        S.dma("pool", wt[:], wo_d[:, oc * 128:(oc + 1) * 128].rearrange("(c p) n -> p c n", p=128), w=[wt])
        for (t0, nb, j) in BLK3:
            p_ = ps[2 + pi % 4]
            pi += 1
            for kc in range(8):
                S.op("pe", lambda e: e.matmul(p_[:, :nb], wt[:, kc, :], mT[:, kc, t0:t0 + nb],
                                              start=(kc == 0), stop=(kc == 7)),
                     r=[wt, arenaB], w=[p_], signal=(kc == 7))
            S.op("dve", lambda e: e.scalar_tensor_tensor(out=xT[:, oc, t0:t0 + nb], in0=p_[:, :nb],
                                                         scalar=modT[:, 16 + oc, j:j + 1], in1=xT[:, oc, t0:t0 + nb],
                                                         op0=ALU.mult, op1=ALU.add),
                 r=[p_, modT, xT], w=[xT])
    emit_gmod(S, gmod, g2_d, modT, 4, "n2")
    sq8t = arenaB
    sq8 = arenaB.t[:, 0:8 * 512].rearrange("p (c t) -> p c t", c=8)
    for (t0, nb, j) in BLK3:
        p_ = ps[pi % 2]
        pi += 1
        S.op("act", lambda e: e.activation(out=sq8[:, :, :nb], in_=xT[:, :, t0:t0 + nb], func=AF.Square),
             r=[xT], w=[sq8t])
        for c in range(8):
            S.op("pe", lambda e: e.matmul(p_[:, :nb], C.ones_bf[:], sq8[:, c, :nb], start=(c == 0), stop=(c == 7)),
                 r=[C.ones_bf, sq8t], w=[p_], signal=(c == 7))
        S.op("act", lambda e: e.activation(out=rstd[:, :nb], in_=p_[:, :nb], func=AF.Sqrt,
                                           bias=C.deps[:, 0:1], scale=1.0), r=[p_, C.deps], w=[rstd])
        S.op("dve", lambda e: e.reciprocal(out=rstd[:, :nb], in_=rstd[:, :nb]), r=[rstd], w=[rstd])
        for c in range(8):
            tf = tmpf[c % 2]
            S.op("dve", lambda e: e.scalar_tensor_tensor(out=tf[:, :nb], in0=xT[:, c, t0:t0 + nb],
                                                         scalar=gmod[:, c, j:j + 1], in1=rstd[:, :nb],
                                                         op0=ALU.mult, op1=ALU.mult),
                 r=[xT, gmod, rstd], w=[tf])
            S.op("act", lambda e: e.activation(out=h2T[:, c, t0:t0 + nb], in_=tf[:, :nb], func=AF.Identity,
                                               bias=modT[:, 24 + c, j:j + 1], scale=1.0),
                 r=[tf, modT], w=[arenaA])
    for i, col in enumerate((0, NL3 - 1, NL3, NT3 - 1)):
        S.op("dve", lambda e: e.tensor_scalar(out=h2T[:, :, col:col + 1], in0=h2T[:, :, col:col + 1],
                                              scalar1=hm[:, i:i + 1], scalar2=None, op0=ALU.mult),
             r=[arenaA, hm], w=[arenaA])
    wup = [S.sbuf("wup%d" % i, [128, 8, 256], BF16) for i in range(2)]
    wdn = [S.sbuf("wdn%d" % i, [128, 22, 128], BF16) for i in range(2)]
    ua = tmpf
    ub = [S.sbuf("ub%d" % i, [128, 512], F32) for i in range(2)]
    ui = 0
    for grp in FFN_GROUPS:
        gcol = []
        o = 0
        for (c0, n, j) in grp:
            gcol.append(o)
            o += n
        for f in range(22):
            wt = wup[f % 2]
            S.dma("pool", wt[:, :, 0:128], up_d[:, f * 128:(f + 1) * 128].rearrange("(c p) n -> p c n", p=128), w=[wt])
            S.dma("pool", wt[:, :, 128:256],
                  up_d[:, D_FF + f * 128:D_FF + (f + 1) * 128].rearrange("(c p) n -> p c n", p=128), w=[wt])
            for bi, (c0, n, j) in enumerate(grp):
                pa = ps[pi % 4]
                pb = ps[4 + pi % 4]
                pi += 1
                for half, p_ in ((0, pa), (1, pb)):
                    for kc in range(8):
                        S.op("pe", lambda e: e.matmul(p_[:, :n + 2], wt[:, kc, half * 128:(half + 1) * 128],
                                                      h2T[:, kc, c0 - 1:c0 + n + 1], start=(kc == 0), stop=(kc == 7)),
                             r=[wt, arenaA], w=[p_], signal=(kc == 7))
                a_ = ua[ui % 2]
                b_ = ub[ui % 2]
                ui += 1
                for half, p_, u_, eng in ((0, pa, a_, "dve"), (1, pb, b_, "pool")):
                    ch = half * 22 + f
                    S.op("act", lambda e: e.activation(out=u_[:, :n], in_=p_[:, 1:n + 1], func=AF.Identity,
                                                       bias=cb[:, ch:ch + 1], scale=cw[:, 1, ch:ch + 1]),
                         r=[p_, cb, cw], w=[u_])
                    S.op("dve", lambda e: e.scalar_tensor_tensor(out=u_[:, :n], in0=p_[:, 0:n],
                                                                 scalar=cw[:, 0, ch:ch + 1], in1=u_[:, :n],
                                                                 op0=ALU.mult, op1=ALU.add),
                         r=[p_, cw, u_], w=[u_])
                    S.op("dve", lambda e: e.scalar_tensor_tensor(out=u_[:, :n], in0=p_[:, 2:n + 2],
                                                                 scalar=cw[:, 2, ch:ch + 1], in1=u_[:, :n],
                                                                 op0=ALU.mult, op1=ALU.add),
                         r=[p_, cw, u_], w=[u_])
                sg = sgt[ui % 2]
                S.op("act", lambda e: e.activation(out=sg[:, :n], in_=a_[:, :n], func=AF.Silu), r=[a_], w=[sg])
                S.op("pool", lambda e: e.tensor_tensor(out=gF[:, f, gcol[bi]:gcol[bi] + n], in0=sg[:, :n],
                                                       in1=b_[:, :n], op=ALU.mult), r=[sg, b_], w=[arenaB])
        for oc in range(8):
            wt = wdn[oc % 2]
            S.dma("pool", wt[:], dn_d[:, oc * 128:(oc + 1) * 128].rearrange("(c p) n -> p c n", p=128), w=[wt])
            for bi, (c0, n, j) in enumerate(grp):
                p_ = ps[pi % 8]
                pi += 1
                for f in range(22):
                    S.op("pe", lambda e: e.matmul(p_[:, :n], wt[:, f, :], gF[:, f, gcol[bi]:gcol[bi] + n],
                                                  start=(f == 0), stop=(f == 21)),
                         r=[wt, arenaB], w=[p_], signal=(f == 21))
                S.op("dve", lambda e: e.scalar_tensor_tensor(out=xT[:, oc, c0:c0 + n], in0=p_[:, :n],
                                                             scalar=modT[:, 40 + oc, j:j + 1], in1=xT[:, oc, c0:c0 + n],
                                                             op0=ALU.mult, op1=ALU.add),
                     r=[p_, modT, xT], w=[xT])
    xo_v = xo_d.rearrange("(c p) t -> p c t", p=128)
    S.dma("sp", xo_v[:, :, 0:TS], xT[:, :, 1:1 + TS], r=[xT])
    S.dma("sp", xo_v[:, :, TS:NT], xT[:, :, NL3 + 1:NL3 + 1 + LS], r=[xT])
    S.finish([xT], "sp")
    return nc, S


def halo_cols(a, b, seg, n, tot):
    lo, hi = seg * n - 1, (seg + 1) * n + 1
    out = np.zeros((hi - lo,) + a.shape[2:], a.dtype)
    s, e = max(lo, 0), min(hi, tot)
    out[s - lo:e - lo] = a[b, s:e]
    return out


def p3_inmaps(inp, l, x, ctx, modT_list, o_lat, o_ctx, g_lat, g_ctx):
    maps = []
    cwl = inp["ffn_conv_w"][l]
    cw = np.ascontiguousarray(cwl.reshape(3, 44, 128).transpose(2, 0, 1))
    for c in range(NCORE):
        b, seg = c // 4, c % 4

        def cols(al, ac):
            return np.ascontiguousarray(np.concatenate([halo_cols(al, b, seg, TS, T), halo_cols(ac, b, seg, LS, L)], 0).T)
        hm = np.array([seg > 0, seg < 3, seg > 0, seg < 3], np.float32)
        maps.append({
            "xT": cols(x, ctx).astype(np.float32),
            "modT": modT_list[c],
            "norm2_g": pc(inp["norm2_g"][l], 8),
            "onorm_g": np.ascontiguousarray(np.stack([inp["gla_out_norm_g"][l], inp["ret_out_norm_g"][l]], 1)),
            "hmask": np.ascontiguousarray(np.broadcast_to(hm, (128, 4))),
            "oT": cols(o_lat, o_ctx),
            "gT": cols(g_lat, g_ctx),
            "w_branch": inp["w_branch"][l],
            "w_out": inp["w_out"][l],
            "ffn_up": inp["ffn_up"][l],
            "conv_w": cw,
            "conv_b": pc(inp["ffn_conv_b"][l], 44),
            "ffn_down": inp["ffn_down"][l],
        })
    return maps


NQB = T // 128
NKT = TB // 128
NEG = -30000.0


def p2_consts_np():
    bf = ml_dtypes.bfloat16
    c = {}
    bo = np.zeros((128, 128), np.float32)
    bo[:64, :64] = 1
    bo[64:, 64:] = 1
    c["BO"] = bo.astype(bf)
    R = np.zeros((64, 64), np.float32)
    for base in (0, 32):
        for f in range(16):
            R[base + f, base + 16 + f] = -1.0
            R[base + 16 + f, base + f] = 1.0
    rp = np.zeros((128, 128), np.float32)
    rp[:64, :64] = R.T
    rp[64:, 64:] = R.T
    c["RP"] = rp.astype(bf)
    c["IDN"] = np.eye(128, dtype=np.float32).astype(bf)
    s = np.arange(128)[:, None]
    i = np.arange(128)[None, :]
    mb = np.zeros((128, 2, 2, 128), np.float32)
    mb[:, 0] = np.where(s >= i, 0.0, NEG)[:, None, :]
    mb[:, 1] = np.where(s <= i, 0.0, NEG)[:, None, :]
    c["MB"] = mb.reshape(128, 2, 256).astype(bf)
    same = (s // 64) == (i // 64)
    sm = np.zeros((128, 2, 4, 128), np.float32)
    sm[:, 0] = (same & (s <= i))[:, None, :]
    sm[:, 1] = (same & (s >= i))[:, None, :]
    c["SM"] = sm.reshape(128, 2, 512).astype(bf)
    tri = np.zeros((128, 4, 128), np.float32)
    tri[:, 0] = same & (s <= i)
    tri[:, 1] = same & (s >= i)
    tri[:, 2] = same & (s > i)
    tri[:, 3] = same & (s < i)
    c["TRI"] = (tri * (-1.0 / 16.0)).astype(np.float32)
    t = np.arange(512) % 64
    idx = np.zeros((64, 2, 512), np.float32)
    idx[:, 0] = (t + 1)[None, :]
    idx[:, 1] = (64 - t)[None, :]
    c["IDX"] = idx
    p = np.arange(128) % 64
    c["PIDX"] = np.stack([63 - p, p], 1).astype(np.float32)
    return c


def rope_tables_np():
    tt = np.arange(T)
    row = (tt // GRID_W).astype(np.float32)
    col = (tt % GRID_W).astype(np.float32)
    inv = (10000.0 ** (-np.arange(16, dtype=np.float32) * 2.0 / 32.0)).astype(np.float32)
    ang = np.zeros((64, T), np.float32)
    for d in range(64):
        pos = row if d < 32 else col
        ang[d] = pos * inv[d % 16]
    cos = np.cos(ang).astype(np.float32)
    sin = np.sin(ang).astype(np.float32)
    return cos, sin


class P2:
    def __init__(self):
        nc = bass.Bass("TRN2", target_bir_lowering=False)
        self.nc = nc
        S = Sched(nc)
        self.S = S
        self.C = Consts(S)
        di = lambda n, sh, dt: nc.dram_tensor(n, list(sh), dt, kind="ExternalInput").ap()
        do = lambda n, sh, dt: nc.dram_tensor(n, list(sh), dt, kind="ExternalOutput").ap()
        self.d = {
            "aqT": di("aqT", [128, TB], BF16), "akT": di("akT", [128, TB], BF16), "av": di("av", [128, NKT * 64], BF16),
            "cosT": di("cosT", [64, T], F32), "sinT": di("sinT", [64, T], F32),
            "anp": di("anp", [128, 4], F32),
            "BO": di("BO", [128, 128], BF16), "RP": di("RP", [128, 128], BF16), "IDN": di("IDN", [128, 128], BF16),
            "MB": di("MB", [128, 2, 256], BF16), "SM": di("SM", [128, 2, 512], BF16),
            "TRI": di("TRI", [128, 4, 128], F32), "IDX": di("IDX", [64, 2, 512], F32), "PIDX": di("PIDX", [128, 2], F32),
            "oaT": do("oaT", [128, TB], BF16),
        }
        self.R = [S.sbuf("R%d" % i, [128, TB], BF16) for i in range(6)]
        self.ps = [S.psum("ps%d" % i, [128, 512]) for i in range(8)]
        self.k = {}
        for n in ("BO", "RP", "IDN"):
            self.k[n] = S.sbuf("k_" + n, [128, 128], BF16)
            S.dma("sp", self.k[n][:], self.d[n], w=[self.k[n]])
        self.k["MB"] = S.sbuf("k_MB", [128, 2, 256], BF16)
        S.dma("sp", self.k["MB"][:], self.d["MB"], w=[self.k["MB"]])
        self.c64 = S.sbuf("c_64eps", [128, 1], F32)
        S.op("pool", lambda e: e.memset(self.c64[:], float(64 * EPS)), w=[self.c64])
        self.tmp = [S.sbuf("p2tmp%d" % i, [128, 512], F32) for i in range(4)]
        self.tbf = [S.sbuf("p2tbf%d" % i, [128, 512], BF16) for i in range(4)]
        self.cs = [S.sbuf("p2cs%d" % i, [128, 2, 512], F32) for i in range(2)]

    def norm_rope_block(self, src_d, dst, c0, n, gain_ap, gain_tok, roped, bi, np_=128):
        S, k = self.S, self.k
        xin = self.tbf[bi % 2]
        sq = self.tbf[2]
        xg = self.tbf[3]
        ps_a, ps_b = self.ps[0], self.ps[1]
        rstd = self.tmp[0]
        S.dma("sp", xin[:np_, :n], src_d[:np_, c0:c0 + n], w=[xin])
        S.op("act", lambda e: e.activation(out=sq[:np_, :n], in_=xin[:np_, :n], func=AF.Square), r=[xin], w=[sq])
        S.op("pe", lambda e: e.matmul(ps_a[:np_, :n], k["BO"][:np_, :np_], sq[:np_, :n], start=True, stop=True),
             r=[k["BO"], sq], w=[ps_a])
        S.op("act", lambda e: e.activation(out=rstd[:np_, :n], in_=ps_a[:np_, :n], func=AF.Sqrt,
                                           bias=self.c64[:np_, 0:1], scale=1.0), r=[ps_a, self.c64], w=[rstd])
        S.op("dve", lambda e: e.reciprocal(out=rstd[:np_, :n], in_=rstd[:np_, :n]), r=[rstd], w=[rstd])
        if not roped:
            S.op("dve", lambda e: e.scalar_tensor_tensor(out=dst[:np_, c0:c0 + n], in0=xin[:np_, :n], scalar=gain_ap,
                                                         in1=rstd[:np_, :n], op0=ALU.mult, op1=ALU.mult),
                 r=[xin, gain_tok, rstd], w=[dst])
            return
        S.op("dve", lambda e: e.scalar_tensor_tensor(out=xg[:np_, :n], in0=xin[:np_, :n], scalar=gain_ap,
                                                     in1=rstd[:np_, :n], op0=ALU.mult, op1=ALU.mult),
             r=[xin, gain_tok, rstd], w=[xg])
        self.rope_block(xg, dst, c0, n, np_)

    def load_cs(self, c0, n):
        S = self.S
        cs = self.cs[(c0 // 512) % 2]
        if getattr(cs, "c0", None) == c0 and getattr(cs, "gen", None) == self.gen:
            return cs
        for hf in range(2):
            S.dma("sp", cs[hf * 64:(hf + 1) * 64, 0, :n], self.d["cosT"][:, c0:c0 + n], w=[cs])
            S.dma("sp", cs[hf * 64:(hf + 1) * 64, 1, :n], self.d["sinT"][:, c0:c0 + n], w=[cs])
        cs.c0 = c0
        cs.gen = self.gen
        return cs

    def rope_block(self, xg, dst, c0, n, np_, scale=None, src_c0=None):
        S, k = self.S, self.k
        ps_b = self.ps[7] if src_c0 is not None else self.ps[1]
        cs = self.load_cs(c0 if src_c0 is None else src_c0, n)
        t1, t2 = self.tmp[1], self.tmp[2]
        xs = 0 if src_c0 is None else src_c0
        xgt = xg
        xg = xg.t[:, xs:xs + n]
        S.op("pe", lambda e: e.matmul(ps_b[:np_, :n], k["RP"][:np_, :np_], xg[:np_, :n], start=True, stop=True),
             r=[k["RP"], xgt], w=[ps_b])
        S.op("dve", lambda e: e.tensor_tensor(out=t1[:np_, :n], in0=xg[:np_, :n], in1=cs[:np_, 0, :n], op=ALU.mult),
             r=[xgt, cs], w=[t1])
        S.op("dve", lambda e: e.tensor_tensor(out=t2[:np_, :n], in0=ps_b[:np_, :n], in1=cs[:np_, 1, :n], op=ALU.mult),
             r=[ps_b, cs], w=[t2])
        if scale is None:
            S.op("pool", lambda e: e.tensor_tensor(out=dst[:np_, c0:c0 + n], in0=t1[:np_, :n], in1=t2[:np_, :n],
                                                   op=ALU.add), r=[t1, t2], w=[dst])
        else:
            S.op("pool", lambda e: e.tensor_tensor(out=t1[:np_, :n], in0=t1[:np_, :n], in1=t2[:np_, :n],
                                                   op=ALU.add), r=[t1, t2], w=[t1])
            S.op("act", lambda e: e.activation(out=dst[:np_, c0:c0 + n], in_=t1[:np_, :n], func=AF.Copy,
                                               scale=float(scale)), r=[t1], w=[dst])

    def attention(self, stage=9, nqb=None):
        S, k, d = self.S, self.k, self.d
        self.gen = "attn"
        qn, kn, vt = self.R[0], self.R[1], self.R[2]
        anp = S.sbuf("anp", [128, 4], F32)
        S.dma("sp", anp[:], d["anp"], w=[anp])
        S.op("dve", lambda e: e.tensor_scalar(out=anp[:, 1:2], in0=anp[:, 1:2], scalar1=8.0, scalar2=None,
                                              op0=ALU.mult), r=[anp], w=[anp])
        es = S.sbuf("esink", [128, 2], F32)
        S.op("act", lambda e: e.activation(out=es[:], in_=anp[:, 2:4], func=AF.Exp), r=[anp], w=[es])
        esink = S.sbuf("esink_t", [64, 256], F32)
        ones_f = S.sbuf("ones_f", [64, 128], F32)
        S.op("pool", lambda e: e.memset(ones_f[:], 1.0), w=[ones_f])
        for h in range(2):
            S.op("dve", lambda e: e.tensor_scalar(out=esink[:, h * 128:(h + 1) * 128], in0=ones_f[:],
                                                  scalar1=es[0:64, h:h + 1], scalar2=None, op0=ALU.mult),
                 r=[ones_f, es], w=[esink])
        vv = vt.t[:, 0:NKT * 64].rearrange("p (i d) -> p i d", d=64)
        S.dma("sp", vt[:, 0:NKT * 64], d["av"], w=[vt])
        blocks = [(i * 512, 512, True) for i in range(T // 512)] + [(T, L, False)]
        self.att_out_toks = [qn, kn, vt]
        if stage < 1:
            return
        for bi, (c0, n, roped) in enumerate(blocks):
            self.norm_rope_block(d["aqT"], qn, c0, n, anp[:, 0:1], anp, roped, bi)
            self.norm_rope_block(d["akT"], kn, c0, n, anp[:, 1:2], anp, roped, bi + 1)
        PT = [S.sbuf("PT%d" % i, [128, 2, 5, 128], BF16) for i in range(2)]
        ost = [S.sbuf("oast%d" % i, [64, 2, 512], BF16) for i in range(2)]
        rec = S.sbuf("arec", [64, 256], F32)
        oview = d["oaT"].rearrange("(h d) t -> d h t", h=2)
        if stage < 2:
            return
        for qi, qb in enumerate(range(NQB + 2) if nqb is None else nqb):
            if qb < NQB:
                chunks = ([(qb - 1, 0)] if qb > 0 else []) + [(qb, None)] + ([(qb + 1, 1)] if qb < NQB - 1 else []) \
                    + [(NQB, None), (NQB + 1, None)]
            else:
                chunks = [(NQB, None), (NQB + 1, None)]
            A = self.ps[0:2] if qi % 2 == 0 else self.ps[2:4]
            Bk = self.ps[4:6]
            pv = self.ps[6 + qi % 2]
            pt = PT[qi % 2]
            qc = slice(qb * 128, (qb + 1) * 128)
            nch = len(chunks)
            for h in range(2):
                for bank, sl in ((A[h], range(0, min(nch, 4))), (Bk[h], range(4, nch))):
                    mms = []
                    for ci in sl:
                        kt, mi = chunks[ci]
                        o_ = bank[:, (ci % 4) * 128:(ci % 4 + 1) * 128]
                        mms.append((o_, kn[h * 64:(h + 1) * 64, kt * 128:(kt + 1) * 128],
                                    qn[h * 64:(h + 1) * 64, qc], [kn, qn]))
                        if mi is not None:
                            mms.append((o_, k["IDN"][:], k["MB"][:, mi, 0:128], [k["IDN"], k["MB"]]))
                    for i_, (o_, l_, r_, rd) in enumerate(mms):
                        S.op("pe", lambda e: e.matmul(o_, l_, r_, start=(i_ == 0), stop=(i_ == len(mms) - 1),
                                                      skip_group_check=True),
                             r=rd, w=[bank], signal=(i_ == len(mms) - 1))
            self.att_out_toks = list(self.ps)
            if stage < 3:
                continue
            for h in range(2):
                na = min(nch, 4)
                S.op("act", lambda e: e.activation(out=pt[:, h, 0:na, :],
                                                   in_=A[h][:, 0:na * 128].rearrange("p (c n) -> p c n", n=128),
                                                   func=AF.Exp), r=[A[h]], w=[pt])
                if nch > 4:
                    S.op("act", lambda e: e.activation(out=pt[:, h, 4, :], in_=Bk[h][:, 0:128], func=AF.Exp),
                         r=[Bk[h]], w=[pt])
            self.att_out_toks = list(self.ps) + PT
            if stage < 4:
                continue
            for ci, (kt, mi) in enumerate(chunks):
                S.op("pe", lambda e: e.matmul(pv[0:64, 0:256], vv[:, kt, :], pt[:, :, ci, :],
                                              start=(ci == 0), stop=(ci == nch - 1)),
                     r=[vt, pt], w=[pv], signal=False)
            for ci, (kt, mi) in enumerate(chunks):
                S.op("pe", lambda e: e.matmul(pv[0:64, 256:512], self.C.ones_bf[:, 0:64], pt[:, :, ci, :],
                                              start=(ci == 0), stop=(ci == nch - 1)),
                     r=[self.C.ones_bf, pt], w=[pv], signal=(ci == nch - 1))
            if stage < 5:
                continue
            st = ost[(qb // 4) % 2]
            so = (qb % 4) * 128
            S.op("dve", lambda e: e.tensor_tensor(out=rec[:], in0=pv[0:64, 256:512], in1=esink[:], op=ALU.add),
                 r=[pv, esink], w=[rec])
            S.op("dve", lambda e: e.reciprocal(out=rec[:], in_=rec[:]), r=[rec], w=[rec])
            S.op("dve", lambda e: e.tensor_tensor(out=st[:, :, so:so + 128],
                                                  in0=pv[0:64, 0:256].rearrange("p (h n) -> p h n", h=2),
                                                  in1=rec[:, :].rearrange("p (h n) -> p h n", h=2), op=ALU.mult),
                 r=[pv, rec], w=[st])
            if qb % 4 == 3 or qb == NQB + 1:
                g0 = (qb // 4) * 512
                wdt = so + 128
                if stage >= 6:
                    S.dma("sp", oview[:, :, g0:g0 + wdt], st[:, :, 0:wdt], r=[st])
            self.att_out_toks = ost + list(self.ps) + PT


    def scan_setup(self):
        S, d, nc = self.S, self.d, self.nc
        if hasattr(self, "scan_ready"):
            return
        self.scan_ready = True
        di = lambda n, sh, dt: nc.dram_tensor(n, list(sh), dt, kind="ExternalInput").ap()
        do = lambda n, sh, dt: nc.dram_tensor(n, list(sh), dt, kind="ExternalOutput").ap()
        d.update({
            "gqT": di("gqT", [64, TB], BF16), "gkT": di("gkT", [64, TB], BF16), "gaT": di("gaT", [32, TB], BF16),
            "gk": di("gk", [128, NKT * 64], BF16), "gv": di("gv", [128, NKT * 128], BF16),
            "wg": di("wg", [33, 128], F32),
            "rqT": di("rqT", [64, TB], BF16), "rkT": di("rkT", [64, TB], BF16), "rv": di("rv", [128, NKT * 128], BF16),
            "rld": di("rld", [128, 2], F32),
            "ogT": do("ogT", [128, TB], BF16), "orT": do("orT", [128, TB], BF16),
        })
        k = self.k
        k["SM"] = S.sbuf("k_SM", [128, 2, 512], BF16)
        S.dma("sp", k["SM"][:], d["SM"], w=[k["SM"]])
        k["TRI"] = S.sbuf("k_TRI", [128, 4, 128], F32)
        S.dma("sp", k["TRI"][:], d["TRI"], w=[k["TRI"]])
        k["IDX"] = S.sbuf("k_IDX", [64, 2, 512], F32)
        S.dma("sp", k["IDX"][:], d["IDX"], w=[k["IDX"]])
        k["PIDX"] = S.sbuf("k_PIDX", [128, 2], F32)
        S.dma("sp", k["PIDX"][:], d["PIDX"], w=[k["PIDX"]])
        self.onef = S.sbuf("c_onef", [128, 1], F32)
        S.op("pool", lambda e: e.memset(self.onef[:], 1.0), w=[self.onef])
        self.acc = S.sbuf("sc_acc", [128, TB], F32)
        self.Sall = [S.sbuf("sc_Sall%d" % i, [64, 9, 128], F32) for i in range(2)]
        self.Sbf = [S.sbuf("sc_Sbf%d" % i, [64, 8, 128], BF16) for i in range(2)]
        self.qin = [S.sbuf("sc_qin%d" % i, [64, 512], BF16) for i in range(2)]
        self.kin = [S.sbuf("sc_kin%d" % i, [64, 512], BF16) for i in range(2)]
        self.kend = [S.sbuf("sc_kend%d" % i, [128, 4, 64], BF16) for i in range(2)]
        self.attm = [S.sbuf("sc_attm%d" % i, [128, 512], BF16) for i in range(2)]
        self.Eq = [S.sbuf("sc_Eq%d" % i, [64, 512], F32) for i in range(2)]
        self.Ek = S.sbuf("sc_Ek", [64, 512], F32)
        self.Eend = S.sbuf("sc_Eend", [128, 4, 64], F32)
        self.sp = S.sbuf("sc_sp", [128, 4, 128], F32)
        self.gaf = S.sbuf("sc_gaf", [33, 512], F32)
        S.op("pool", lambda e: e.memset(self.gaf[32:33, :], 1.0), w=[self.gaf])
        self.gi = 0

    def scan_groups(self, z):
        lat = [(g * 512, 512) for g in range(T // 512)]
        return [(T, L)] + (lat if z == 0 else lat[::-1])

    def chunk_loop(self, z, c0, n, qin, kin, kend, kend_tok, vtm, dec_fn, dec_toks, outb):
        S, k = self.S, self.k
        A, Bk, Ck, Dk = self.ps[0], self.ps[1], self.ps[2], self.ps[3]
        nt = n // 128
        nchunk = 2 * nt
        t0 = c0 // 128
        gi = self.gi
        self.gi += 1
        attm = self.attm[gi % 2]
        Sall, Snext = self.Sall[gi % 2], self.Sall[(gi + 1) % 2]
        Sbf = self.Sbf[gi % 2]
        vv = vtm.t[:, 0:NKT * 128].rearrange("p (i d) -> p i d", d=128)
        for p in range(nt):
            S.op("pe", lambda e: e.matmul(A[:, p * 128:(p + 1) * 128], kin[:, p * 128:(p + 1) * 128],
                                          qin[:, p * 128:(p + 1) * 128], start=(p == 0), stop=(p == nt - 1),
                                          skip_group_check=True), r=[kin, qin], w=[A], signal=(p == nt - 1))
        S.op("dve", lambda e: e.tensor_tensor(out=attm[:, :n], in0=A[:, :n], in1=k["SM"][:, z, :n], op=ALU.mult),
             r=[A, k["SM"]], w=[attm])
        for cc in range(nchunk):
            p, hf = cc // 2, cc % 2
            bank = Bk if hf == 0 else Ck
            S.op("pe", lambda e: e.matmul(bank[0:64, p * 128:(p + 1) * 128], kend[hf * 64:(hf + 1) * 64, p, :],
                                          vv[hf * 64:(hf + 1) * 64, t0 + p, :], start=(p == 0), stop=(p == nt - 1),
                                          skip_group_check=True), r=[kend_tok, vtm], w=[bank], signal=(p == nt - 1))
        order = list(range(nchunk)) if z == 0 else list(range(nchunk - 1, -1, -1))
        pos = {}
        for i, cc in enumerate(order):
            pos[cc] = i
            p, hf = cc // 2, cc % 2
            bank = Bk if hf == 0 else Ck
            last = (i == nchunk - 1)
            o_ = Snext[:, 0, :] if last else Sall[:, i + 1, :]
            S.op("dve", lambda e: e.scalar_tensor_tensor(out=o_, in0=Sall[:, i, :], scalar=dec_fn(cc),
                                                         in1=bank[0:64, p * 128:(p + 1) * 128],
                                                         op0=ALU.mult, op1=ALU.add),
                 r=[Sall, bank] + dec_toks, w=[Snext if last else Sall])
        S.op("act", lambda e: e.activation(out=Sbf[:, 0:nchunk, :], in_=Sall[:, 0:nchunk, :], func=AF.Copy),
             r=[Sall], w=[Sbf])
        nmm = nt * 3
        mi = 0
        for p in range(nt):
            S.op("pe", lambda e: e.matmul(Dk[:, p * 128:(p + 1) * 128], vv[:, t0 + p, :], attm[:, p * 128:(p + 1) * 128],
                                          start=(mi == 0), stop=False, skip_group_check=True),
                 r=[vtm, attm], w=[Dk], signal=False)
            mi += 1
            for hf in range(2):
                cc = 2 * p + hf
                S.op("pe", lambda e: e.matmul(Dk[:, cc * 64:(cc + 1) * 64], Sbf[:, pos[cc], :],
                                              qin[:, cc * 64:(cc + 1) * 64], start=False, stop=(mi == nmm - 1),
                                              skip_group_check=True),
                     r=[Sbf, qin], w=[Dk], signal=(mi == nmm - 1))
                mi += 1
        if z == 0:
            S.op("act", lambda e: e.activation(out=self.acc[:, c0:c0 + n], in_=Dk[:, :n], func=AF.Copy),
                 r=[Dk], w=[self.acc])
        else:
            S.op("dve", lambda e: e.tensor_tensor(out=outb[:, c0:c0 + n], in0=Dk[:, :n], in1=self.acc[:, c0:c0 + n],
                                                  op=ALU.add), r=[Dk, self.acc], w=[outb])

    def scan_init_state(self):
        S = self.S
        Sall = self.Sall[self.gi % 2]
        S.op("pool", lambda e: e.memset(Sall[:, 0, :], 0.0), w=[Sall])

    def gla(self):
        S, k, d = self.S, self.k, self.d
        self.scan_setup()
        self.gen = "gla"
        qT, kT, vtm, ktm, gaT, outb = self.R[0], self.R[1], self.R[2], self.R[3], self.R[4], self.R[5]
        S.dma("sp", qT[0:64, :], d["gqT"], w=[qT])
        S.dma("sp", kT[0:64, :], d["gkT"], w=[kT])
        S.dma("sp", gaT[0:32, :], d["gaT"], w=[gaT])
        S.dma("sp", vtm[:, 0:NKT * 128], d["gv"], w=[vtm])
        S.dma("sp", ktm[:, 0:NKT * 64], d["gk"], w=[ktm])
        kk = ktm.t[:, 0:NKT * 64].rearrange("p (i d) -> p i d", d=64)
        wg = S.sbuf("gla_wg", [33, 128], F32)
        S.dma("sp", wg[:], d["wg"], w=[wg])
        E, Fk, Gk = self.ps[4], self.ps[5], self.ps[6]
        sp, gaf = self.sp, self.gaf
        ex = self.tmp[3]
        for z in range(2):
            self.scan_init_state()
            for (c0, n) in self.scan_groups(z):
                nt = n // 128
                t0 = c0 // 128
                gi = self.gi
                qin, kin, kend, Eq = self.qin[gi % 2], self.kin[gi % 2], self.kend[gi % 2], self.Eq[gi % 2]
                S.op("dve", lambda e: e.tensor_copy(out=gaf[0:32, :n], in_=gaT[0:32, c0:c0 + n]), r=[gaT], w=[gaf])
                for p in range(nt):
                    S.op("pe", lambda e: e.matmul(E[:, p * 128:(p + 1) * 128], gaf[0:33, p * 128:(p + 1) * 128],
                                                  wg[:, :], start=(p == 0), stop=(p == nt - 1), skip_group_check=True),
                         r=[gaf, wg], w=[E], signal=(p == nt - 1))
                S.op("act", lambda e: e.activation(out=ex[:, :n], in_=E[:, :n], func=AF.Exp, scale=-1.0),
                     r=[E], w=[ex])
                S.op("act", lambda e: e.activation(out=sp[:, 0:nt, :], in_=ex[:, :n].rearrange("p (i c) -> p i c", c=128),
                                                   func=AF.Ln, bias=self.onef[:, 0:1], scale=1.0),
                     r=[ex, self.onef], w=[sp])
                for p in range(nt):
                    S.op("pe", lambda e: e.matmul(Fk[0:64, p * 128:(p + 1) * 128], sp[:, p, z * 64:(z + 1) * 64],
                                                  k["TRI"][:, z, :], start=(p == 0), stop=(p == nt - 1),
                                                  skip_group_check=True),
                         r=[sp, k["TRI"]], w=[Fk], signal=(p == nt - 1))
                S.op("act", lambda e: e.activation(out=Eq[:, :n], in_=Fk[0:64, :n], func=AF.Exp), r=[Fk], w=[Eq])
                S.op("act", lambda e: e.activation(out=self.Ek[:, :n], in_=Fk[0:64, :n], func=AF.Exp, scale=-1.0),
                     r=[Fk], w=[self.Ek])
                S.op("dve", lambda e: e.scalar_tensor_tensor(out=qin[:, :n], in0=qT[0:64, c0:c0 + n], scalar=0.125,
                                                             in1=Eq[:, :n], op0=ALU.mult, op1=ALU.mult),
                     r=[qT, Eq], w=[qin])
                S.op("dve", lambda e: e.tensor_tensor(out=kin[:, :n], in0=kT[0:64, c0:c0 + n], in1=self.Ek[:, :n],
                                                      op=ALU.mult), r=[kT, self.Ek], w=[kin])
                for p in range(nt):
                    S.op("pe", lambda e: e.matmul(Gk[:, p * 64:(p + 1) * 64], k["TRI"][:, 2 + z, :],
                                                  sp[:, p, z * 64:(z + 1) * 64], start=(p == 0), stop=(p == nt - 1),
                                                  skip_group_check=True),
                         r=[sp, k["TRI"]], w=[Gk], signal=(p == nt - 1))
                S.op("act", lambda e: e.activation(out=self.Eend[:, 0:nt, :],
                                                   in_=Gk[:, 0:nt * 64].rearrange("p (i c) -> p i c", c=64),
                                                   func=AF.Exp), r=[Gk], w=[self.Eend])
                S.op("dve", lambda e: e.tensor_tensor(out=kend[:, 0:nt, :], in0=kk[:, t0:t0 + nt, :],
                                                      in1=self.Eend[:, 0:nt, :], op=ALU.mult),
                     r=[ktm, self.Eend], w=[kend])
                col = (lambda cc: cc * 64 + 63) if z == 0 else (lambda cc: cc * 64)
                self.chunk_loop(z, c0, n, qin, kin, kend, kend, vtm,
                                lambda cc, Eq=Eq, col=col: Eq[:, col(cc):col(cc) + 1], [Eq], outb)
        S.dma("sp", d["ogT"], outb[:, :], r=[outb])
        self.gla_out_toks = [outb]

    def ret(self):
        S, k, d = self.S, self.k, self.d
        self.scan_setup()
        self.gen = "ret"
        qT, kT, vtm, outb = self.R[0], self.R[1], self.R[2], self.R[5]
        S.dma("sp", qT[0:64, :], d["rqT"], w=[qT])
        S.dma("sp", kT[0:64, :], d["rkT"], w=[kT])
        S.dma("sp", vtm[:, 0:NKT * 128], d["rv"], w=[vtm])
        rld = S.sbuf("ret_rld", [128, 2], F32)
        S.dma("sp", rld[:], d["rld"], w=[rld])
        nlg = S.sbuf("ret_nlg", [128, 2], F32)
        lg = S.sbuf("ret_lg", [128, 2], F32)
        S.op("act", lambda e: e.activation(out=nlg[:], in_=rld[:], func=AF.Exp), r=[rld], w=[nlg])
        S.op("dve", lambda e: e.tensor_scalar(out=lg[:], in0=nlg[:], scalar1=-1.0, scalar2=None, op0=ALU.mult),
             r=[nlg], w=[lg])
        EqR = self.Eq
        spv = Tile(self.sp.t[0:64, :, :].rearrange("p a b -> p (a b)"), "spv")
        spv.b = self.sp.b
        EkR = [self.Ek, spv]
        EendR = S.sbuf("ret_EendR", [128, 2], F32)
        decR = S.sbuf("ret_decR", [64, 2], F32)
        for z in range(2):
            S.op("act", lambda e: e.activation(out=EqR[z][:], in_=k["IDX"][:, z, :], func=AF.Exp,
                                               scale=lg[0:64, z:z + 1]), r=[k["IDX"], lg], w=[EqR[z]])
            S.op("act", lambda e: e.activation(out=EkR[z][:], in_=k["IDX"][:, z, :], func=AF.Exp,
                                               scale=nlg[0:64, z:z + 1]), r=[k["IDX"], nlg], w=[EkR[z]])
            S.op("act", lambda e: e.activation(out=EendR[:, z:z + 1], in_=k["PIDX"][:, z:z + 1], func=AF.Exp,
                                               scale=lg[:, z:z + 1]), r=[k["PIDX"], lg], w=[EendR])
        S.op("act", lambda e: e.activation(out=decR[:], in_=lg[0:64, :], func=AF.Exp, scale=64.0), r=[lg], w=[decR])
        Hk = self.ps[7]
        rq_r, rk_r = self.tbf[0], self.tbf[1]
        for z in range(2):
            self.scan_init_state()
            for (c0, n) in self.scan_groups(z):
                nt = n // 128
                gi = self.gi
                qin, kin, kend = self.qin[gi % 2], self.kin[gi % 2], self.kend[gi % 2]
                if c0 < T:
                    self.rope_block(qT, rq_r, 0, n, 64, src_c0=c0)
                    self.rope_block(kT, rk_r, 0, n, 64, scale=0.125, src_c0=c0)
                    qsrc = rq_r[0:64, :n]
                    qtok = rq_r
                else:
                    S.op("act", lambda e: e.activation(out=rk_r[0:64, :n], in_=kT[0:64, c0:c0 + n], func=AF.Copy,
                                                       scale=0.125), r=[kT], w=[rk_r])
                    qsrc = qT[0:64, c0:c0 + n]
                    qtok = qT
                S.op("dve", lambda e: e.tensor_tensor(out=qin[:, :n], in0=qsrc, in1=EqR[z][:, :n], op=ALU.mult),
                     r=[qtok, EqR[z]], w=[qin])
                S.op("dve", lambda e: e.tensor_tensor(out=kin[:, :n], in0=rk_r[0:64, :n], in1=EkR[z][:, :n],
                                                      op=ALU.mult), r=[rk_r, EkR[z]], w=[kin])
                for p in range(nt):
                    S.op("pe", lambda e: e.matmul(Hk[:, p * 64:(p + 1) * 64], rk_r[0:64, p * 128:(p + 1) * 128],
                                                  k["IDN"][0:64, 0:64], start=(p == 0), stop=(p == nt - 1),
                                                  skip_group_check=True),
                         r=[rk_r, k["IDN"]], w=[Hk], signal=(p == nt - 1))
                S.op("dve", lambda e: e.tensor_scalar(out=kend[:, 0:nt, :],
                                                      in0=Hk[:, 0:nt * 64].rearrange("p (i c) -> p i c", c=64),
                                                      scalar1=EendR[:, z:z + 1], scalar2=None, op0=ALU.mult),
                     r=[Hk, EendR], w=[kend])
                self.chunk_loop(z, c0, n, qin, kin, kend, kend, vtm, lambda cc, z=z: decR[:, z:z + 1], [decR], outb)
        S.dma("sp", d["orT"], outb[:, :], r=[outb])
        self.ret_out_toks = [outb]


def build_p2(which=("attn", "gla", "ret"), stage=9, nqb=None):
    P = P2()
    outs = []
    if "attn" in which:
        P.attention(stage, nqb)
        outs += P.att_out_toks
    if "gla" in which:
        P.gla()
        outs += P.gla_out_toks
    if "ret" in which:
        P.ret()
        outs += P.ret_out_toks
    P.S.finish(outs, "sp")
    return P.nc, P.S


def tm_layout(a):
    n = a.shape[1]
    return np.ascontiguousarray(a.reshape(-1, 128, n).transpose(1, 0, 2).reshape(128, -1))


_PROG = {}


def _prog(name):
    if name not in _PROG:
        if name == "p1":
            _PROG[name] = build_p1()[0]
        elif name == "p2":
            _PROG[name] = build_p2()[0]
        else:
            _PROG[name] = build_p3()[0]
    return _PROG[name]


def _run(name, maps):
    res = run_bass_kernel_spmd(_prog(name), maps, core_ids=list(range(NCORE)))
    return res.results


def p2_inmaps(inp, l, zfm_b, ztm_b, consts, cos, sin):
    maps = []
    gw, gb_ = inp["gla_gate_w"][l], inp["gla_gate_b"][l]
    for c in range(NCORE):
        b, hh = c // 4, c % 4
        kvh = hh // 2
        zf, zt = zfm_b[b], ztm_b[b]

        def fm(name, o, n):
            return np.ascontiguousarray(zf[FMO[name] + o:FMO[name] + o + n])

        def tm(name, o, n):
            return tm_layout(zt[:, TMO[name] + o:TMO[name] + o + n])
        ak = fm("ak", kvh * 64, 64)
        anp = np.zeros((128, 4), np.float32)
        anp[:, 0] = np.tile(inp["attn_q_norm_g"][l], 2)
        anp[:, 1] = np.tile(inp["attn_k_norm_g"][l], 2)
        anp[:, 2] = inp["attn_sink"][l][2 * hh]
        anp[:, 3] = inp["attn_sink"][l][2 * hh + 1]
        wg = np.zeros((33, 128), np.float32)
        wg[0:16, 0:64] = gw[0][:, hh * 64:(hh + 1) * 64]
        wg[16:32, 64:128] = gw[1][:, hh * 64:(hh + 1) * 64]
        wg[32, 0:64] = gb_[0, hh * 64:(hh + 1) * 64]
        wg[32, 64:128] = gb_[1, hh * 64:(hh + 1) * 64]
        m = {
            "aqT": fm("aq", 2 * hh * 64, 128), "akT": np.ascontiguousarray(np.concatenate([ak, ak], 0)),
            "av": tm("av", kvh * 64, 64), "cosT": cos, "sinT": sin, "anp": anp,
            "gqT": fm("gq", hh * 64, 64), "gkT": fm("gk", hh * 64, 64), "gaT": fm("ga", 0, 32),
            "gk": tm("gk", hh * 64, 64), "gv": tm("gv", hh * 128, 128), "wg": wg,
            "rqT": fm("rq", hh * 64, 64), "rkT": fm("rk", hh * 64, 64), "rv": tm("rv", hh * 128, 128),
            "rld": np.ascontiguousarray(np.broadcast_to(inp["ret_log_decay"][l][:, hh], (128, 2))).astype(np.float32),
        }
        m.update(consts)
        maps.append(m)
    return maps


def kernel(**inp):
    inp = {k: np.asarray(v) for k, v in inp.items()}
    x = inp["x"].astype(np.float32)
    ctx = inp["ctx"].astype(np.float32)
    consts = p2_consts_np()
    cos, sin = rope_tables_np()
    for l in range(DEPTH):
        r1 = _run("p1", p1_inmaps(inp, l, x, ctx))
        zfm_b, ztm_b = [], []
        for b in range(B):
            zf = [np.asarray(r1[b * 4 + s]["zfm"]) for s in range(4)]
            zt = [np.asarray(r1[b * 4 + s]["ztm"]) for s in range(4)]
            zfm_b.append(np.concatenate([z[:, :TS] for z in zf] + [z[:, TS:] for z in zf], axis=1))
            ztm_b.append(np.concatenate([z[:TS] for z in zt] + [z[TS:] for z in zt], axis=0))
        modT_list = [np.asarray(r1[c]["modT"]) for c in range(NCORE)]
        del r1
        r2 = _run("p2", p2_inmaps(inp, l, zfm_b, ztm_b, consts, cos, sin))
        o_b = []
        for b in range(B):
            oa = np.concatenate([np.asarray(r2[b * 4 + hh]["oaT"]) for hh in range(4)], axis=0)
            og = np.concatenate([np.asarray(r2[b * 4 + hh]["ogT"]) for hh in range(4)], axis=0)
            orr = np.concatenate([np.asarray(r2[b * 4 + hh]["orT"]) for hh in range(4)], axis=0)
            o_b.append(np.ascontiguousarray(np.concatenate([oa, og, orr], axis=0).T))
        del r2
        o_all = np.stack(o_b, 0)
        g_all = np.stack([np.ascontiguousarray(
            np.concatenate([zfm_b[b][FMO["gr"]:FMO["gr"] + 512], zfm_b[b][FMO["rg"]:FMO["rg"] + 512],
                            zfm_b[b][FMO["mg"]:FMO["mg"] + 3072]], axis=0).T) for b in range(B)], 0)
        del zfm_b, ztm_b
        r3 = _run("p3", p3_inmaps(inp, l, x, ctx, modT_list, o_all[:, :T], o_all[:, T:], g_all[:, :T], g_all[:, T:]))
        xn = np.empty_like(x)
        cn = np.empty_like(ctx)
        for c in range(NCORE):
            b, seg = c // 4, c % 4
            xo = np.asarray(r3[c]["xo"])
            xn[b, seg * TS:(seg + 1) * TS] = xo[:, :TS].T
            cn[b, seg * LS:(seg + 1) * LS] = xo[:, TS:].T
        x, ctx = xn, cn
    return x.astype(np.float32)
```

```python
import numpy as np
import ml_dtypes
import concourse.bass as bass
import concourse.mybir as mybir
from concourse.bass_utils import run_bass_kernel_spmd

F32 = mybir.dt.float32
BF16 = mybir.dt.bfloat16
AF = mybir.ActivationFunctionType
ALU = mybir.AluOpType
AX = mybir.AxisListType

D = 1024
B = 2
T = 8192
L = 256
DEPTH = 4
NCORE = 8
TS = T // 4
LS = L // 4
NT = TS + LS
TB = T + L
D_FF = 2816
D_IN = 6944
EPS = 1e-6
GRID_W = 64

OFF = {}
_o = 0
for _n, _s in (("aq", 512), ("ak", 128), ("av", 128), ("gq", 256), ("gk", 256), ("gv", 512), ("gr", 512),
               ("ga", 32), ("rq", 256), ("rk", 256), ("rv", 512), ("rg", 512), ("mg", 3072)):
    OFF[_n] = (_o, _s)
    _o += _s
assert _o == D_IN


class Buf:
    __slots__ = ("name", "w", "r")

    def __init__(self, name):
        self.name = name
        self.w = None
        self.r = {}


class Tile:
    def __init__(self, t, name):
        self.t = t
        self.b = Buf(name)

    def __getitem__(self, idx):
        return self.t[idx]


class Sched:
    def __init__(self, nc):
        self.nc = nc
        self.eng = {"pe": nc.tensor, "act": nc.scalar, "dve": nc.vector, "pool": nc.gpsimd, "sp": nc.sync}
        self.sem = {}
        self.cnt = {}
        self.seen = {k: {} for k in self.eng}
        self.nwait = 0
        self.nins = 0
        for k in ("pe", "act", "dve", "pool"):
            self.sem[k] = nc.alloc_semaphore("s_" + k)
            self.cnt[k] = 0
        self._uid = 0

    def sbuf(self, name, shape, dtype):
        return Tile(self.nc.alloc_sbuf_tensor("sb_" + name, list(shape), dtype), name)

    def psum(self, name, shape, dtype=F32):
        return Tile(self.nc.alloc_psum_tensor("pp_" + name, list(shape), dtype), name)

    @staticmethod
    def _b(x):
        return x.b if isinstance(x, Tile) else x

    def _deps(self, ek, reads, writes):
        deps = {}

        def add(m):
            if m is None:
                return
            k, v = m
            if deps.get(k, 0) < v:
                deps[k] = v
        for b in reads:
            add(self._b(b).w)
        for b in writes:
            b = self._b(b)
            add(b.w)
            for k, v in b.r.items():
                add((k, v))
        if ek == "pe":
            deps.pop("pe", None)
        e = self.eng[ek]
        seen = self.seen[ek]
        for k, v in deps.items():
            if seen.get(k, 0) < v:
                e.wait_ge(self.sem[k], v)
                seen[k] = v
                self.nwait += 1

    def _mark(self, mark, reads, writes):
        k, v = mark
        for b in writes:
            b = self._b(b)
            b.w = mark
            b.r = {}
        for b in reads:
            b = self._b(b)
            if b.r.get(k, 0) < v:
                b.r[k] = v

    def op(self, ek, fn, r=(), w=(), signal=True):
        self._deps(ek, r, w)
        ins = fn(self.eng[ek])
        self.nins += 1
        if signal:
            self.cnt[ek] += 1
            ins.then_inc(self.sem[ek], 1)
            mark = (ek, self.cnt[ek])
        else:
            assert ek == "pe"
            mark = (ek, self.cnt[ek] + 1)
        self._mark(mark, r, w)
        return ins

    def dma(self, qk, out, in_, r=(), w=(), key=None):
        self._deps(qk, r, w)
        tok = self._b(w[0]) if len(w) else self._b(r[0])
        k = ("dma", key or tok.name)
        if k not in self.sem:
            self._uid += 1
            self.sem[k] = self.nc.alloc_semaphore("sd%d" % self._uid)
            self.cnt[k] = 0
        self.cnt[k] += 16
        self.eng[qk].dma_start(out=out, in_=in_).then_inc(self.sem[k], 16)
        self.nins += 1
        self._mark((k, self.cnt[k]), r, w)

    def barrier(self):
        for ek, e in self.eng.items():
            for k, v in self.cnt.items():
                if v > 0 and self.seen[ek].get(k, 0) < v:
                    e.wait_ge(self.sem[k], v)
                    self.seen[ek][k] = v

    def finish(self, toks, ek="sp"):
        self._deps(ek, [], toks)
        e = self.eng[ek]
        for k, v in self.cnt.items():
            if v > 0 and self.seen[ek].get(k, 0) < v:
                e.wait_ge(self.sem[k], v)
                self.seen[ek][k] = v


def token_blocks():
    blks = [(i * 512, 512, 0) for i in range(TS // 512)]
    blks.append((TS, LS, 1))
    return blks


def token_tiles():
    tl = [(i * 128, 128) for i in range(TS // 128)]
    tl.append((TS, LS))
    return tl


FM_GROUPS = [
    (0, 512), (512, 128), (768, 512), (1792, 512), (2304, 32), (2336, 512), (3360, 512),
    (3872, 512), (4384, 512), (4896, 512), (5408, 512), (5920, 512), (6432, 512)]
FM_ROWS = sum(n for _, n in FM_GROUPS)
FMO = {"aq": 0, "ak": 512, "gq": 640, "gk": 896, "gr": 1152, "ga": 1664, "rq": 1696, "rk": 1952,
       "rg": 2208, "mg": 2720}
TM_GROUPS = [(640, 128), (1280, 512), (2848, 512), (1024, 256), (2592, 256)]
TM_COLS = sum(n for _, n in TM_GROUPS)
TMO = {"av": 0, "gv": 128, "rv": 640, "gk": 1152, "rk": 1408}


class Consts:
    def __init__(self, S):
        self.ones_bf = S.sbuf("ones_bf", [128, 128], BF16)
        S.op("pool", lambda e: e.memset(self.ones_bf[:], 1.0), w=[self.ones_bf])
        self.deps = S.sbuf("c_deps", [128, 1], F32)
        S.op("pool", lambda e: e.memset(self.deps[:], float(D * EPS)), w=[self.deps])


def emit_mod(S, cs_d, ada_w_d, ada_b_d, modT, ps):
    nc = S.nc
    cs = S.sbuf("cs", [128, 8, 2], F32)
    sg = S.sbuf("cs_sg", [128, 8, 2], F32)
    ab = S.sbuf("ada_b", [128, 48], F32)
    S.dma("sp", cs[:], cs_d, w=[cs])
    S.dma("sp", ab[:], ada_b_d, w=[ab])
    S.op("act", lambda e: e.activation(out=sg[:], in_=cs[:], func=AF.Sigmoid), r=[cs], w=[sg])
    S.op("dve", lambda e: e.tensor_tensor(out=cs[:], in0=cs[:], in1=sg[:], op=ALU.mult), r=[sg, cs], w=[cs])
    GW = 384
    wb = [S.sbuf("adaw%d" % i, [128, 8, GW], F32) for i in range(2)]
    for g in range(6144 // GW):
        wt = wb[g % 2]
        S.dma("sp", wt[:], ada_w_d[:, g * GW:(g + 1) * GW].rearrange("(c p) n -> p c n", p=128), w=[wt])
        for fc in range(GW // 128):
            ch = g * (GW // 128) + fc
            for kc in range(8):
                S.op("pe", lambda e: e.matmul(ps[:, ch * 2:ch * 2 + 2], wt[:, kc, fc * 128:(fc + 1) * 128],
                                              cs[:, kc, :], start=(kc == 0), stop=(kc == 7)),
                     r=[wt, cs], w=[ps], signal=(kc == 7))
    for j in range(2):
        S.op("dve", lambda e: e.tensor_tensor(out=modT[:, :, j], in0=ps[:, j:96:2], in1=ab[:], op=ALU.add),
             r=[ps, ab], w=[modT])


def emit_norm_mod(S, C, xT, hT, htoks, gmod, modT, part_sh, ps_ss, name):
    sq = S.sbuf(name + "_sq", [128, 8, 512], BF16)
    rstd = S.sbuf(name + "_rstd", [128, 512], F32)
    tmp = [S.sbuf(name + "_tmp%d" % i, [128, 512], F32) for i in range(2)]
    for bi, (t0, nb, j) in enumerate(token_blocks()):
        S.op("act", lambda e: e.activation(out=sq[:, :, :nb], in_=xT[:, :, t0:t0 + nb], func=AF.Square),
             r=[xT], w=[sq])
        for c in range(8):
            S.op("pe", lambda e: e.matmul(ps_ss[:, :nb], C.ones_bf[:], sq[:, c, :nb], start=(c == 0), stop=(c == 7)),
                 r=[C.ones_bf, sq], w=[ps_ss], signal=(c == 7))
        S.op("act", lambda e: e.activation(out=rstd[:, :nb], in_=ps_ss[:, :nb], func=AF.Sqrt,
                                           bias=C.deps[:, 0:1], scale=1.0), r=[ps_ss, C.deps], w=[rstd])
        S.op("dve", lambda e: e.reciprocal(out=rstd[:, :nb], in_=rstd[:, :nb]), r=[rstd], w=[rstd])
        for c in range(8):
            tt = tmp[c % 2]
            S.op("dve", lambda e: e.scalar_tensor_tensor(out=tt[:, :nb], in0=xT[:, c, t0:t0 + nb],
                                                         scalar=gmod[:, c, j:j + 1], in1=rstd[:, :nb],
                                                         op0=ALU.mult, op1=ALU.mult),
                 r=[xT, gmod, rstd], w=[tt])
            S.op("act", lambda e: e.activation(out=hT[:, c, t0:t0 + nb], in_=tt[:, :nb], func=AF.Identity,
                                               bias=modT[:, part_sh * 8 + c, j:j + 1], scale=1.0),
                 r=[tt, modT], w=[htoks[bi]])


def emit_gmod(S, gmod, g_d, modT, part_sc, name):
    g = S.sbuf(name + "_g", [128, 8], F32)
    S.dma("sp", g[:], g_d, w=[g])
    for j in range(2):
        S.op("dve", lambda e: e.tensor_scalar(out=gmod[:, :, j], in0=modT[:, part_sc * 8:(part_sc + 1) * 8, j],
                                              scalar1=1.0, scalar2=float(np.sqrt(D)), op0=ALU.add, op1=ALU.mult),
             r=[modT], w=[gmod])
        S.op("dve", lambda e: e.tensor_tensor(out=gmod[:, :, j], in0=gmod[:, :, j], in1=g[:], op=ALU.mult),
             r=[gmod, g], w=[gmod])


def build_p1():
    nc = bass.Bass("TRN2", target_bir_lowering=False)
    S = Sched(nc)
    xT_d = nc.dram_tensor("xT", [D, NT], F32, kind="ExternalInput").ap()
    cs_d = nc.dram_tensor("cs", [128, 8, 2], F32, kind="ExternalInput").ap()
    ada_w_d = nc.dram_tensor("ada_w", [D, 6 * D], F32, kind="ExternalInput").ap()
    ada_b_d = nc.dram_tensor("ada_b", [128, 48], F32, kind="ExternalInput").ap()
    g1_d = nc.dram_tensor("norm1_g", [128, 8], F32, kind="ExternalInput").ap()
    w_in_d = nc.dram_tensor("w_in", [D, D_IN], F32, kind="ExternalInput").ap()
    zfm_d = nc.dram_tensor("zfm", [FM_ROWS, NT], BF16, kind="ExternalOutput").ap()
    ztm_d = nc.dram_tensor("ztm", [NT, TM_COLS], BF16, kind="ExternalOutput").ap()
    mod_d = nc.dram_tensor("modT", [128, 96], F32, kind="ExternalOutput").ap()

    C = Consts(S)
    xT = S.sbuf("xT", [128, 8, NT], F32)
    hT = S.sbuf("hT", [128, 8, NT], BF16)
    htoks = [Buf("hT%d" % i) for i in range(len(token_blocks()))]
    modT = S.sbuf("modT", [128, 48, 2], F32)
    gmod = S.sbuf("gmod1", [128, 8, 2], F32)
    ps_mod = S.psum("ps_mod", [128, 512])
    ps_ss = S.psum("ps_ss", [128, 512])
    ps_o = [S.psum("ps_o%d" % i, [128, 512]) for i in range(4)]

    S.dma("sp", xT[:], xT_d.rearrange("(c p) t -> p c t", p=128), w=[xT])
    emit_mod(S, cs_d, ada_w_d, ada_b_d, modT, ps_mod)
    S.dma("sp", mod_d.rearrange("p (c j) -> p c j", j=2), modT[:], r=[modT])
    emit_gmod(S, gmod, g1_d, modT, 1, "n1")
    emit_norm_mod(S, C, xT, hT, htoks, gmod, modT, 0, ps_ss, "n1")

    wbufs = [S.sbuf("w%d" % i, [128, 8, 512], BF16) for i in range(3)]
    stg_fm = [S.sbuf("stgfm%d" % i, [128, NT], BF16) for i in range(3)]
    stg_tm = [S.sbuf("stgtm%d" % i, [128, 17, 512], BF16) for i in range(1)]
    gi = 0
    pi = 0
    ei = 0
    si = 0
    row0 = 0
    blks = token_blocks()
    for (c0, n) in FM_GROUPS:
        wt = wbufs[gi % 3]
        gi += 1
        S.dma("pool", wt[:, :, :n], w_in_d[:, c0:c0 + n].rearrange("(c p) n -> p c n", p=128), w=[wt])
        for f0 in range(0, n, 128):
            m = min(128, n - f0)
            stg = stg_fm[si % 3]
            si += 1
            for bi, (t0, nb, j) in enumerate(blks):
                ps = ps_o[pi % 4]
                pi += 1
                for kc in range(8):
                    S.op("pe", lambda e: e.matmul(ps[:m, :nb], wt[:, kc, f0:f0 + m], hT[:, kc, t0:t0 + nb],
                                                  start=(kc == 0), stop=(kc == 7)),
                         r=[wt, htoks[bi]], w=[ps], signal=(kc == 7))
                if ei % 2 == 0:
                    S.op("act", lambda e: e.activation(out=stg[:m, t0:t0 + nb], in_=ps[:m, :nb], func=AF.Copy),
                         r=[ps], w=[stg])
                else:
                    S.op("dve", lambda e: e.tensor_copy(out=stg[:m, t0:t0 + nb], in_=ps[:m, :nb]), r=[ps], w=[stg])
                ei += 1
            S.dma("sp", zfm_d[row0 + f0:row0 + f0 + m, :], stg[:m, :], r=[stg])
        row0 += n
    col0 = 0
    tiles = token_tiles()
    for gidx, (c0, n) in enumerate(TM_GROUPS):
        wt = wbufs[gi % 3]
        gi += 1
        S.dma("pool", wt[:, :, :n], w_in_d[:, c0:c0 + n].rearrange("(c p) n -> p c n", p=128), w=[wt])
        stg = stg_tm[0]
        for ti, (t0, nt) in enumerate(tiles):
            ps = ps_o[pi % 4]
            pi += 1
            bi = min(t0 // 512, len(blks) - 1)
            for kc in range(8):
                S.op("pe", lambda e: e.matmul(ps[:nt, :n], hT[:, kc, t0:t0 + nt], wt[:, kc, :n],
                                              start=(kc == 0), stop=(kc == 7)),
                     r=[wt, htoks[bi]], w=[ps], signal=(kc == 7))
            if ei % 2 == 0:
                S.op("act", lambda e: e.activation(out=stg[:nt, ti, :n], in_=ps[:nt, :n], func=AF.Copy),
                     r=[ps], w=[stg])
            else:
                S.op("dve", lambda e: e.tensor_copy(out=stg[:nt, ti, :n], in_=ps[:nt, :n]), r=[ps], w=[stg])
            ei += 1
        S.dma("sp", ztm_d[0:TS, col0:col0 + n].rearrange("(i p) n -> p i n", p=128), stg[:, 0:16, :n], r=[stg])
        S.dma("sp", ztm_d[TS:NT, col0:col0 + n], stg[:LS, 16, :n], r=[stg])
        col0 += n
    S.finish(stg_fm + stg_tm + [modT], "sp")
    return nc, S


def pc(v, nchunk):
    return np.ascontiguousarray(np.asarray(v).reshape(nchunk, 128).T)


def p1_inmaps(inp, l, x, ctx):
    maps = []
    for c in range(NCORE):
        b, seg = c // 4, c % 4
        xt = np.concatenate([x[b, seg * TS:(seg + 1) * TS], ctx[b, seg * LS:(seg + 1) * LS]], axis=0).T
        cs = np.stack([inp["c"][b], inp["c_ctx"]], axis=1)
        cs = np.ascontiguousarray(cs.reshape(8, 128, 2).transpose(1, 0, 2))
        maps.append({
            "xT": np.ascontiguousarray(xt, dtype=np.float32),
            "cs": cs.astype(np.float32),
            "ada_w": inp["ada_w"][l],
            "ada_b": pc(inp["ada_b"][l], 48),
            "norm1_g": pc(inp["norm1_g"][l], 8),
            "w_in": inp["w_in"][l],
        })
    return maps


NL3 = TS + 2
NC3 = LS + 2
NT3 = NL3 + NC3
BLK3 = [(0, 512, 0), (512, 512, 0), (1024, 512, 0), (1536, 512, 0), (2048, 2, 0), (NL3, NC3, 1)]
FFN_GROUPS = [[(1, 352, 0), (353, 352, 0)], [(705, 352, 0), (1057, 352, 0)],
              [(1409, 352, 0), (1761, 288, 0), (NL3 + 1, LS, 1)]]
GW3 = 770


def build_p3():
    nc = bass.Bass("TRN2", target_bir_lowering=False)
    S = Sched(nc)
    xT_d = nc.dram_tensor("xT", [D, NT3], F32, kind="ExternalInput").ap()
    mod_d = nc.dram_tensor("modT", [128, 96], F32, kind="ExternalInput").ap()
    g2_d = nc.dram_tensor("norm2_g", [128, 8], F32, kind="ExternalInput").ap()
    og_d = nc.dram_tensor("onorm_g", [128, 2], F32, kind="ExternalInput").ap()
    hm_d = nc.dram_tensor("hmask", [128, 4], F32, kind="ExternalInput").ap()
    oT_d = nc.dram_tensor("oT", [3 * 512, NT3], BF16, kind="ExternalInput").ap()
    gT_d = nc.dram_tensor("gT", [4096, NT3], BF16, kind="ExternalInput").ap()
    wb_d = nc.dram_tensor("w_branch", [3, 512, D], F32, kind="ExternalInput").ap()
    wo_d = nc.dram_tensor("w_out", [D, D], F32, kind="ExternalInput").ap()
    up_d = nc.dram_tensor("ffn_up", [D, 2 * D_FF], F32, kind="ExternalInput").ap()
    cw_d = nc.dram_tensor("conv_w", [128, 3, 44], F32, kind="ExternalInput").ap()
    cb_d = nc.dram_tensor("conv_b", [128, 44], F32, kind="ExternalInput").ap()
    dn_d = nc.dram_tensor("ffn_down", [D_FF, D], F32, kind="ExternalInput").ap()
    xo_d = nc.dram_tensor("xo", [D, NT], F32, kind="ExternalOutput").ap()
    xm_d = nc.dram_tensor("xmid_scratch", [D, NT3], F32, kind="Internal").ap()

    C = Consts(S)
    NX = 8 * NT3 * 2
    big = S.sbuf("big", [128, NX + 8 * NT3], BF16)
    xT = Tile(big.t[:, 0:NX].bitcast(F32).rearrange("p (c t) -> p c t", c=8), "xTv")
    mTt = Tile(big.t[:, NX:NX + 8 * NT3].rearrange("p (c t) -> p c t", c=8), "mTv")
    mT = mTt.t
    arenaA = S.sbuf("arenaA", [128, 12 * NT3], BF16)
    yb = arenaA.t[:, :].rearrange("p (c t) -> p c t", c=12)
    h2T = arenaA.t[:, 0:8 * NT3].rearrange("p (c t) -> p c t", c=8)
    scr = S.sbuf("scr", [128, 3 * NT3], BF16)
    mgt = [Tile(scr.t[:, i * NT3:(i + 1) * NT3], "mgt%d" % i) for i in range(3)]
    gate = mgt[0:2]
    modT = S.sbuf("modT", [128, 48, 2], F32)
    gmod = S.sbuf("gmod2", [128, 8, 2], F32)
    ong = S.sbuf("ong", [128, 2], F32)
    hm = S.sbuf("hm", [128, 4], F32)
    cw = S.sbuf("cw", [128, 3, 44], F32)
    cb = S.sbuf("cb", [128, 44], F32)
    c128 = S.sbuf("c_128eps", [128, 1], F32)
    S.op("pool", lambda e: e.memset(c128[:], float(128 * EPS)), w=[c128])
    ps = [S.psum("ps%d" % i, [128, 512]) for i in range(8)]

    S.dma("sp", xT[:], xT_d.rearrange("(c p) t -> p c t", p=128), w=[xT])
    S.dma("sp", modT[:], mod_d.rearrange("p (c j) -> p c j", j=2), w=[modT])
    S.dma("sp", ong[:], og_d, w=[ong])
    S.dma("sp", hm[:], hm_d, w=[hm])
    S.dma("sp", cw[:], cw_d, w=[cw])
    S.dma("sp", cb[:], cb_d, w=[cb])
    S.op("dve", lambda e: e.tensor_scalar(out=ong[:], in0=ong[:], scalar1=float(np.sqrt(128.0)), scalar2=None,
                                          op0=ALU.mult), r=[ong], w=[ong])
    S.dma("sp", yb, oT_d.rearrange("(c p) t -> p c t", p=128), w=[arenaA])
    sqs = [S.sbuf("sq3_%d" % i, [128, 512], BF16) for i in range(2)]
    rstds = [S.sbuf("rstd3_%d" % i, [128, 512], F32) for i in range(2)]
    tmpf = [S.sbuf("tmpf%d" % i, [128, 512], F32) for i in range(2)]
    sgt = [S.sbuf("sgt%d" % i, [128, 512], F32) for i in range(2)]
    ybt = [Buf("yb%d" % i) for i in range(12)]
    pi = 0
    k = 0
    first = True
    for z in (1, 2):
        for hc in range(4):
            ci = z * 4 + hc
            gt = gate[hc % 2]
            S.dma("sp", gt[:], gT_d[(z - 1) * 512 + hc * 128:(z - 1) * 512 + (hc + 1) * 128, :], w=[gt])
            S.op("act", lambda e: e.activation(out=gt[:], in_=gt[:], func=AF.Silu), r=[gt], w=[gt])
            for (t0, nb, j) in BLK3:
                p_ = ps[pi % 4]
                pi += 1
                sq, rstd = sqs[k % 2], rstds[k % 2]
                S.op("pool", lambda e: e.tensor_tensor(out=sq[:, :nb], in0=yb[:, ci, t0:t0 + nb],
                                                       in1=yb[:, ci, t0:t0 + nb], op=ALU.mult),
                     r=[arenaA, ybt[ci]], w=[sq])
                S.op("pe", lambda e: e.matmul(p_[:, :nb], C.ones_bf[:], sq[:, :nb], start=True, stop=True),
                     r=[sq, C.ones_bf], w=[p_])
                S.op("act", lambda e: e.activation(out=rstd[:, :nb], in_=p_[:, :nb], func=AF.Ln,
                                                   bias=c128[:, 0:1], scale=1.0), r=[p_, c128], w=[rstd])
                S.op("act", lambda e: e.activation(out=rstd[:, :nb], in_=rstd[:, :nb], func=AF.Exp, scale=-0.5),
                     r=[rstd], w=[rstd])
                tf = tmpf[k % 2]
                S.op("dve", lambda e: e.scalar_tensor_tensor(out=tf[:, :nb], in0=yb[:, ci, t0:t0 + nb],
                                                             scalar=ong[:, z - 1:z], in1=rstd[:, :nb],
                                                             op0=ALU.mult, op1=ALU.mult),
                     r=[arenaA, ybt[ci], ong, rstd], w=[tf])
                S.op("pool", lambda e: e.tensor_tensor(out=yb[:, ci, t0:t0 + nb], in0=tf[:, :nb],
                                                       in1=gt[:, t0:t0 + nb], op=ALU.mult),
                     r=[tf, gt, arenaA], w=[ybt[ci]])
                k += 1
    ybr = [ybt[i] for i in range(12)]
    wbr = [S.sbuf("wbr%d" % i, [128, 4, 128], BF16) for i in range(6)]
    wi = 0
    for oc in range(8):
        wts = []
        for z in range(3):
            wt = wbr[wi % 6]
            wi += 1
            mg = mgt[z]
            S.dma("pool", wt[:], wb_d[z, :, oc * 128:(oc + 1) * 128].rearrange("(c p) n -> p c n", p=128), w=[wt])
            S.dma("sp", mg[:], gT_d[1024 + z * 1024 + oc * 128:1024 + z * 1024 + (oc + 1) * 128, :], w=[mg])
            S.op("act", lambda e: e.activation(out=mg[:], in_=mg[:], func=AF.Sigmoid), r=[mg], w=[mg])
            wts.append(wt)
        for (t0, nb, j) in BLK3:
            pz = []
            for z in range(3):
                p_ = ps[2 + pi % 6]
                pi += 1
                for kc in range(4):
                    S.op("pe", lambda e: e.matmul(p_[:, :nb], wts[z][:, kc, :], yb[:, z * 4 + kc, t0:t0 + nb],
                                                  start=(kc == 0), stop=(kc == 3)),
                         r=[wts[z], arenaA] + ybr[z * 4:z * 4 + 4], w=[p_], signal=(kc == 3))
                pz.append(p_)
            tA, tB = tmpf[0], tmpf[1]
            S.op("dve", lambda e: e.tensor_tensor(out=tA[:, :nb], in0=pz[0][:, :nb], in1=mgt[0][:, t0:t0 + nb],
                                                  op=ALU.mult), r=[pz[0], mgt[0]], w=[tA])
            S.op("dve", lambda e: e.tensor_tensor(out=tB[:, :nb], in0=pz[1][:, :nb], in1=mgt[1][:, t0:t0 + nb],
                                                  op=ALU.mult), r=[pz[1], mgt[1]], w=[tB])
            S.op("pool", lambda e: e.tensor_tensor(out=tA[:, :nb], in0=tA[:, :nb], in1=tB[:, :nb], op=ALU.add),
                 r=[tA, tB], w=[tA])
            S.op("dve", lambda e: e.tensor_tensor(out=tB[:, :nb], in0=pz[2][:, :nb], in1=mgt[2][:, t0:t0 + nb],
                                                  op=ALU.mult), r=[pz[2], mgt[2]], w=[tB])
            S.op("pool", lambda e: e.tensor_tensor(out=mT[:, oc, t0:t0 + nb], in0=tA[:, :nb], in1=tB[:, :nb],
                                                   op=ALU.add), r=[tA, tB], w=[mTt])
    wo = [S.sbuf("wo%d" % i, [128, 8, 128], BF16) for i in range(2)]
    for oc in range(8):
        wt = wo[oc % 2]
        S.dma("pool", wt[:], wo_d[:, oc * 128:(oc + 1) * 128].rearrange("(c p) n -> p c n", p=128), w=[wt])
        for (t0, nb, j) in BLK3:
            p_ = ps[2 + pi % 4]
            pi += 1
            for kc in range(8):
                S.op("pe", lambda e: e.matmul(p_[:, :nb], wt[:, kc, :], mT[:, kc, t0:t0 + nb],
                                              start=(kc == 0), stop=(kc == 7)),
                     r=[wt, mTt], w=[p_], signal=(kc == 7))
            S.op("dve", lambda e: e.scalar_tensor_tensor(out=xT[:, oc, t0:t0 + nb], in0=p_[:, :nb],
                                                         scalar=modT[:, 16 + oc, j:j + 1], in1=xT[:, oc, t0:t0 + nb],
                                                         op0=ALU.mult, op1=ALU.add),
                 r=[p_, modT, xT], w=[xT])
    S.dma("sp", xm_d.rearrange("(c p) t -> p c t", p=128), xT[:], r=[xT])
    emit_gmod(S, gmod, g2_d, modT, 4, "n2")
    sq8 = scr.t[:, 0:8 * 512].rearrange("p (c t) -> p c t", c=8)
    sq8t = [mgt[0], mgt[1], mgt[2]]
    for (t0, nb, j) in BLK3:
        p_ = ps[pi % 2]
        pi += 1
        rstd = rstds[pi % 2]
        S.op("act", lambda e: e.activation(out=sq8[:, :, :nb], in_=xT[:, :, t0:t0 + nb], func=AF.Square),
             r=[xT], w=sq8t)
        for c in range(8):
            S.op("pe", lambda e: e.matmul(p_[:, :nb], C.ones_bf[:], sq8[:, c, :nb], start=(c == 0), stop=(c == 7)),
                 r=[C.ones_bf] + sq8t, w=[p_], signal=(c == 7))
        S.op("act", lambda e: e.activation(out=rstd[:, :nb], in_=p_[:, :nb], func=AF.Sqrt,
                                           bias=C.deps[:, 0:1], scale=1.0), r=[p_, C.deps], w=[rstd])
        S.op("dve", lambda e: e.reciprocal(out=rstd[:, :nb], in_=rstd[:, :nb]), r=[rstd], w=[rstd])
        for c in range(8):
            tf = tmpf[c % 2]
            S.op("dve", lambda e: e.scalar_tensor_tensor(out=tf[:, :nb], in0=xT[:, c, t0:t0 + nb],
                                                         scalar=gmod[:, c, j:j + 1], in1=rstd[:, :nb],
                                                         op0=ALU.mult, op1=ALU.mult),
                 r=[xT, gmod, rstd], w=[tf])
            S.op("act", lambda e: e.activation(out=h2T[:, c, t0:t0 + nb], in_=tf[:, :nb], func=AF.Identity,
                                               bias=modT[:, 24 + c, j:j + 1], scale=1.0),
                 r=[tf, modT] + ybr, w=[arenaA])
    for i, col in enumerate((0, NL3 - 1, NL3, NT3 - 1)):
        S.op("dve", lambda e: e.tensor_scalar(out=h2T[:, :, col:col + 1], in0=h2T[:, :, col:col + 1],
                                              scalar1=hm[:, i:i + 1], scalar2=None, op0=ALU.mult),
             r=[arenaA, hm], w=[arenaA])
    S.barrier()
    gFt = Buf("gF")
    gF = big.t[:, 0:22 * NT].rearrange("p (c t) -> p c t", c=22)
    spare = arenaA.t[:, 8 * NT3:12 * NT3]
    wupA = [Tile(spare[:, i * 4096:(i + 1) * 4096].rearrange("p (c n) -> p c n", c=8), "wupA%d" % i) for i in range(2)]
    wupB = [S.sbuf("wupB%d" % i, [128, 8, 512], BF16) for i in range(2)]
    wdn = [Tile(scr.t[:, i * 2816:(i + 1) * 2816].rearrange("p (c n) -> p c n", c=22), "wdn%d" % i) for i in range(2)]
    ua = tmpf
    ub = [S.sbuf("ub%d" % i, [128, 512], F32) for i in range(2)]
    xmt = sgt
    blocks = [b_ for grp in FFN_GROUPS for b_ in grp]
    gcol = []
    o = 0
    for (c0, n, j) in blocks:
        gcol.append(o)
        o += n
    assert o == NT
    ui = 0
    for fg in range(6):
        nf = 4 if fg < 5 else 2
        wa, wb_ = wupA[fg % 2], wupB[fg % 2]
        S.dma("pool", wa[:, :, 0:nf * 128], up_d[:, fg * 512:fg * 512 + nf * 128].rearrange("(c p) n -> p c n", p=128),
              w=[wa])
        S.dma("pool", wb_[:, :, 0:nf * 128],
              up_d[:, D_FF + fg * 512:D_FF + fg * 512 + nf * 128].rearrange("(c p) n -> p c n", p=128), w=[wb_])
        for fl in range(nf):
            f = fg * 4 + fl
            for bi, (c0, n, j) in enumerate(blocks):
                pa = ps[pi % 4]
                pb = ps[4 + pi % 4]
                pi += 1
                for wt, p_ in ((wa, pa), (wb_, pb)):
                    for kc in range(8):
                        S.op("pe", lambda e: e.matmul(p_[:, :n + 2], wt[:, kc, fl * 128:(fl + 1) * 128],
                                                      h2T[:, kc, c0 - 1:c0 + n + 1], start=(kc == 0), stop=(kc == 7)),
                             r=[wt, arenaA], w=[p_], signal=(kc == 7))
                a_ = ua[ui % 2]
                b_ = ub[ui % 2]
                ui += 1
                for half, p_, u_ in ((0, pa, a_), (1, pb, b_)):
                    ch = half * 22 + f
                    S.op("act", lambda e: e.activation(out=u_[:, :n], in_=p_[:, 1:n + 1], func=AF.Identity,
                                                       bias=cb[:, ch:ch + 1], scale=cw[:, 1, ch:ch + 1]),
                         r=[p_, cb, cw], w=[u_])
                    S.op("dve", lambda e: e.scalar_tensor_tensor(out=u_[:, :n], in0=p_[:, 0:n],
                                                                 scalar=cw[:, 0, ch:ch + 1], in1=u_[:, :n],
                                                                 op0=ALU.mult, op1=ALU.add),
                         r=[p_, cw, u_], w=[u_])
                    S.op("dve", lambda e: e.scalar_tensor_tensor(out=u_[:, :n], in0=p_[:, 2:n + 2],
                                                                 scalar=cw[:, 2, ch:ch + 1], in1=u_[:, :n],
                                                                 op0=ALU.mult, op1=ALU.add),
                         r=[p_, cw, u_], w=[u_])
                sg = sgt[ui % 2]
                S.op("act", lambda e: e.activation(out=sg[:, :n], in_=a_[:, :n], func=AF.Silu), r=[a_], w=[sg])
                S.op("pool", lambda e: e.tensor_tensor(out=gF[:, f, gcol[bi]:gcol[bi] + n], in0=sg[:, :n],
                                                       in1=b_[:, :n], op=ALU.mult), r=[sg, b_], w=[gFt])
    xo_v = xo_d.rearrange("(c p) t -> p c t", p=128)
    xm_v = xm_d.rearrange("(c p) t -> p c t", p=128)
    xi = 0
    for oc in range(8):
        wt = wdn[oc % 2]
        S.dma("pool", wt[:], dn_d[:, oc * 128:(oc + 1) * 128].rearrange("(c p) n -> p c n", p=128), w=[wt])
        for bi, (c0, n, j) in enumerate(blocks):
            p_ = ps[pi % 8]
            pi += 1
            xm = xmt[xi % 2]
            xi += 1
            S.dma("sp", xm[:, :n], xm_v[:, oc, c0:c0 + n], w=[xm])
            for f in range(22):
                S.op("pe", lambda e: e.matmul(p_[:, :n], wt[:, f, :], gF[:, f, gcol[bi]:gcol[bi] + n],
                                              start=(f == 0), stop=(f == 21)),
                     r=[wt, gFt], w=[p_], signal=(f == 21))
            S.op("dve", lambda e: e.scalar_tensor_tensor(out=xm[:, :n], in0=p_[:, :n],
                                                         scalar=modT[:, 40 + oc, j:j + 1], in1=xm[:, :n],
                                                         op0=ALU.mult, op1=ALU.add),
                 r=[p_, modT, xm], w=[xm])
            S.dma("sp", xo_v[:, oc, gcol[bi]:gcol[bi] + n], xm[:, :n], r=[xm])
    S.finish(xmt, "sp")
    return nc, S


def halo_cols(a, b, seg, n, tot):
    lo, hi = seg * n - 1, (seg + 1) * n + 1
    out = np.zeros((hi - lo,) + a.shape[2:], a.dtype)
    s, e = max(lo, 0), min(hi, tot)
    out[s - lo:e - lo] = a[b, s:e]
    return out


def p3_inmaps(inp, l, x, ctx, modT_list, o_lat, o_ctx, g_lat, g_ctx):
    maps = []
    cwl = inp["ffn_conv_w"][l]
    cw = np.ascontiguousarray(cwl.reshape(3, 44, 128).transpose(2, 0, 1))
    for c in range(NCORE):
        b, seg = c // 4, c % 4

        def cols(al, ac):
            return np.ascontiguousarray(np.concatenate([halo_cols(al, b, seg, TS, T), halo_cols(ac, b, seg, LS, L)], 0).T)
        hm = np.array([seg > 0, seg < 3, seg > 0, seg < 3], np.float32)
        maps.append({
            "xT": cols(x, ctx).astype(np.float32),
            "modT": modT_list[c],
            "norm2_g": pc(inp["norm2_g"][l], 8),
            "onorm_g": np.ascontiguousarray(np.stack([inp["gla_out_norm_g"][l], inp["ret_out_norm_g"][l]], 1)),
            "hmask": np.ascontiguousarray(np.broadcast_to(hm, (128, 4))),
            "oT": cols(o_lat, o_ctx),
            "gT": cols(g_lat, g_ctx),
            "w_branch": inp["w_branch"][l],
            "w_out": inp["w_out"][l],
            "ffn_up": inp["ffn_up"][l],
            "conv_w": cw,
            "conv_b": pc(inp["ffn_conv_b"][l], 44),
            "ffn_down": inp["ffn_down"][l],
        })
    return maps


NQB = T // 128
NKT = TB // 128
NEG = -30000.0


def p2_consts_np():
    bf = ml_dtypes.bfloat16
    c = {}
    bo = np.zeros((128, 128), np.float32)
    bo[:64, :64] = 1
    bo[64:, 64:] = 1
    c["BO"] = bo.astype(bf)
    R = np.zeros((64, 64), np.float32)
    for base in (0, 32):
        for f in range(16):
            R[base + f, base + 16 + f] = -1.0
            R[base + 16 + f, base + f] = 1.0
    rp = np.zeros((128, 128), np.float32)
    rp[:64, :64] = R.T
    rp[64:, 64:] = R.T
    c["RP"] = rp.astype(bf)
    c["IDN"] = np.eye(128, dtype=np.float32).astype(bf)
    s = np.arange(128)[:, None]
    i = np.arange(128)[None, :]
    mb = np.zeros((128, 2, 2, 128), np.float32)
    mb[:, 0] = np.where(s >= i, 0.0, NEG)[:, None, :]
    mb[:, 1] = np.where(s <= i, 0.0, NEG)[:, None, :]
    c["MB"] = mb.reshape(128, 2, 256).astype(bf)
    same = (s // 64) == (i // 64)
    sm = np.zeros((128, 2, 4, 128), np.float32)
    sm[:, 0] = (same & (s <= i))[:, None, :]
    sm[:, 1] = (same & (s >= i))[:, None, :]
    c["SM"] = sm.reshape(128, 2, 512).astype(bf)
    tri = np.zeros((128, 4, 128), np.float32)
    tri[:, 0] = same & (s <= i)
    tri[:, 1] = same & (s >= i)
    tri[:, 2] = same & (s > i)
    tri[:, 3] = same & (s < i)
    c["TRI"] = (tri * (-1.0 / 16.0)).astype(np.float32)
    t = np.arange(512) % 64
    idx = np.zeros((64, 2, 512), np.float32)
    idx[:, 0] = (t + 1)[None, :]
    idx[:, 1] = (64 - t)[None, :]
    c["IDX"] = idx
    p = np.arange(128) % 64
    c["PIDX"] = np.stack([63 - p, p], 1).astype(np.float32)
    return c


def rope_tables_np():
    tt = np.arange(T)
    row = (tt // GRID_W).astype(np.float32)
    col = (tt % GRID_W).astype(np.float32)
    inv = (10000.0 ** (-np.arange(16, dtype=np.float32) * 2.0 / 32.0)).astype(np.float32)
    ang = np.zeros((64, T), np.float32)
    for d in range(64):
        pos = row if d < 32 else col
        ang[d] = pos * inv[d % 16]
    cos = np.cos(ang).astype(np.float32)
    sin = np.sin(ang).astype(np.float32)
    return cos, sin


class P2:
    def __init__(self):
        nc = bass.Bass("TRN2", target_bir_lowering=False)
        self.nc = nc
        S = Sched(nc)
        self.S = S
        self.C = Consts(S)
        di = lambda n, sh, dt: nc.dram_tensor(n, list(sh), dt, kind="ExternalInput").ap()
        do = lambda n, sh, dt: nc.dram_tensor(n, list(sh), dt, kind="ExternalOutput").ap()
        self.d = {
            "aqT": di("aqT", [128, TB], BF16), "akT": di("akT", [128, TB], BF16), "av": di("av", [128, NKT * 64], BF16),
            "cosT": di("cosT", [64, T], F32), "sinT": di("sinT", [64, T], F32),
            "anp": di("anp", [128, 4], F32),
            "BO": di("BO", [128, 128], BF16), "RP": di("RP", [128, 128], BF16), "IDN": di("IDN", [128, 128], BF16),
            "MB": di("MB", [128, 2, 256], BF16), "SM": di("SM", [128, 2, 512], BF16),
            "TRI": di("TRI", [128, 4, 128], F32), "IDX": di("IDX", [64, 2, 512], F32), "PIDX": di("PIDX", [128, 2], F32),
            "oaT": do("oaT", [128, TB], BF16),
        }
        self.R = [S.sbuf("R%d" % i, [128, TB], BF16) for i in range(6)]
        self.ps = [S.psum("ps%d" % i, [128, 512]) for i in range(8)]
        self.k = {}
        for n in ("BO", "RP", "IDN"):
            self.k[n] = S.sbuf("k_" + n, [128, 128], BF16)
            S.dma("sp", self.k[n][:], self.d[n], w=[self.k[n]])
        self.k["MB"] = S.sbuf("k_MB", [128, 2, 256], BF16)
        S.dma("sp", self.k["MB"][:], self.d["MB"], w=[self.k["MB"]])
        self.c64 = S.sbuf("c_64eps", [128, 1], F32)
        S.op("pool", lambda e: e.memset(self.c64[:], float(64 * EPS)), w=[self.c64])
        self.tmp = [S.sbuf("p2tmp%d" % i, [128, 512], F32) for i in range(4)]
        self.tbf = [S.sbuf("p2tbf%d" % i, [128, 512], BF16) for i in range(4)]
        self.cs = [S.sbuf("p2cs%d" % i, [128, 2, 512], F32) for i in range(2)]

    def norm_rope_block(self, src_d, dst, c0, n, gain_ap, gain_tok, roped, bi, np_=128):
        S, k = self.S, self.k
        xin = self.tbf[bi % 2]
        sq = self.tbf[2]
        xg = self.tbf[3]
        ps_a, ps_b = self.ps[0], self.ps[1]
        rstd = self.tmp[0]
        S.dma("sp", xin[:np_, :n], src_d[:np_, c0:c0 + n], w=[xin])
        S.op("act", lambda e: e.activation(out=sq[:np_, :n], in_=xin[:np_, :n], func=AF.Square), r=[xin], w=[sq])
        S.op("pe", lambda e: e.matmul(ps_a[:np_, :n], k["BO"][:np_, :np_], sq[:np_, :n], start=True, stop=True),
             r=[k["BO"], sq], w=[ps_a])
        S.op("act", lambda e: e.activation(out=rstd[:np_, :n], in_=ps_a[:np_, :n], func=AF.Ln,
                                           bias=self.c64[:np_, 0:1], scale=1.0), r=[ps_a, self.c64], w=[rstd])
        S.op("act", lambda e: e.activation(out=rstd[:np_, :n], in_=rstd[:np_, :n], func=AF.Exp, scale=-0.5),
             r=[rstd], w=[rstd])
        if not roped:
            S.op("dve", lambda e: e.scalar_tensor_tensor(out=dst[:np_, c0:c0 + n], in0=xin[:np_, :n], scalar=gain_ap,
                                                         in1=rstd[:np_, :n], op0=ALU.mult, op1=ALU.mult),
                 r=[xin, gain_tok, rstd], w=[dst])
            return
        S.op("dve", lambda e: e.scalar_tensor_tensor(out=xg[:np_, :n], in0=xin[:np_, :n], scalar=gain_ap,
                                                     in1=rstd[:np_, :n], op0=ALU.mult, op1=ALU.mult),
             r=[xin, gain_tok, rstd], w=[xg])
        self.rope_block(xg, dst, c0, n, np_)

    def load_cs(self, c0, n):
        S = self.S
        cs = self.cs[(c0 // 512) % 2]
        if getattr(cs, "c0", None) == c0 and getattr(cs, "gen", None) == self.gen:
            return cs
        for hf in range(2):
            S.dma("sp", cs[hf * 64:(hf + 1) * 64, 0, :n], self.d["cosT"][:, c0:c0 + n], w=[cs])
            S.dma("sp", cs[hf * 64:(hf + 1) * 64, 1, :n], self.d["sinT"][:, c0:c0 + n], w=[cs])
        cs.c0 = c0
        cs.gen = self.gen
        return cs

    def rope_block(self, xg, dst, c0, n, np_, scale=None, src_c0=None):
        S, k = self.S, self.k
        ps_b = self.ps[7] if src_c0 is not None else self.ps[1]
        cs = self.load_cs(c0 if src_c0 is None else src_c0, n)
        t1, t2 = self.tmp[1], self.tmp[2]
        xs = 0 if src_c0 is None else src_c0
        xgt = xg
        xg = xg.t[:, xs:xs + n]
        S.op("pe", lambda e: e.matmul(ps_b[:np_, :n], k["RP"][:np_, :np_], xg[:np_, :n], start=True, stop=True),
             r=[k["RP"], xgt], w=[ps_b])
        S.op("dve", lambda e: e.tensor_tensor(out=t1[:np_, :n], in0=xg[:np_, :n], in1=cs[:np_, 0, :n], op=ALU.mult),
             r=[xgt, cs], w=[t1])
        S.op("dve", lambda e: e.tensor_tensor(out=t2[:np_, :n], in0=ps_b[:np_, :n], in1=cs[:np_, 1, :n], op=ALU.mult),
             r=[ps_b, cs], w=[t2])
        if scale is None:
            S.op("pool", lambda e: e.tensor_tensor(out=dst[:np_, c0:c0 + n], in0=t1[:np_, :n], in1=t2[:np_, :n],
                                                   op=ALU.add), r=[t1, t2], w=[dst])
        else:
            S.op("pool", lambda e: e.tensor_tensor(out=t1[:np_, :n], in0=t1[:np_, :n], in1=t2[:np_, :n],
                                                   op=ALU.add), r=[t1, t2], w=[t1])
            S.op("act", lambda e: e.activation(out=dst[:np_, c0:c0 + n], in_=t1[:np_, :n], func=AF.Copy,
                                               scale=float(scale)), r=[t1], w=[dst])

    def attention(self, stage=9, nqb=None):
        S, k, d = self.S, self.k, self.d
        self.gen = "attn"
        qn, kn, vt = self.R[0], self.R[1], self.R[2]
        anp = S.sbuf("anp", [128, 4], F32)
        S.dma("sp", anp[:], d["anp"], w=[anp])
        S.op("dve", lambda e: e.tensor_scalar(out=anp[:, 1:2], in0=anp[:, 1:2], scalar1=8.0, scalar2=None,
                                              op0=ALU.mult), r=[anp], w=[anp])
        es = S.sbuf("esink", [128, 2], F32)
        S.op("act", lambda e: e.activation(out=es[:], in_=anp[:, 2:4], func=AF.Exp), r=[anp], w=[es])
        esink = S.sbuf("esink_t", [64, 256], F32)
        ones_f = S.sbuf("ones_f", [64, 128], F32)
        S.op("pool", lambda e: e.memset(ones_f[:], 1.0), w=[ones_f])
        for h in range(2):
            S.op("dve", lambda e: e.tensor_scalar(out=esink[:, h * 128:(h + 1) * 128], in0=ones_f[:],
                                                  scalar1=es[0:64, h:h + 1], scalar2=None, op0=ALU.mult),
                 r=[ones_f, es], w=[esink])
        vv = vt.t[:, 0:NKT * 64].rearrange("p (i d) -> p i d", d=64)
        S.dma("sp", vt[:, 0:NKT * 64], d["av"], w=[vt])
        blocks = [(i * 512, 512, True) for i in range(T // 512)] + [(T, L, False)]
        self.att_out_toks = [qn, kn, vt]
        if stage < 1:
            return
        for bi, (c0, n, roped) in enumerate(blocks):
            self.norm_rope_block(d["aqT"], qn, c0, n, anp[:, 0:1], anp, roped, bi)
            self.norm_rope_block(d["akT"], kn, c0, n, anp[:, 1:2], anp, roped, bi + 1)
        PT = [S.sbuf("PT%d" % i, [128, 2, 5, 128], BF16) for i in range(2)]
        ost = [S.sbuf("oast%d" % i, [64, 2, 512], BF16) for i in range(2)]
        rec = S.sbuf("arec", [64, 256], F32)
        oview = d["oaT"].rearrange("(h d) t -> d h t", h=2)
        if stage < 2:
            return
        for qi, qb in enumerate(range(NQB + 2) if nqb is None else nqb):
            if qb < NQB:
                chunks = ([(qb - 1, 0)] if qb > 0 else []) + [(qb, None)] + ([(qb + 1, 1)] if qb < NQB - 1 else []) \
                    + [(NQB, None), (NQB + 1, None)]
            else:
                chunks = [(NQB, None), (NQB + 1, None)]
            A = self.ps[0:2] if qi % 2 == 0 else self.ps[2:4]
            Bk = self.ps[4:6]
            pv = self.ps[6 + qi % 2]
            pt = PT[qi % 2]
            qc = slice(qb * 128, (qb + 1) * 128)
            nch = len(chunks)
            for h in range(2):
                for bank, sl in ((A[h], range(0, min(nch, 4))), (Bk[h], range(4, nch))):
                    mms = []
                    for ci in sl:
                        kt, mi = chunks[ci]
                        o_ = bank[:, (ci % 4) * 128:(ci % 4 + 1) * 128]
                        mms.append((o_, kn[h * 64:(h + 1) * 64, kt * 128:(kt + 1) * 128],
                                    qn[h * 64:(h + 1) * 64, qc], [kn, qn]))
                        if mi is not None:
                            mms.append((o_, k["IDN"][:], k["MB"][:, mi, 0:128], [k["IDN"], k["MB"]]))
                    for i_, (o_, l_, r_, rd) in enumerate(mms):
                        S.op("pe", lambda e: e.matmul(o_, l_, r_, start=(i_ == 0), stop=(i_ == len(mms) - 1),
                                                      skip_group_check=True),
                             r=rd, w=[bank], signal=(i_ == len(mms) - 1))
            self.att_out_toks = list(self.ps)
            if stage < 3:
                continue
            for h in range(2):
                na = min(nch, 4)
                S.op("act", lambda e: e.activation(out=pt[:, h, 0:na, :],
                                                   in_=A[h][:, 0:na * 128].rearrange("p (c n) -> p c n", n=128),
                                                   func=AF.Exp), r=[A[h]], w=[pt])
                if nch > 4:
                    S.op("act", lambda e: e.activation(out=pt[:, h, 4, :], in_=Bk[h][:, 0:128], func=AF.Exp),
                         r=[Bk[h]], w=[pt])
            self.att_out_toks = list(self.ps) + PT
            if stage < 4:
                continue
            for ci, (kt, mi) in enumerate(chunks):
                S.op("pe", lambda e: e.matmul(pv[0:64, 0:256], vv[:, kt, :], pt[:, :, ci, :],
                                              start=(ci == 0), stop=(ci == nch - 1)),
                     r=[vt, pt], w=[pv], signal=False)
            for ci, (kt, mi) in enumerate(chunks):
                S.op("pe", lambda e: e.matmul(pv[0:64, 256:512], self.C.ones_bf[:, 0:64], pt[:, :, ci, :],
                                              start=(ci == 0), stop=(ci == nch - 1)),
                     r=[self.C.ones_bf, pt], w=[pv], signal=(ci == nch - 1))
            if stage < 5:
                continue
            st = ost[(qb // 4) % 2]
            so = (qb % 4) * 128
            S.op("dve", lambda e: e.tensor_tensor(out=rec[:], in0=pv[0:64, 256:512], in1=esink[:], op=ALU.add),
                 r=[pv, esink], w=[rec])
            S.op("dve", lambda e: e.reciprocal(out=rec[:], in_=rec[:]), r=[rec], w=[rec])
            S.op("dve", lambda e: e.tensor_tensor(out=st[:, :, so:so + 128],
                                                  in0=pv[0:64, 0:256].rearrange("p (h n) -> p h n", h=2),
                                                  in1=rec[:, :].rearrange("p (h n) -> p h n", h=2), op=ALU.mult),
                 r=[pv, rec], w=[st])
            if qb % 4 == 3 or qb == NQB + 1:
                g0 = (qb // 4) * 512
                wdt = so + 128
                if stage >= 6:
                    S.dma("sp", oview[:, :, g0:g0 + wdt], st[:, :, 0:wdt], r=[st])
            self.att_out_toks = ost + list(self.ps) + PT


    def scan_setup(self):
        S, d, nc = self.S, self.d, self.nc
        if hasattr(self, "scan_ready"):
            return
        self.scan_ready = True
        di = lambda n, sh, dt: nc.dram_tensor(n, list(sh), dt, kind="ExternalInput").ap()
        do = lambda n, sh, dt: nc.dram_tensor(n, list(sh), dt, kind="ExternalOutput").ap()
        d.update({
            "gqT": di("gqT", [64, TB], BF16), "gkT": di("gkT", [64, TB], BF16), "gaT": di("gaT", [32, TB], BF16),
            "gk": di("gk", [128, NKT * 64], BF16), "gv": di("gv", [128, NKT * 128], BF16),
            "wg": di("wg", [33, 128], F32),
            "rqT": di("rqT", [64, TB], BF16), "rkT": di("rkT", [64, TB], BF16), "rv": di("rv", [128, NKT * 128], BF16),
            "rld": di("rld", [128, 2], F32),
            "ogT": do("ogT", [128, TB], BF16), "orT": do("orT", [128, TB], BF16),
        })
        k = self.k
        k["SM"] = S.sbuf("k_SM", [128, 2, 512], BF16)
        S.dma("sp", k["SM"][:], d["SM"], w=[k["SM"]])
        k["TRI"] = S.sbuf("k_TRI", [128, 4, 128], F32)
        S.dma("sp", k["TRI"][:], d["TRI"], w=[k["TRI"]])
        k["IDX"] = S.sbuf("k_IDX", [64, 2, 512], F32)
        S.dma("sp", k["IDX"][:], d["IDX"], w=[k["IDX"]])
        k["PIDX"] = S.sbuf("k_PIDX", [128, 2], F32)
        S.dma("sp", k["PIDX"][:], d["PIDX"], w=[k["PIDX"]])
        self.onef = S.sbuf("c_onef", [128, 1], F32)
        S.op("pool", lambda e: e.memset(self.onef[:], 1.0), w=[self.onef])
        self.acc = S.sbuf("sc_acc", [128, TB], F32)
        self.Sall = [S.sbuf("sc_Sall%d" % i, [64, 9, 128], F32) for i in range(2)]
        self.Sbf = [S.sbuf("sc_Sbf%d" % i, [64, 8, 128], BF16) for i in range(2)]
        self.qin = [S.sbuf("sc_qin%d" % i, [64, 512], BF16) for i in range(2)]
        self.kin = [S.sbuf("sc_kin%d" % i, [64, 512], BF16) for i in range(2)]
        self.kend = [S.sbuf("sc_kend%d" % i, [128, 4, 64], BF16) for i in range(2)]
        self.attm = [S.sbuf("sc_attm%d" % i, [128, 512], BF16) for i in range(2)]
        self.Eq = [S.sbuf("sc_Eq%d" % i, [64, 512], F32) for i in range(2)]
        self.Ek = S.sbuf("sc_Ek", [64, 512], F32)
        self.Eend = S.sbuf("sc_Eend", [128, 4, 64], F32)
        self.sp = S.sbuf("sc_sp", [128, 4, 128], F32)
        self.gaf = S.sbuf("sc_gaf", [33, 512], F32)
        S.op("pool", lambda e: e.memset(self.gaf[32:33, :], 1.0), w=[self.gaf])
        self.gi = 0

    def scan_groups(self, z):
        lat = [(g * 512, 512) for g in range(T // 512)]
        return [(T, L)] + (lat if z == 0 else lat[::-1])

    def chunk_loop(self, z, c0, n, qin, kin, kend, kend_tok, vtm, dec_fn, dec_toks, outb):
        S, k = self.S, self.k
        A, Bk, Ck, Dk = self.ps[0], self.ps[1], self.ps[2], self.ps[3]
        nt = n // 128
        nchunk = 2 * nt
        t0 = c0 // 128
        gi = self.gi
        self.gi += 1
        attm = self.attm[gi % 2]
        Sall, Snext = self.Sall[gi % 2], self.Sall[(gi + 1) % 2]
        Sbf = self.Sbf[gi % 2]
        vv = vtm.t[:, 0:NKT * 128].rearrange("p (i d) -> p i d", d=128)
        for p in range(nt):
            S.op("pe", lambda e: e.matmul(A[:, p * 128:(p + 1) * 128], kin[:, p * 128:(p + 1) * 128],
                                          qin[:, p * 128:(p + 1) * 128], start=(p == 0), stop=(p == nt - 1),
                                          skip_group_check=True), r=[kin, qin], w=[A], signal=(p == nt - 1))
        S.op("dve", lambda e: e.tensor_tensor(out=attm[:, :n], in0=A[:, :n], in1=k["SM"][:, z, :n], op=ALU.mult),
             r=[A, k["SM"]], w=[attm])
        for cc in range(nchunk):
            p, hf = cc // 2, cc % 2
            bank = Bk if hf == 0 else Ck
            S.op("pe", lambda e: e.matmul(bank[0:64, p * 128:(p + 1) * 128], kend[hf * 64:(hf + 1) * 64, p, :],
                                          vv[hf * 64:(hf + 1) * 64, t0 + p, :], start=(p == 0), stop=(p == nt - 1),
                                          skip_group_check=True), r=[kend_tok, vtm], w=[bank], signal=(p == nt - 1))
        order = list(range(nchunk)) if z == 0 else list(range(nchunk - 1, -1, -1))
        pos = {}
        for i, cc in enumerate(order):
            pos[cc] = i
            p, hf = cc // 2, cc % 2
            bank = Bk if hf == 0 else Ck
            last = (i == nchunk - 1)
            o_ = Snext[:, 0, :] if last else Sall[:, i + 1, :]
            S.op("dve", lambda e: e.scalar_tensor_tensor(out=o_, in0=Sall[:, i, :], scalar=dec_fn(cc),
                                                         in1=bank[0:64, p * 128:(p + 1) * 128],
                                                         op0=ALU.mult, op1=ALU.add),
                 r=[Sall, bank] + dec_toks, w=[Snext if last else Sall])
        S.op("act", lambda e: e.activation(out=Sbf[:, 0:nchunk, :], in_=Sall[:, 0:nchunk, :], func=AF.Copy),
             r=[Sall], w=[Sbf])
        nmm = nt * 3
        mi = 0
        for p in range(nt):
            S.op("pe", lambda e: e.matmul(Dk[:, p * 128:(p + 1) * 128], vv[:, t0 + p, :], attm[:, p * 128:(p + 1) * 128],
                                          start=(mi == 0), stop=False, skip_group_check=True),
                 r=[vtm, attm], w=[Dk], signal=False)
            mi += 1
            for hf in range(2):
                cc = 2 * p + hf
                S.op("pe", lambda e: e.matmul(Dk[:, cc * 64:(cc + 1) * 64], Sbf[:, pos[cc], :],
                                              qin[:, cc * 64:(cc + 1) * 64], start=False, stop=(mi == nmm - 1),
                                              skip_group_check=True),
                     r=[Sbf, qin], w=[Dk], signal=(mi == nmm - 1))
                mi += 1
        if z == 0:
            S.op("act", lambda e: e.activation(out=self.acc[:, c0:c0 + n], in_=Dk[:, :n], func=AF.Copy),
                 r=[Dk], w=[self.acc])
        else:
            S.op("dve", lambda e: e.tensor_tensor(out=outb[:, c0:c0 + n], in0=Dk[:, :n], in1=self.acc[:, c0:c0 + n],
                                                  op=ALU.add), r=[Dk, self.acc], w=[outb])

    def scan_init_state(self):
        S = self.S
        Sall = self.Sall[self.gi % 2]
        S.op("pool", lambda e: e.memset(Sall[:, 0, :], 0.0), w=[Sall])

    def gla(self):
        S, k, d = self.S, self.k, self.d
        self.scan_setup()
        self.gen = "gla"
        qT, kT, vtm, ktm, gaT, outb = self.R[0], self.R[1], self.R[2], self.R[3], self.R[4], self.R[5]
        S.dma("sp", qT[0:64, :], d["gqT"], w=[qT])
        S.dma("sp", kT[0:64, :], d["gkT"], w=[kT])
        S.dma("sp", gaT[0:32, :], d["gaT"], w=[gaT])
        S.dma("sp", vtm[:, 0:NKT * 128], d["gv"], w=[vtm])
        S.dma("sp", ktm[:, 0:NKT * 64], d["gk"], w=[ktm])
        kk = ktm.t[:, 0:NKT * 64].rearrange("p (i d) -> p i d", d=64)
        wg = S.sbuf("gla_wg", [33, 128], F32)
        S.dma("sp", wg[:], d["wg"], w=[wg])
        E, Fk, Gk = self.ps[4], self.ps[5], self.ps[6]
        sp, gaf = self.sp, self.gaf
        ex = self.tmp[3]
        for z in range(2):
            self.scan_init_state()
            for (c0, n) in self.scan_groups(z):
                nt = n // 128
                t0 = c0 // 128
                gi = self.gi
                qin, kin, kend, Eq = self.qin[gi % 2], self.kin[gi % 2], self.kend[gi % 2], self.Eq[gi % 2]
                S.op("dve", lambda e: e.tensor_copy(out=gaf[0:32, :n], in_=gaT[0:32, c0:c0 + n]), r=[gaT], w=[gaf])
                for p in range(nt):
                    S.op("pe", lambda e: e.matmul(E[:, p * 128:(p + 1) * 128], gaf[0:33, p * 128:(p + 1) * 128],
                                                  wg[:, :], start=(p == 0), stop=(p == nt - 1), skip_group_check=True),
                         r=[gaf, wg], w=[E], signal=(p == nt - 1))
                S.op("act", lambda e: e.activation(out=ex[:, :n], in_=E[:, :n], func=AF.Exp, scale=-1.0),
                     r=[E], w=[ex])
                S.op("act", lambda e: e.activation(out=sp[:, 0:nt, :], in_=ex[:, :n].rearrange("p (i c) -> p i c", c=128),
                                                   func=AF.Ln, bias=self.onef[:, 0:1], scale=1.0),
                     r=[ex, self.onef], w=[sp])
                for p in range(nt):
                    S.op("pe", lambda e: e.matmul(Fk[0:64, p * 128:(p + 1) * 128], sp[:, p, z * 64:(z + 1) * 64],
                                                  k["TRI"][:, z, :], start=(p == 0), stop=(p == nt - 1),
                                                  skip_group_check=True),
                         r=[sp, k["TRI"]], w=[Fk], signal=(p == nt - 1))
                S.op("act", lambda e: e.activation(out=Eq[:, :n], in_=Fk[0:64, :n], func=AF.Exp), r=[Fk], w=[Eq])
                S.op("act", lambda e: e.activation(out=self.Ek[:, :n], in_=Fk[0:64, :n], func=AF.Exp, scale=-1.0),
                     r=[Fk], w=[self.Ek])
                S.op("dve", lambda e: e.scalar_tensor_tensor(out=qin[:, :n], in0=qT[0:64, c0:c0 + n], scalar=0.125,
                                                             in1=Eq[:, :n], op0=ALU.mult, op1=ALU.mult),
                     r=[qT, Eq], w=[qin])
                S.op("dve", lambda e: e.tensor_tensor(out=kin[:, :n], in0=kT[0:64, c0:c0 + n], in1=self.Ek[:, :n],
                                                      op=ALU.mult), r=[kT, self.Ek], w=[kin])
                for p in range(nt):
                    S.op("pe", lambda e: e.matmul(Gk[:, p * 64:(p + 1) * 64], k["TRI"][:, 2 + z, :],
                                                  sp[:, p, z * 64:(z + 1) * 64], start=(p == 0), stop=(p == nt - 1),
                                                  skip_group_check=True),
                         r=[sp, k["TRI"]], w=[Gk], signal=(p == nt - 1))
                S.op("act", lambda e: e.activation(out=self.Eend[:, 0:nt, :],
                                                   in_=Gk[:, 0:nt * 64].rearrange("p (i c) -> p i c", c=64),
                                                   func=AF.Exp), r=[Gk], w=[self.Eend])
                S.op("dve", lambda e: e.tensor_tensor(out=kend[:, 0:nt, :], in0=kk[:, t0:t0 + nt, :],
                                                      in1=self.Eend[:, 0:nt, :], op=ALU.mult),
                     r=[ktm, self.Eend], w=[kend])
                col = (lambda cc: cc * 64 + 63) if z == 0 else (lambda cc: cc * 64)
                self.chunk_loop(z, c0, n, qin, kin, kend, kend, vtm,
                                lambda cc, Eq=Eq, col=col: Eq[:, col(cc):col(cc) + 1], [Eq], outb)
        S.dma("sp", d["ogT"], outb[:, :], r=[outb])
        self.gla_out_toks = [outb]

    def ret(self):
        S, k, d = self.S, self.k, self.d
        self.scan_setup()
        self.gen = "ret"
        qT, kT, vtm, outb = self.R[0], self.R[1], self.R[2], self.R[5]
        S.dma("sp", qT[0:64, :], d["rqT"], w=[qT])
        S.dma("sp", kT[0:64, :], d["rkT"], w=[kT])
        S.dma("sp", vtm[:, 0:NKT * 128], d["rv"], w=[vtm])
        rld = S.sbuf("ret_rld", [128, 2], F32)
        S.dma("sp", rld[:], d["rld"], w=[rld])
        nlg = S.sbuf("ret_nlg", [128, 2], F32)
        lg = S.sbuf("ret_lg", [128, 2], F32)
        S.op("act", lambda e: e.activation(out=nlg[:], in_=rld[:], func=AF.Exp), r=[rld], w=[nlg])
        S.op("dve", lambda e: e.tensor_scalar(out=lg[:], in0=nlg[:], scalar1=-1.0, scalar2=None, op0=ALU.mult),
             r=[nlg], w=[lg])
        EqR = self.Eq
        spv = Tile(self.sp.t[0:64, :, :].rearrange("p a b -> p (a b)"), "spv")
        spv.b = self.sp.b
        EkR = [self.Ek, spv]
        EendR = S.sbuf("ret_EendR", [128, 2], F32)
        decR = S.sbuf("ret_decR", [64, 2], F32)
        for z in range(2):
            S.op("act", lambda e: e.activation(out=EqR[z][:], in_=k["IDX"][:, z, :], func=AF.Exp,
                                               scale=lg[0:64, z:z + 1]), r=[k["IDX"], lg], w=[EqR[z]])
            S.op("act", lambda e: e.activation(out=EkR[z][:], in_=k["IDX"][:, z, :], func=AF.Exp,
                                               scale=nlg[0:64, z:z + 1]), r=[k["IDX"], nlg], w=[EkR[z]])
            S.op("act", lambda e: e.activation(out=EendR[:, z:z + 1], in_=k["PIDX"][:, z:z + 1], func=AF.Exp,
                                               scale=lg[:, z:z + 1]), r=[k["PIDX"], lg], w=[EendR])
        S.op("act", lambda e: e.activation(out=decR[:], in_=lg[0:64, :], func=AF.Exp, scale=64.0), r=[lg], w=[decR])
        Hk = self.ps[7]
        rq_r, rk_r = self.tbf[0], self.tbf[1]
        for z in range(2):
            self.scan_init_state()
            for (c0, n) in self.scan_groups(z):
                nt = n // 128
                gi = self.gi
                qin, kin, kend = self.qin[gi % 2], self.kin[gi % 2], self.kend[gi % 2]
                if c0 < T:
                    self.rope_block(qT, rq_r, 0, n, 64, src_c0=c0)
                    self.rope_block(kT, rk_r, 0, n, 64, scale=0.125, src_c0=c0)
                    qsrc = rq_r[0:64, :n]
                    qtok = rq_r
                else:
                    S.op("act", lambda e: e.activation(out=rk_r[0:64, :n], in_=kT[0:64, c0:c0 + n], func=AF.Copy,
                                                       scale=0.125), r=[kT], w=[rk_r])
                    qsrc = qT[0:64, c0:c0 + n]
                    qtok = qT
                S.op("dve", lambda e: e.tensor_tensor(out=qin[:, :n], in0=qsrc, in1=EqR[z][:, :n], op=ALU.mult),
                     r=[qtok, EqR[z]], w=[qin])
                S.op("dve", lambda e: e.tensor_tensor(out=kin[:, :n], in0=rk_r[0:64, :n], in1=EkR[z][:, :n],
                                                      op=ALU.mult), r=[rk_r, EkR[z]], w=[kin])
                for p in range(nt):
                    S.op("pe", lambda e: e.matmul(Hk[:, p * 64:(p + 1) * 64], rk_r[0:64, p * 128:(p + 1) * 128],
                                                  k["IDN"][0:64, 0:64], start=(p == 0), stop=(p == nt - 1),
                                                  skip_group_check=True),
                         r=[rk_r, k["IDN"]], w=[Hk], signal=(p == nt - 1))
                S.op("dve", lambda e: e.tensor_scalar(out=kend[:, 0:nt, :],
                                                      in0=Hk[:, 0:nt * 64].rearrange("p (i c) -> p i c", c=64),
                                                      scalar1=EendR[:, z:z + 1], scalar2=None, op0=ALU.mult),
                     r=[Hk, EendR], w=[kend])
                self.chunk_loop(z, c0, n, qin, kin, kend, kend, vtm, lambda cc, z=z: decR[:, z:z + 1], [decR], outb)
        S.dma("sp", d["orT"], outb[:, :], r=[outb])
        self.ret_out_toks = [outb]


def build_p2(which=("attn", "gla", "ret"), stage=9, nqb=None):
    P = P2()
    outs = []
    if "attn" in which:
        P.attention(stage, nqb)
        outs += P.att_out_toks
    if "gla" in which:
        P.gla()
        outs += P.gla_out_toks
    if "ret" in which:
        P.ret()
        outs += P.ret_out_toks
    P.S.finish(outs, "sp")
    return P.nc, P.S


def tm_layout(a):
    n = a.shape[1]
    return np.ascontiguousarray(a.reshape(-1, 128, n).transpose(1, 0, 2).reshape(128, -1))


_PROG = {}


def _prog(name):
    if name not in _PROG:
        if name == "p1":
            _PROG[name] = build_p1()[0]
        elif name == "p2":
            _PROG[name] = build_p2()[0]
        else:
            _PROG[name] = build_p3()[0]
    return _PROG[name]


def _run(name, maps):
    res = run_bass_kernel_spmd(_prog(name), maps, core_ids=list(range(NCORE)))
    return res.results


def p2_inmaps(inp, l, zfm_b, ztm_b, consts, cos, sin):
    maps = []
    gw, gb_ = inp["gla_gate_w"][l], inp["gla_gate_b"][l]
    for c in range(NCORE):
        b, hh = c // 4, c % 4
        kvh = hh // 2
        zf, zt = zfm_b[b], ztm_b[b]

        def fm(name, o, n):
            return np.ascontiguousarray(zf[FMO[name] + o:FMO[name] + o + n])

        def tm(name, o, n):
            return tm_layout(zt[:, TMO[name] + o:TMO[name] + o + n])
        ak = fm("ak", kvh * 64, 64)
        anp = np.zeros((128, 4), np.float32)
        anp[:, 0] = np.tile(inp["attn_q_norm_g"][l], 2)
        anp[:, 1] = np.tile(inp["attn_k_norm_g"][l], 2)
        anp[:, 2] = inp["attn_sink"][l][2 * hh]
        anp[:, 3] = inp["attn_sink"][l][2 * hh + 1]
        wg = np.zeros((33, 128), np.float32)
        wg[0:16, 0:64] = gw[0][:, hh * 64:(hh + 1) * 64]
        wg[16:32, 64:128] = gw[1][:, hh * 64:(hh + 1) * 64]
        wg[32, 0:64] = gb_[0, hh * 64:(hh + 1) * 64]
        wg[32, 64:128] = gb_[1, hh * 64:(hh + 1) * 64]
        m = {
            "aqT": fm("aq", 2 * hh * 64, 128), "akT": np.ascontiguousarray(np.concatenate([ak, ak], 0)),
            "av": tm("av", kvh * 64, 64), "cosT": cos, "sinT": sin, "anp": anp,
            "gqT": fm("gq", hh * 64, 64), "gkT": fm("gk", hh * 64, 64), "gaT": fm("ga", 0, 32),
            "gk": tm("gk", hh * 64, 64), "gv": tm("gv", hh * 128, 128), "wg": wg,
            "rqT": fm("rq", hh * 64, 64), "rkT": fm("rk", hh * 64, 64), "rv": tm("rv", hh * 128, 128),
            "rld": np.ascontiguousarray(np.broadcast_to(inp["ret_log_decay"][l][:, hh], (128, 2))).astype(np.float32),
        }
        m.update(consts)
        maps.append(m)
    return maps


def kernel(**inp):
    inp = {k: np.asarray(v) for k, v in inp.items()}
    x = inp["x"].astype(np.float32)
    ctx = inp["ctx"].astype(np.float32)
    consts = p2_consts_np()
    cos, sin = rope_tables_np()
    for l in range(DEPTH):
        r1 = _run("p1", p1_inmaps(inp, l, x, ctx))
        zfm_b, ztm_b = [], []
        for b in range(B):
            zf = [np.asarray(r1[b * 4 + s]["zfm"]) for s in range(4)]
            zt = [np.asarray(r1[b * 4 + s]["ztm"]) for s in range(4)]
            zfm_b.append(np.concatenate([z[:, :TS] for z in zf] + [z[:, TS:] for z in zf], axis=1))
            ztm_b.append(np.concatenate([z[:TS] for z in zt] + [z[TS:] for z in zt], axis=0))
        modT_list = [np.asarray(r1[c]["modT"]) for c in range(NCORE)]
        del r1
        r2 = _run("p2", p2_inmaps(inp, l, zfm_b, ztm_b, consts, cos, sin))
        o_b = []
        for b in range(B):
            oa = np.concatenate([np.asarray(r2[b * 4 + hh]["oaT"]) for hh in range(4)], axis=0)
            og = np.concatenate([np.asarray(r2[b * 4 + hh]["ogT"]) for hh in range(4)], axis=0)
            orr = np.concatenate([np.asarray(r2[b * 4 + hh]["orT"]) for hh in range(4)], axis=0)
            o_b.append(np.ascontiguousarray(np.concatenate([oa, og, orr], axis=0).T))
        del r2
        o_all = np.stack(o_b, 0)
        g_all = np.stack([np.ascontiguousarray(
            np.concatenate([zfm_b[b][FMO["gr"]:FMO["gr"] + 512], zfm_b[b][FMO["rg"]:FMO["rg"] + 512],
                            zfm_b[b][FMO["mg"]:FMO["mg"] + 3072]], axis=0).T) for b in range(B)], 0)
        del zfm_b, ztm_b
        r3 = _run("p3", p3_inmaps(inp, l, x, ctx, modT_list, o_all[:, :T], o_all[:, T:], g_all[:, :T], g_all[:, T:]))
        xn = np.empty_like(x)
        cn = np.empty_like(ctx)
        for c in range(NCORE):
            b, seg = c // 4, c % 4
            xo = np.asarray(r3[c]["xo"])
            xn[b, seg * TS:(seg + 1) * TS] = xo[:, :TS].T
            cn[b, seg * LS:(seg + 1) * LS] = xo[:, TS:].T
        x, ctx = xn, cn
    return x.astype(np.float32)
```

```python
import numpy as np
import ml_dtypes
import concourse.bass as bass
import concourse.mybir as mybir
from concourse.bass_utils import run_bass_kernel_spmd

F32 = mybir.dt.float32
BF16 = mybir.dt.bfloat16
AF = mybir.ActivationFunctionType
ALU = mybir.AluOpType
AX = mybir.AxisListType

D = 1024
B = 2
T = 8192
L = 256
DEPTH = 4
NCORE = 8
TS = T // 4
LS = L // 4
NT = TS + LS
TB = T + L
D_FF = 2816
D_IN = 6944
EPS = 1e-6
GRID_W = 64

OFF = {}
_o = 0
for _n, _s in (("aq", 512), ("ak", 128), ("av", 128), ("gq", 256), ("gk", 256), ("gv", 512), ("gr", 512),
               ("ga", 32), ("rq", 256), ("rk", 256), ("rv", 512), ("rg", 512), ("mg", 3072)):
    OFF[_n] = (_o, _s)
    _o += _s
assert _o == D_IN


class Buf:
    __slots__ = ("name", "w", "r")

    def __init__(self, name):
        self.name = name
        self.w = None
        self.r = {}


class Tile:
    def __init__(self, t, name):
        self.t = t
        self.b = Buf(name)

    def __getitem__(self, idx):
        return self.t[idx]


class Sched:
    def __init__(self, nc):
        self.nc = nc
        self.eng = {"pe": nc.tensor, "act": nc.scalar, "dve": nc.vector, "pool": nc.gpsimd, "sp": nc.sync}
        self.sem = {}
        self.cnt = {}
        self.seen = {k: {} for k in self.eng}
        self.nwait = 0
        self.nins = 0
        for k in ("pe", "act", "dve", "pool"):
            self.sem[k] = nc.alloc_semaphore("s_" + k)
            self.cnt[k] = 0
        self._uid = 0

    def sbuf(self, name, shape, dtype):
        return Tile(self.nc.alloc_sbuf_tensor("sb_" + name, list(shape), dtype), name)

    def psum(self, name, shape, dtype=F32):
        return Tile(self.nc.alloc_psum_tensor("pp_" + name, list(shape), dtype), name)

    @staticmethod
    def _b(x):
        return x.b if isinstance(x, Tile) else x

    def _deps(self, ek, reads, writes):
        deps = {}

        def add(m):
            if m is None:
                return
            k, v = m
            if deps.get(k, 0) < v:
                deps[k] = v
        for b in reads:
            add(self._b(b).w)
        for b in writes:
            b = self._b(b)
            add(b.w)
            for k, v in b.r.items():
                add((k, v))
        if ek == "pe":
            deps.pop("pe", None)
        e = self.eng[ek]
        seen = self.seen[ek]
        for k, v in deps.items():
            if seen.get(k, 0) < v:
                e.wait_ge(self.sem[k], v)
                seen[k] = v
                self.nwait += 1

    def _mark(self, mark, reads, writes):
        k, v = mark
        for b in writes:
            b = self._b(b)
            b.w = mark
            b.r = {}
        for b in reads:
            b = self._b(b)
            if b.r.get(k, 0) < v:
                b.r[k] = v

    def op(self, ek, fn, r=(), w=(), signal=True):
        self._deps(ek, r, w)
        ins = fn(self.eng[ek])
        self.nins += 1
        if signal:
            self.cnt[ek] += 1
            ins.then_inc(self.sem[ek], 1)
            mark = (ek, self.cnt[ek])
        else:
            assert ek == "pe"
            mark = (ek, self.cnt[ek] + 1)
        self._mark(mark, r, w)
        return ins

    def dma(self, qk, out, in_, r=(), w=(), key=None):
        self._deps(qk, r, w)
        tok = self._b(w[0]) if len(w) else self._b(r[0])
        k = ("dma", key or tok.name)
        if k not in self.sem:
            self._uid += 1
            self.sem[k] = self.nc.alloc_semaphore("sd%d" % self._uid)
            self.cnt[k] = 0
        self.cnt[k] += 16
        self.eng[qk].dma_start(out=out, in_=in_).then_inc(self.sem[k], 16)
        self.nins += 1
        self._mark((k, self.cnt[k]), r, w)

    def barrier(self):
        for ek, e in self.eng.items():
            for k, v in self.cnt.items():
                if v > 0 and self.seen[ek].get(k, 0) < v:
                    e.wait_ge(self.sem[k], v)
                    self.seen[ek][k] = v

    def finish(self, toks, ek="sp"):
        self._deps(ek, [], toks)
        e = self.eng[ek]
        for k, v in self.cnt.items():
            if v > 0 and self.seen[ek].get(k, 0) < v:
                e.wait_ge(self.sem[k], v)
                self.seen[ek][k] = v


def token_blocks():
    blks = [(i * 512, 512, 0) for i in range(TS // 512)]
    blks.append((TS, LS, 1))
    return blks


def token_tiles():
    tl = [(i * 128, 128) for i in range(TS // 128)]
    tl.append((TS, LS))
    return tl


FM_GROUPS = [
    (0, 512), (512, 128), (768, 512), (1792, 512), (2304, 32), (2336, 512), (3360, 512),
    (3872, 512), (4384, 512), (4896, 512), (5408, 512), (5920, 512), (6432, 512)]
FM_ROWS = sum(n for _, n in FM_GROUPS)
FMO = {"aq": 0, "ak": 512, "gq": 640, "gk": 896, "gr": 1152, "ga": 1664, "rq": 1696, "rk": 1952,
       "rg": 2208, "mg": 2720}
TM_GROUPS = [(640, 128), (1280, 512), (2848, 512), (1024, 256), (2592, 256)]
TM_COLS = sum(n for _, n in TM_GROUPS)
TMO = {"av": 0, "gv": 128, "rv": 640, "gk": 1152, "rk": 1408}


class Consts:
    def __init__(self, S):
        self.ones_bf = S.sbuf("ones_bf", [128, 128], BF16)
        S.op("pool", lambda e: e.memset(self.ones_bf[:], 1.0), w=[self.ones_bf])
        self.deps = S.sbuf("c_deps", [128, 1], F32)
        S.op("pool", lambda e: e.memset(self.deps[:], float(D * EPS)), w=[self.deps])


def emit_mod(S, cs_d, ada_w_d, ada_b_d, modT, ps):
    nc = S.nc
    cs = S.sbuf("cs", [128, 8, 2], F32)
    sg = S.sbuf("cs_sg", [128, 8, 2], F32)
    ab = S.sbuf("ada_b", [128, 48], F32)
    S.dma("sp", cs[:], cs_d, w=[cs])
    S.dma("sp", ab[:], ada_b_d, w=[ab])
    S.op("act", lambda e: e.activation(out=sg[:], in_=cs[:], func=AF.Sigmoid), r=[cs], w=[sg])
    S.op("dve", lambda e: e.tensor_tensor(out=cs[:], in0=cs[:], in1=sg[:], op=ALU.mult), r=[sg, cs], w=[cs])
    GW = 384
    wb = [S.sbuf("adaw%d" % i, [128, 8, GW], F32) for i in range(2)]
    for g in range(6144 // GW):
        wt = wb[g % 2]
        S.dma("sp", wt[:], ada_w_d[:, g * GW:(g + 1) * GW].rearrange("(c p) n -> p c n", p=128), w=[wt])
        for fc in range(GW // 128):
            ch = g * (GW // 128) + fc
            for kc in range(8):
                S.op("pe", lambda e: e.matmul(ps[:, ch * 2:ch * 2 + 2], wt[:, kc, fc * 128:(fc + 1) * 128],
                                              cs[:, kc, :], start=(kc == 0), stop=(kc == 7)),
                     r=[wt, cs], w=[ps], signal=(kc == 7))
    for j in range(2):
        S.op("dve", lambda e: e.tensor_tensor(out=modT[:, :, j], in0=ps[:, j:96:2], in1=ab[:], op=ALU.add),
             r=[ps, ab], w=[modT])


def emit_norm_mod(S, C, xT, hT, htoks, gmod, modT, part_sh, ps_ss, name):
    sq = S.sbuf(name + "_sq", [128, 8, 512], BF16)
    rstd = S.sbuf(name + "_rstd", [128, 512], F32)
    tmp = [S.sbuf(name + "_tmp%d" % i, [128, 512], F32) for i in range(2)]
    for bi, (t0, nb, j) in enumerate(token_blocks()):
        S.op("act", lambda e: e.activation(out=sq[:, :, :nb], in_=xT[:, :, t0:t0 + nb], func=AF.Square),
             r=[xT], w=[sq])
        for c in range(8):
            S.op("pe", lambda e: e.matmul(ps_ss[:, :nb], C.ones_bf[:], sq[:, c, :nb], start=(c == 0), stop=(c == 7)),
                 r=[C.ones_bf, sq], w=[ps_ss], signal=(c == 7))
        S.op("act", lambda e: e.activation(out=rstd[:, :nb], in_=ps_ss[:, :nb], func=AF.Sqrt,
                                           bias=C.deps[:, 0:1], scale=1.0), r=[ps_ss, C.deps], w=[rstd])
        S.op("dve", lambda e: e.reciprocal(out=rstd[:, :nb], in_=rstd[:, :nb]), r=[rstd], w=[rstd])
        for c in range(8):
            tt = tmp[c % 2]
            S.op("dve", lambda e: e.scalar_tensor_tensor(out=tt[:, :nb], in0=xT[:, c, t0:t0 + nb],
                                                         scalar=gmod[:, c, j:j + 1], in1=rstd[:, :nb],
                                                         op0=ALU.mult, op1=ALU.mult),
                 r=[xT, gmod, rstd], w=[tt])
            S.op("act", lambda e: e.activation(out=hT[:, c, t0:t0 + nb], in_=tt[:, :nb], func=AF.Identity,
                                               bias=modT[:, part_sh * 8 + c, j:j + 1], scale=1.0),
                 r=[tt, modT], w=[htoks[bi]])


def emit_gmod(S, gmod, g_d, modT, part_sc, name):
    g = S.sbuf(name + "_g", [128, 8], F32)
    S.dma("sp", g[:], g_d, w=[g])
    for j in range(2):
        S.op("dve", lambda e: e.tensor_scalar(out=gmod[:, :, j], in0=modT[:, part_sc * 8:(part_sc + 1) * 8, j],
                                              scalar1=1.0, scalar2=float(np.sqrt(D)), op0=ALU.add, op1=ALU.mult),
             r=[modT], w=[gmod])
        S.op("dve", lambda e: e.tensor_tensor(out=gmod[:, :, j], in0=gmod[:, :, j], in1=g[:], op=ALU.mult),
             r=[gmod, g], w=[gmod])


def build_p1():
    nc = bass.Bass("TRN2", target_bir_lowering=False)
    S = Sched(nc)
    xT_d = nc.dram_tensor("xT", [D, NT], F32, kind="ExternalInput").ap()
    cs_d = nc.dram_tensor("cs", [128, 8, 2], F32, kind="ExternalInput").ap()
    ada_w_d = nc.dram_tensor("ada_w", [D, 6 * D], F32, kind="ExternalInput").ap()
    ada_b_d = nc.dram_tensor("ada_b", [128, 48], F32, kind="ExternalInput").ap()
    g1_d = nc.dram_tensor("norm1_g", [128, 8], F32, kind="ExternalInput").ap()
    w_in_d = nc.dram_tensor("w_in", [D, D_IN], F32, kind="ExternalInput").ap()
    zfm_d = nc.dram_tensor("zfm", [FM_ROWS, NT], BF16, kind="ExternalOutput").ap()
    ztm_d = nc.dram_tensor("ztm", [NT, TM_COLS], BF16, kind="ExternalOutput").ap()
    mod_d = nc.dram_tensor("modT", [128, 96], F32, kind="ExternalOutput").ap()

    C = Consts(S)
    xT = S.sbuf("xT", [128, 8, NT], F32)
    hT = S.sbuf("hT", [128, 8, NT], BF16)
    htoks = [Buf("hT%d" % i) for i in range(len(token_blocks()))]
    modT = S.sbuf("modT", [128, 48, 2], F32)
    gmod = S.sbuf("gmod1", [128, 8, 2], F32)
    ps_mod = S.psum("ps_mod", [128, 512])
    ps_ss = S.psum("ps_ss", [128, 512])
    ps_o = [S.psum("ps_o%d" % i, [128, 512]) for i in range(4)]

    S.dma("sp", xT[:], xT_d.rearrange("(c p) t -> p c t", p=128), w=[xT])
    emit_mod(S, cs_d, ada_w_d, ada_b_d, modT, ps_mod)
    S.dma("sp", mod_d.rearrange("p (c j) -> p c j", j=2), modT[:], r=[modT])
    emit_gmod(S, gmod, g1_d, modT, 1, "n1")
    emit_norm_mod(S, C, xT, hT, htoks, gmod, modT, 0, ps_ss, "n1")

    wbufs = [S.sbuf("w%d" % i, [128, 8, 512], BF16) for i in range(3)]
    stg_fm = [S.sbuf("stgfm%d" % i, [128, NT], BF16) for i in range(3)]
    stg_tm = [S.sbuf("stgtm%d" % i, [128, 17, 512], BF16) for i in range(1)]
    gi = 0
    pi = 0
    ei = 0
    si = 0
    row0 = 0
    blks = token_blocks()
    for (c0, n) in FM_GROUPS:
        wt = wbufs[gi % 3]
        gi += 1
        S.dma("pool", wt[:, :, :n], w_in_d[:, c0:c0 + n].rearrange("(c p) n -> p c n", p=128), w=[wt])
        for f0 in range(0, n, 128):
            m = min(128, n - f0)
            stg = stg_fm[si % 3]
            si += 1
            for bi, (t0, nb, j) in enumerate(blks):
                ps = ps_o[pi % 4]
                pi += 1
                for kc in range(8):
                    S.op("pe", lambda e: e.matmul(ps[:m, :nb], wt[:, kc, f0:f0 + m], hT[:, kc, t0:t0 + nb],
                                                  start=(kc == 0), stop=(kc == 7)),
                         r=[wt, htoks[bi]], w=[ps], signal=(kc == 7))
                if ei % 2 == 0:
                    S.op("act", lambda e: e.activation(out=stg[:m, t0:t0 + nb], in_=ps[:m, :nb], func=AF.Copy),
                         r=[ps], w=[stg])
                else:
                    S.op("dve", lambda e: e.tensor_copy(out=stg[:m, t0:t0 + nb], in_=ps[:m, :nb]), r=[ps], w=[stg])
                ei += 1
            S.dma("sp", zfm_d[row0 + f0:row0 + f0 + m, :], stg[:m, :], r=[stg])
        row0 += n
    col0 = 0
    tiles = token_tiles()
    for gidx, (c0, n) in enumerate(TM_GROUPS):
        wt = wbufs[gi % 3]
        gi += 1
        S.dma("pool", wt[:, :, :n], w_in_d[:, c0:c0 + n].rearrange("(c p) n -> p c n", p=128), w=[wt])
        stg = stg_tm[0]
        for ti, (t0, nt) in enumerate(tiles):
            ps = ps_o[pi % 4]
            pi += 1
            bi = min(t0 // 512, len(blks) - 1)
            for kc in range(8):
                S.op("pe", lambda e: e.matmul(ps[:nt, :n], hT[:, kc, t0:t0 + nt], wt[:, kc, :n],
                                              start=(kc == 0), stop=(kc == 7)),
                     r=[wt, htoks[bi]], w=[ps], signal=(kc == 7))
            if ei % 2 == 0:
                S.op("act", lambda e: e.activation(out=stg[:nt, ti, :n], in_=ps[:nt, :n], func=AF.Copy),
                     r=[ps], w=[stg])
            else:
                S.op("dve", lambda e: e.tensor_copy(out=stg[:nt, ti, :n], in_=ps[:nt, :n]), r=[ps], w=[stg])
            ei += 1
        S.dma("sp", ztm_d[0:TS, col0:col0 + n].rearrange("(i p) n -> p i n", p=128), stg[:, 0:16, :n], r=[stg])
        S.dma("sp", ztm_d[TS:NT, col0:col0 + n], stg[:LS, 16, :n], r=[stg])
        col0 += n
    S.finish(stg_fm + stg_tm + [modT], "sp")
    return nc, S


def pc(v, nchunk):
    return np.ascontiguousarray(np.asarray(v).reshape(nchunk, 128).T)


def p1_inmaps(inp, l, x, ctx):
    maps = []
    for c in range(NCORE):
        b, seg = c // 4, c % 4
        xt = np.concatenate([x[b, seg * TS:(seg + 1) * TS], ctx[b, seg * LS:(seg + 1) * LS]], axis=0).T
        cs = np.stack([inp["c"][b], inp["c_ctx"]], axis=1)
        cs = np.ascontiguousarray(cs.reshape(8, 128, 2).transpose(1, 0, 2))
        maps.append({
            "xT": np.ascontiguousarray(xt, dtype=np.float32),
            "cs": cs.astype(np.float32),
            "ada_w": inp["ada_w"][l],
            "ada_b": pc(inp["ada_b"][l], 48),
            "norm1_g": pc(inp["norm1_g"][l], 8),
            "w_in": inp["w_in"][l],
        })
    return maps


NL3 = TS + 2
NC3 = LS + 2
NT3 = NL3 + NC3
BLK3 = [(0, 512, 0), (512, 512, 0), (1024, 512, 0), (1536, 512, 0), (2048, 2, 0), (NL3, NC3, 1)]
FFN_GROUPS = [[(1, 352, 0), (353, 352, 0)], [(705, 352, 0), (1057, 352, 0)],
              [(1409, 352, 0), (1761, 288, 0), (NL3 + 1, LS, 1)]]
GW3 = 770


def build_p3():
    nc = bass.Bass("TRN2", target_bir_lowering=False)
    S = Sched(nc)
    xT_d = nc.dram_tensor("xT", [D, NT3], F32, kind="ExternalInput").ap()
    mod_d = nc.dram_tensor("modT", [128, 96], F32, kind="ExternalInput").ap()
    g2_d = nc.dram_tensor("norm2_g", [128, 8], F32, kind="ExternalInput").ap()
    og_d = nc.dram_tensor("onorm_g", [128, 2], F32, kind="ExternalInput").ap()
    hm_d = nc.dram_tensor("hmask", [128, 4], F32, kind="ExternalInput").ap()
    oT_d = nc.dram_tensor("oT", [3 * 512, NT3], BF16, kind="ExternalInput").ap()
    gT_d = nc.dram_tensor("gT", [4096, NT3], BF16, kind="ExternalInput").ap()
    wb_d = nc.dram_tensor("w_branch", [3, 512, D], F32, kind="ExternalInput").ap()
    wo_d = nc.dram_tensor("w_out", [D, D], F32, kind="ExternalInput").ap()
    up_d = nc.dram_tensor("ffn_up", [D, 2 * D_FF], F32, kind="ExternalInput").ap()
    cw_d = nc.dram_tensor("conv_w", [128, 3, 44], F32, kind="ExternalInput").ap()
    cb_d = nc.dram_tensor("conv_b", [128, 44], F32, kind="ExternalInput").ap()
    dn_d = nc.dram_tensor("ffn_down", [D_FF, D], F32, kind="ExternalInput").ap()
    xo_d = nc.dram_tensor("xo", [D, NT], F32, kind="ExternalOutput").ap()
    xm_d = nc.dram_tensor("xmid_scratch", [D, NT3], F32, kind="Internal").ap()

    C = Consts(S)
    NX = 8 * NT3 * 2
    big = S.sbuf("big", [128, NX + 8 * NT3], BF16)
    xT = Tile(big.t[:, 0:NX].bitcast(F32).rearrange("p (c t) -> p c t", c=8), "xTv")
    mTt = Tile(big.t[:, NX:NX + 8 * NT3].rearrange("p (c t) -> p c t", c=8), "mTv")
    mT = mTt.t
    arenaA = S.sbuf("arenaA", [128, 12 * NT3], BF16)
    yb = arenaA.t[:, :].rearrange("p (c t) -> p c t", c=12)
    h2T = arenaA.t[:, 0:8 * NT3].rearrange("p (c t) -> p c t", c=8)
    scr = S.sbuf("scr", [128, 3 * NT3], BF16)
    mgt = [Tile(scr.t[:, i * NT3:(i + 1) * NT3], "mgt%d" % i) for i in range(3)]
    gate = mgt[0:2]
    modT = S.sbuf("modT", [128, 48, 2], F32)
    gmod = S.sbuf("gmod2", [128, 8, 2], F32)
    ong = S.sbuf("ong", [128, 2], F32)
    hm = S.sbuf("hm", [128, 4], F32)
    cw = S.sbuf("cw", [128, 3, 44], F32)
    cb = S.sbuf("cb", [128, 44], F32)
    c128 = S.sbuf("c_128eps", [128, 1], F32)
    S.op("pool", lambda e: e.memset(c128[:], float(128 * EPS)), w=[c128])
    ps = [S.psum("ps%d" % i, [128, 512]) for i in range(8)]

    S.dma("sp", xT[:], xT_d.rearrange("(c p) t -> p c t", p=128), w=[xT])
    S.dma("sp", modT[:], mod_d.rearrange("p (c j) -> p c j", j=2), w=[modT])
    S.dma("sp", ong[:], og_d, w=[ong])
    S.dma("sp", hm[:], hm_d, w=[hm])
    S.dma("sp", cw[:], cw_d, w=[cw])
    S.dma("sp", cb[:], cb_d, w=[cb])
    S.op("dve", lambda e: e.tensor_scalar(out=ong[:], in0=ong[:], scalar1=float(np.sqrt(128.0)), scalar2=None,
                                          op0=ALU.mult), r=[ong], w=[ong])
    S.dma("sp", yb, oT_d.rearrange("(c p) t -> p c t", p=128), w=[arenaA])
    sqs = [S.sbuf("sq3_%d" % i, [128, 512], BF16) for i in range(2)]
    rstds = [S.sbuf("rstd3_%d" % i, [128, 512], F32) for i in range(2)]
    tmpf = [S.sbuf("tmpf%d" % i, [128, 512], F32) for i in range(2)]
    sgt = [S.sbuf("sgt%d" % i, [128, 512], F32) for i in range(2)]
    ybt = [Buf("yb%d" % i) for i in range(12)]
    pi = 0
    k = 0
    first = True
    for z in (1, 2):
        for hc in range(4):
            ci = z * 4 + hc
            gt = gate[hc % 2]
            S.dma("sp", gt[:], gT_d[(z - 1) * 512 + hc * 128:(z - 1) * 512 + (hc + 1) * 128, :], w=[gt])
            S.op("act", lambda e: e.activation(out=gt[:], in_=gt[:], func=AF.Silu), r=[gt], w=[gt])
            for (t0, nb, j) in BLK3:
                p_ = ps[pi % 4]
                pi += 1
                sq, rstd = sqs[k % 2], rstds[k % 2]
                S.op("pool", lambda e: e.tensor_tensor(out=sq[:, :nb], in0=yb[:, ci, t0:t0 + nb],
                                                       in1=yb[:, ci, t0:t0 + nb], op=ALU.mult),
                     r=[arenaA, ybt[ci]], w=[sq])
                S.op("pe", lambda e: e.matmul(p_[:, :nb], C.ones_bf[:], sq[:, :nb], start=True, stop=True),
                     r=[sq, C.ones_bf], w=[p_])
                S.op("act", lambda e: e.activation(out=rstd[:, :nb], in_=p_[:, :nb], func=AF.Ln,
                                                   bias=c128[:, 0:1], scale=1.0), r=[p_, c128], w=[rstd])
                S.op("act", lambda e: e.activation(out=rstd[:, :nb], in_=rstd[:, :nb], func=AF.Exp, scale=-0.5),
                     r=[rstd], w=[rstd])
                tf = tmpf[k % 2]
                S.op("dve", lambda e: e.scalar_tensor_tensor(out=tf[:, :nb], in0=yb[:, ci, t0:t0 + nb],
                                                             scalar=ong[:, z - 1:z], in1=rstd[:, :nb],
                                                             op0=ALU.mult, op1=ALU.mult),
                     r=[arenaA, ybt[ci], ong, rstd], w=[tf])
                S.op("pool", lambda e: e.tensor_tensor(out=yb[:, ci, t0:t0 + nb], in0=tf[:, :nb],
                                                       in1=gt[:, t0:t0 + nb], op=ALU.mult),
                     r=[tf, gt, arenaA], w=[ybt[ci]])
                k += 1
    ybr = [ybt[i] for i in range(12)]
    wbr = [S.sbuf("wbr%d" % i, [128, 4, 128], BF16) for i in range(6)]
    wi = 0
    for oc in range(8):
        wts = []
        for z in range(3):
            wt = wbr[wi % 6]
            wi += 1
            mg = mgt[z]
            S.dma("pool", wt[:], wb_d[z, :, oc * 128:(oc + 1) * 128].rearrange("(c p) n -> p c n", p=128), w=[wt])
            S.dma("sp", mg[:], gT_d[1024 + z * 1024 + oc * 128:1024 + z * 1024 + (oc + 1) * 128, :], w=[mg])
            S.op("act", lambda e: e.activation(out=mg[:], in_=mg[:], func=AF.Sigmoid), r=[mg], w=[mg])
            wts.append(wt)
        for (t0, nb, j) in BLK3:
            pz = []
            for z in range(3):
                p_ = ps[2 + pi % 6]
                pi += 1
                for kc in range(4):
                    S.op("pe", lambda e: e.matmul(p_[:, :nb], wts[z][:, kc, :], yb[:, z * 4 + kc, t0:t0 + nb],
                                                  start=(kc == 0), stop=(kc == 3)),
                         r=[wts[z], arenaA] + ybr[z * 4:z * 4 + 4], w=[p_], signal=(kc == 3))
                pz.append(p_)
            tA, tB = tmpf[0], tmpf[1]
            S.op("dve", lambda e: e.tensor_tensor(out=tA[:, :nb], in0=pz[0][:, :nb], in1=mgt[0][:, t0:t0 + nb],
                                                  op=ALU.mult), r=[pz[0], mgt[0]], w=[tA])
            S.op("dve", lambda e: e.tensor_tensor(out=tB[:, :nb], in0=pz[1][:, :nb], in1=mgt[1][:, t0:t0 + nb],
                                                  op=ALU.mult), r=[pz[1], mgt[1]], w=[tB])
            S.op("pool", lambda e: e.tensor_tensor(out=tA[:, :nb], in0=tA[:, :nb], in1=tB[:, :nb], op=ALU.add),
                 r=[tA, tB], w=[tA])
            S.op("dve", lambda e: e.tensor_tensor(out=tB[:, :nb], in0=pz[2][:, :nb], in1=mgt[2][:, t0:t0 + nb],
                                                  op=ALU.mult), r=[pz[2], mgt[2]], w=[tB])
            S.op("pool", lambda e: e.tensor_tensor(out=mT[:, oc, t0:t0 + nb], in0=tA[:, :nb], in1=tB[:, :nb],
                                                   op=ALU.add), r=[tA, tB], w=[mTt])
    wo = [S.sbuf("wo%d" % i, [128, 8, 128], BF16) for i in range(2)]
    for oc in range(8):
        wt = wo[oc % 2]
        S.dma("pool", wt[:], wo_d[:, oc * 128:(oc + 1) * 128].rearrange("(c p) n -> p c n", p=128), w=[wt])
        for (t0, nb, j) in BLK3:
            p_ = ps[2 + pi % 4]
            pi += 1
            for kc in range(8):
                S.op("pe", lambda e: e.matmul(p_[:, :nb], wt[:, kc, :], mT[:, kc, t0:t0 + nb],
                                              start=(kc == 0), stop=(kc == 7)),
                     r=[wt, mTt], w=[p_], signal=(kc == 7))
            S.op("dve", lambda e: e.scalar_tensor_tensor(out=xT[:, oc, t0:t0 + nb], in0=p_[:, :nb],
                                                         scalar=modT[:, 16 + oc, j:j + 1], in1=xT[:, oc, t0:t0 + nb],
                                                         op0=ALU.mult, op1=ALU.add),
                 r=[p_, modT, xT], w=[xT])
    S.dma("sp", xm_d.rearrange("(c p) t -> p c t", p=128), xT[:], r=[xT])
    emit_gmod(S, gmod, g2_d, modT, 4, "n2")
    sq8 = scr.t[:, 0:8 * 512].rearrange("p (c t) -> p c t", c=8)
    sq8t = [mgt[0], mgt[1], mgt[2]]
    for (t0, nb, j) in BLK3:
        p_ = ps[pi % 2]
        pi += 1
        rstd = rstds[pi % 2]
        S.op("act", lambda e: e.activation(out=sq8[:, :, :nb], in_=xT[:, :, t0:t0 + nb], func=AF.Square),
             r=[xT], w=sq8t)
        for c in range(8):
            S.op("pe", lambda e: e.matmul(p_[:, :nb], C.ones_bf[:], sq8[:, c, :nb], start=(c == 0), stop=(c == 7)),
                 r=[C.ones_bf] + sq8t, w=[p_], signal=(c == 7))
        S.op("act", lambda e: e.activation(out=rstd[:, :nb], in_=p_[:, :nb], func=AF.Sqrt,
                                           bias=C.deps[:, 0:1], scale=1.0), r=[p_, C.deps], w=[rstd])
        S.op("dve", lambda e: e.reciprocal(out=rstd[:, :nb], in_=rstd[:, :nb]), r=[rstd], w=[rstd])
        for c in range(8):
            tf = tmpf[c % 2]
            S.op("dve", lambda e: e.scalar_tensor_tensor(out=tf[:, :nb], in0=xT[:, c, t0:t0 + nb],
                                                         scalar=gmod[:, c, j:j + 1], in1=rstd[:, :nb],
                                                         op0=ALU.mult, op1=ALU.mult),
                 r=[xT, gmod, rstd], w=[tf])
            S.op("act", lambda e: e.activation(out=h2T[:, c, t0:t0 + nb], in_=tf[:, :nb], func=AF.Identity,
                                               bias=modT[:, 24 + c, j:j + 1], scale=1.0),
                 r=[tf, modT] + ybr, w=[arenaA])
    for i, col in enumerate((0, NL3 - 1, NL3, NT3 - 1)):
        S.op("dve", lambda e: e.tensor_scalar(out=h2T[:, :, col:col + 1], in0=h2T[:, :, col:col + 1],
                                              scalar1=hm[:, i:i + 1], scalar2=None, op0=ALU.mult),
             r=[arenaA, hm], w=[arenaA])
    S.barrier()
    gFt = Buf("gF")
    gF = big.t[:, 0:22 * NT].rearrange("p (c t) -> p c t", c=22)
    spare = arenaA.t[:, 8 * NT3:12 * NT3]
    wupA = [Tile(spare[:, i * 4096:(i + 1) * 4096].rearrange("p (c n) -> p c n", c=8), "wupA%d" % i) for i in range(2)]
    wupB = [S.sbuf("wupB%d" % i, [128, 8, 512], BF16) for i in range(2)]
    wdn = [Tile(scr.t[:, i * 2816:(i + 1) * 2816].rearrange("p (c n) -> p c n", c=22), "wdn%d" % i) for i in range(2)]
    ua = tmpf
    ub = [S.sbuf("ub%d" % i, [128, 512], F32) for i in range(2)]
    xmt = sgt
    blocks = [b_ for grp in FFN_GROUPS for b_ in grp]
    gcol = []
    o = 0
    for (c0, n, j) in blocks:
        gcol.append(o)
        o += n
    assert o == NT
    ui = 0
    for fg in range(6):
        nf = 4 if fg < 5 else 2
        wa, wb_ = wupA[fg % 2], wupB[fg % 2]
        S.dma("pool", wa[:, :, 0:nf * 128], up_d[:, fg * 512:fg * 512 + nf * 128].rearrange("(c p) n -> p c n", p=128),
              w=[wa])
        S.dma("pool", wb_[:, :, 0:nf * 128],
              up_d[:, D_FF + fg * 512:D_FF + fg * 512 + nf * 128].rearrange("(c p) n -> p c n", p=128), w=[wb_])
        for fl in range(nf):
            f = fg * 4 + fl
            for bi, (c0, n, j) in enumerate(blocks):
                pa = ps[pi % 4]
                pb = ps[4 + pi % 4]
                pi += 1
                for wt, p_ in ((wa, pa), (wb_, pb)):
                    for kc in range(8):
                        S.op("pe", lambda e: e.matmul(p_[:, :n + 2], wt[:, kc, fl * 128:(fl + 1) * 128],
                                                      h2T[:, kc, c0 - 1:c0 + n + 1], start=(kc == 0), stop=(kc == 7)),
                             r=[wt, arenaA], w=[p_], signal=(kc == 7))
                a_ = ua[ui % 2]
                b_ = ub[ui % 2]
                ui += 1
                for half, p_, u_ in ((0, pa, a_), (1, pb, b_)):
                    ch = half * 22 + f
                    S.op("act", lambda e: e.activation(out=u_[:, :n], in_=p_[:, 1:n + 1], func=AF.Identity,
                                                       bias=cb[:, ch:ch + 1], scale=cw[:, 1, ch:ch + 1]),
                         r=[p_, cb, cw], w=[u_])
                    S.op("dve", lambda e: e.scalar_tensor_tensor(out=u_[:, :n], in0=p_[:, 0:n],
                                                                 scalar=cw[:, 0, ch:ch + 1], in1=u_[:, :n],
                                                                 op0=ALU.mult, op1=ALU.add),
                         r=[p_, cw, u_], w=[u_])
                    S.op("dve", lambda e: e.scalar_tensor_tensor(out=u_[:, :n], in0=p_[:, 2:n + 2],
                                                                 scalar=cw[:, 2, ch:ch + 1], in1=u_[:, :n],
                                                                 op0=ALU.mult, op1=ALU.add),
                         r=[p_, cw, u_], w=[u_])
                sg = sgt[ui % 2]
                S.op("act", lambda e: e.activation(out=sg[:, :n], in_=a_[:, :n], func=AF.Silu), r=[a_], w=[sg])
                S.op("pool", lambda e: e.tensor_tensor(out=gF[:, f, gcol[bi]:gcol[bi] + n], in0=sg[:, :n],
                                                       in1=b_[:, :n], op=ALU.mult), r=[sg, b_], w=[gFt])
    xo_v = xo_d.rearrange("(c p) t -> p c t", p=128)
    xm_v = xm_d.rearrange("(c p) t -> p c t", p=128)
    xi = 0
    for oc in range(8):
        wt = wdn[oc % 2]
        S.dma("pool", wt[:], dn_d[:, oc * 128:(oc + 1) * 128].rearrange("(c p) n -> p c n", p=128), w=[wt])
        for bi, (c0, n, j) in enumerate(blocks):
            p_ = ps[pi % 8]
            pi += 1
            xm = xmt[xi % 2]
            xi += 1
            S.dma("sp", xm[:, :n], xm_v[:, oc, c0:c0 + n], w=[xm])
            for f in range(22):
                S.op("pe", lambda e: e.matmul(p_[:, :n], wt[:, f, :], gF[:, f, gcol[bi]:gcol[bi] + n],
                                              start=(f == 0), stop=(f == 21)),
                     r=[wt, gFt], w=[p_], signal=(f == 21))
            S.op("dve", lambda e: e.scalar_tensor_tensor(out=xm[:, :n], in0=p_[:, :n],
                                                         scalar=modT[:, 40 + oc, j:j + 1], in1=xm[:, :n],
                                                         op0=ALU.mult, op1=ALU.add),
                 r=[p_, modT, xm], w=[xm])
            S.dma("sp", xo_v[:, oc, gcol[bi]:gcol[bi] + n], xm[:, :n], r=[xm])
    S.finish(xmt, "sp")
    return nc, S


def halo_cols(a, b, seg, n, tot):
    lo, hi = seg * n - 1, (seg + 1) * n + 1
    out = np.zeros((hi - lo,) + a.shape[2:], a.dtype)
    s, e = max(lo, 0), min(hi, tot)
    out[s - lo:e - lo] = a[b, s:e]
    return out


def p3_inmaps(inp, l, x, ctx, modT_list, o_lat, o_ctx, g_lat, g_ctx):
    maps = []
    cwl = inp["ffn_conv_w"][l]
    cw = np.ascontiguousarray(cwl.reshape(3, 44, 128).transpose(2, 0, 1))
    for c in range(NCORE):
        b, seg = c // 4, c % 4

        def cols(al, ac):
            return np.ascontiguousarray(np.concatenate([halo_cols(al, b, seg, TS, T), halo_cols(ac, b, seg, LS, L)], 0).T)
        hm = np.array([seg > 0, seg < 3, seg > 0, seg < 3], np.float32)
        maps.append({
            "xT": cols(x, ctx).astype(np.float32),
            "modT": modT_list[c],
            "norm2_g": pc(inp["norm2_g"][l], 8),
            "onorm_g": np.ascontiguousarray(np.stack([inp["gla_out_norm_g"][l], inp["ret_out_norm_g"][l]], 1)),
            "hmask": np.ascontiguousarray(np.broadcast_to(hm, (128, 4))),
            "oT": cols(o_lat, o_ctx),
            "gT": cols(g_lat, g_ctx),
            "w_branch": inp["w_branch"][l],
            "w_out": inp["w_out"][l],
            "ffn_up": inp["ffn_up"][l],
            "conv_w": cw,
            "conv_b": pc(inp["ffn_conv_b"][l], 44),
            "ffn_down": inp["ffn_down"][l],
        })
    return maps


NQB = T // 128
NKT = TB // 128
NEG = -30000.0


def p2_consts_np():
    bf = ml_dtypes.bfloat16
    c = {}
    bo = np.zeros((128, 128), np.float32)
    bo[:64, :64] = 1
    bo[64:, 64:] = 1
    c["BO"] = bo.astype(bf)
    R = np.zeros((64, 64), np.float32)
    for base in (0, 32):
        for f in range(16):
            R[base + f, base + 16 + f] = -1.0
            R[base + 16 + f, base + f] = 1.0
    rp = np.zeros((128, 128), np.float32)
    rp[:64, :64] = R.T
    rp[64:, 64:] = R.T
    c["RP"] = rp.astype(bf)
    c["IDN"] = np.eye(128, dtype=np.float32).astype(bf)
    s = np.arange(128)[:, None]
    i = np.arange(128)[None, :]
    mb = np.zeros((128, 2, 2, 128), np.float32)
    mb[:, 0] = np.where(s >= i, 0.0, NEG)[:, None, :]
    mb[:, 1] = np.where(s <= i, 0.0, NEG)[:, None, :]
    c["MB"] = mb.reshape(128, 2, 256).astype(bf)
    same = (s // 64) == (i // 64)
    sm = np.zeros((128, 2, 4, 128), np.float32)
    sm[:, 0] = (same & (s <= i))[:, None, :]
    sm[:, 1] = (same & (s >= i))[:, None, :]
    c["SM"] = sm.reshape(128, 2, 512).astype(bf)
    tri = np.zeros((128, 4, 128), np.float32)
    tri[:, 0] = same & (s <= i)
    tri[:, 1] = same & (s >= i)
    tri[:, 2] = same & (s > i)
    tri[:, 3] = same & (s < i)
    c["TRI"] = (tri * (-1.0 / 16.0)).astype(bf)
    t = np.arange(512) % 64
    idx = np.zeros((64, 2, 512), np.float32)
    idx[:, 0] = (t + 1)[None, :]
    idx[:, 1] = (64 - t)[None, :]
    c["IDX"] = idx
    p = np.arange(128) % 64
    c["PIDX"] = np.stack([63 - p, p], 1).astype(np.float32)
    return c


def rope_tables_np():
    tt = np.arange(T)
    row = (tt // GRID_W).astype(np.float32)
    col = (tt % GRID_W).astype(np.float32)
    inv = (10000.0 ** (-np.arange(16, dtype=np.float32) * 2.0 / 32.0)).astype(np.float32)
    ang = np.zeros((64, T), np.float32)
    for d in range(64):
        pos = row if d < 32 else col
        ang[d] = pos * inv[d % 16]
    cos = np.cos(ang).astype(np.float32)
    sin = np.sin(ang).astype(np.float32)
    return cos, sin


class P2:
    def __init__(self):
        nc = bass.Bass("TRN2", target_bir_lowering=False)
        self.nc = nc
        S = Sched(nc)
        self.S = S
        self.C = Consts(S)
        di = lambda n, sh, dt: nc.dram_tensor(n, list(sh), dt, kind="ExternalInput").ap()
        do = lambda n, sh, dt: nc.dram_tensor(n, list(sh), dt, kind="ExternalOutput").ap()
        self.d = {
            "aqT": di("aqT", [128, TB], BF16), "akT": di("akT", [128, TB], BF16), "av": di("av", [128, NKT * 64], BF16),
            "cosT": di("cosT", [64, T], F32), "sinT": di("sinT", [64, T], F32),
            "anp": di("anp", [128, 4], F32),
            "BO": di("BO", [128, 128], BF16), "RP": di("RP", [128, 128], BF16), "IDN": di("IDN", [128, 128], BF16),
            "MB": di("MB", [128, 2, 256], BF16), "SM": di("SM", [128, 2, 512], BF16),
            "TRI": di("TRI", [128, 4, 128], BF16), "IDX": di("IDX", [64, 2, 512], F32), "PIDX": di("PIDX", [128, 2], F32),
            "oaT": do("oaT", [128, TB], BF16),
        }
        self.R = [S.sbuf("R%d" % i, [128, TB], BF16) for i in range(6)]
        self.ps = [S.psum("ps%d" % i, [128, 512]) for i in range(8)]
        self.k = {}
        for n in ("BO", "RP", "IDN"):
            self.k[n] = S.sbuf("k_" + n, [128, 128], BF16)
            S.dma("sp", self.k[n][:], self.d[n], w=[self.k[n]])
        self.k["MB"] = S.sbuf("k_MB", [128, 2, 256], BF16)
        S.dma("sp", self.k["MB"][:], self.d["MB"], w=[self.k["MB"]])
        self.c64 = S.sbuf("c_64eps", [128, 1], F32)
        S.op("pool", lambda e: e.memset(self.c64[:], float(64 * EPS)), w=[self.c64])
        self.tmp = [S.sbuf("p2tmp%d" % i, [128, 512], F32) for i in range(4)]
        self.tbf = [S.sbuf("p2tbf%d" % i, [128, 512], BF16) for i in range(4)]
        self.cs = [S.sbuf("p2cs%d" % i, [128, 2, 512], F32) for i in range(2)]

    def prep_slots(self):
        if hasattr(self, "slots"):
            return
        r3 = self.R[3].t
        def v(lo, hi, f32, name):
            ap = r3[:, lo:hi]
            if f32:
                ap = ap.bitcast(F32)
            return Tile(ap, name)
        self.slots = [
            dict(xin=self.tbf[0], sq=self.tbf[2], xg=self.tbf[3], rstd=self.tmp[0], t1=self.tmp[1], t2=self.tmp[2],
                 pa=self.ps[0], pb=self.ps[1]),
            dict(xin=self.tbf[1], sq=v(0, 512, False, "s1sq"), xg=v(512, 1024, False, "s1xg"),
                 rstd=v(1024, 2048, True, "s1rstd"), t1=v(2048, 3072, True, "s1t1"), t2=v(3072, 4096, True, "s1t2"),
                 pa=self.ps[2], pb=self.ps[3]),
        ]

    def norm_rope_block(self, src_d, dst, c0, n, gain_ap, gain_tok, roped, bi, np_=128):
        S, k = self.S, self.k
        self.prep_slots()
        sl = self.slots[bi % 2]
        xin, sq, xg, rstd = sl["xin"], sl["sq"], sl["xg"], sl["rstd"]
        ps_a, ps_b = sl["pa"], sl["pb"]
        S.dma("sp", xin[:np_, :n], src_d[:np_, c0:c0 + n], w=[xin])
        S.op("act", lambda e: e.activation(out=sq[:np_, :n], in_=xin[:np_, :n], func=AF.Square), r=[xin], w=[sq])
        S.op("pe", lambda e: e.matmul(ps_a[:np_, :n], k["BO"][:np_, :np_], sq[:np_, :n], start=True, stop=True),
             r=[k["BO"], sq], w=[ps_a])
        S.op("act", lambda e: e.activation(out=rstd[:np_, :n], in_=ps_a[:np_, :n], func=AF.Ln,
                                           bias=self.c64[:np_, 0:1], scale=1.0), r=[ps_a, self.c64], w=[rstd])
        S.op("act", lambda e: e.activation(out=rstd[:np_, :n], in_=rstd[:np_, :n], func=AF.Exp, scale=-0.5),
             r=[rstd], w=[rstd])
        if not roped:
            S.op("dve", lambda e: e.scalar_tensor_tensor(out=dst[:np_, c0:c0 + n], in0=xin[:np_, :n], scalar=gain_ap,
                                                         in1=rstd[:np_, :n], op0=ALU.mult, op1=ALU.mult),
                 r=[xin, gain_tok, rstd], w=[dst])
            return
        S.op("dve", lambda e: e.scalar_tensor_tensor(out=xg[:np_, :n], in0=xin[:np_, :n], scalar=gain_ap,
                                                     in1=rstd[:np_, :n], op0=ALU.mult, op1=ALU.mult),
             r=[xin, gain_tok, rstd], w=[xg])
        self.rope_block(xg, dst, c0, n, np_, slot=sl)

    def load_cs(self, c0, n):
        S = self.S
        cs = self.cs[(c0 // 512) % 2]
        if getattr(cs, "c0", None) == c0 and getattr(cs, "gen", None) == self.gen:
            return cs
        for hf in range(2):
            S.dma("sp", cs[hf * 64:(hf + 1) * 64, 0, :n], self.d["cosT"][:, c0:c0 + n], w=[cs])
            S.dma("sp", cs[hf * 64:(hf + 1) * 64, 1, :n], self.d["sinT"][:, c0:c0 + n], w=[cs])
        cs.c0 = c0
        cs.gen = self.gen
        return cs

    def rope_block(self, xg, dst, c0, n, np_, scale=None, src_c0=None, slot=None):
        S, k = self.S, self.k
        ps_b = self.ps[7] if src_c0 is not None else (slot["pb"] if slot else self.ps[1])
        cs = self.load_cs(c0 if src_c0 is None else src_c0, n)
        t1, t2 = (slot["t1"], slot["t2"]) if slot else (self.tmp[1], self.tmp[2])
        xs = 0 if src_c0 is None else src_c0
        xgt = xg
        xg = xg.t[:, xs:xs + n]
        S.op("pe", lambda e: e.matmul(ps_b[:np_, :n], k["RP"][:np_, :np_], xg[:np_, :n], start=True, stop=True),
             r=[k["RP"], xgt], w=[ps_b])
        S.op("dve", lambda e: e.tensor_tensor(out=t1[:np_, :n], in0=xg[:np_, :n], in1=cs[:np_, 0, :n], op=ALU.mult),
             r=[xgt, cs], w=[t1])
        S.op("dve", lambda e: e.tensor_tensor(out=t2[:np_, :n], in0=ps_b[:np_, :n], in1=cs[:np_, 1, :n], op=ALU.mult),
             r=[ps_b, cs], w=[t2])
        if scale is None:
            S.op("pool", lambda e: e.tensor_tensor(out=dst[:np_, c0:c0 + n], in0=t1[:np_, :n], in1=t2[:np_, :n],
                                                   op=ALU.add), r=[t1, t2], w=[dst])
        else:
            S.op("pool", lambda e: e.tensor_tensor(out=t1[:np_, :n], in0=t1[:np_, :n], in1=t2[:np_, :n],
                                                   op=ALU.add), r=[t1, t2], w=[t1])
            S.op("act", lambda e: e.activation(out=dst[:np_, c0:c0 + n], in_=t1[:np_, :n], func=AF.Copy,
                                               scale=float(scale)), r=[t1], w=[dst])

    def attention(self, stage=9, nqb=None):
        S, k, d = self.S, self.k, self.d
        self.gen = "attn"
        qn, kn, vt = self.R[0], self.R[1], self.R[2]
        anp = S.sbuf("anp", [128, 4], F32)
        S.dma("sp", anp[:], d["anp"], w=[anp])
        S.op("dve", lambda e: e.tensor_scalar(out=anp[:, 1:2], in0=anp[:, 1:2], scalar1=8.0, scalar2=None,
                                              op0=ALU.mult), r=[anp], w=[anp])
        es = S.sbuf("esink", [128, 2], F32)
        S.op("act", lambda e: e.activation(out=es[:], in_=anp[:, 2:4], func=AF.Exp), r=[anp], w=[es])
        esink = S.sbuf("esink_t", [64, 256], F32)
        ones_f = S.sbuf("ones_f", [64, 128], F32)
        S.op("pool", lambda e: e.memset(ones_f[:], 1.0), w=[ones_f])
        for h in range(2):
            S.op("dve", lambda e: e.tensor_scalar(out=esink[:, h * 128:(h + 1) * 128], in0=ones_f[:],
                                                  scalar1=es[0:64, h:h + 1], scalar2=None, op0=ALU.mult),
                 r=[ones_f, es], w=[esink])
        vv = vt.t[:, 0:NKT * 64].rearrange("p (i d) -> p i d", d=64)
        S.dma("sp", vt[:, 0:NKT * 64], d["av"], w=[vt])
        blocks = [(i * 512, 512, True) for i in range(T // 512)] + [(T, L, False)]
        self.att_out_toks = [qn, kn, vt]
        if stage < 1:
            return
        for bi, (c0, n, roped) in enumerate(blocks):
            self.norm_rope_block(d["aqT"], qn, c0, n, anp[:, 0:1], anp, roped, 0)
            self.norm_rope_block(d["akT"], kn, c0, n, anp[:, 1:2], anp, roped, 1)
        PT = [S.sbuf("PT%d" % i, [128, 2, 5, 128], BF16) for i in range(2)]
        ost = [S.sbuf("oast%d" % i, [64, 2, 512], BF16) for i in range(2)]
        rec = S.sbuf("arec", [64, 256], F32)
        oview = d["oaT"].rearrange("(h d) t -> d h t", h=2)
        if stage < 2:
            return
        for qi, qb in enumerate(range(NQB + 2) if nqb is None else nqb):
            if qb < NQB:
                chunks = ([(qb - 1, 0)] if qb > 0 else []) + [(qb, None)] + ([(qb + 1, 1)] if qb < NQB - 1 else []) \
                    + [(NQB, None), (NQB + 1, None)]
            else:
                chunks = [(NQB, None), (NQB + 1, None)]
            A = self.ps[0:2] if qi % 2 == 0 else self.ps[2:4]
            Bk = self.ps[4:6]
            pv = self.ps[6 + qi % 2]
            pt = PT[qi % 2]
            qc = slice(qb * 128, (qb + 1) * 128)
            nch = len(chunks)
            for h in range(2):
                for bank, sl in ((A[h], range(0, min(nch, 4))), (Bk[h], range(4, nch))):
                    mms = []
                    for ci in sl:
                        kt, mi = chunks[ci]
                        o_ = bank[:, (ci % 4) * 128:(ci % 4 + 1) * 128]
                        mms.append((o_, kn[h * 64:(h + 1) * 64, kt * 128:(kt + 1) * 128],
                                    qn[h * 64:(h + 1) * 64, qc], [kn, qn]))
                        if mi is not None:
                            mms.append((o_, k["IDN"][:], k["MB"][:, mi, 0:128], [k["IDN"], k["MB"]]))
                    for i_, (o_, l_, r_, rd) in enumerate(mms):
                        S.op("pe", lambda e: e.matmul(o_, l_, r_, start=(i_ == 0), stop=(i_ == len(mms) - 1),
                                                      skip_group_check=True),
                             r=rd, w=[bank], signal=(i_ == len(mms) - 1))
            self.att_out_toks = list(self.ps)
            if stage < 3:
                continue
            for h in range(2):
                na = min(nch, 4)
                S.op("act", lambda e: e.activation(out=pt[:, h, 0:na, :],
                                                   in_=A[h][:, 0:na * 128].rearrange("p (c n) -> p c n", n=128),
                                                   func=AF.Exp), r=[A[h]], w=[pt])
                if nch > 4:
                    S.op("act", lambda e: e.activation(out=pt[:, h, 4, :], in_=Bk[h][:, 0:128], func=AF.Exp),
                         r=[Bk[h]], w=[pt])
            self.att_out_toks = list(self.ps) + PT
            if stage < 4:
                continue
            for ci, (kt, mi) in enumerate(chunks):
                S.op("pe", lambda e: e.matmul(pv[0:64, 0:256], vv[:, kt, :], pt[:, :, ci, :],
                                              start=(ci == 0), stop=(ci == nch - 1)),
                     r=[vt, pt], w=[pv], signal=False)
            for ci, (kt, mi) in enumerate(chunks):
                S.op("pe", lambda e: e.matmul(pv[0:64, 256:512], self.C.ones_bf[:, 0:64], pt[:, :, ci, :],
                                              start=(ci == 0), stop=(ci == nch - 1)),
                     r=[self.C.ones_bf, pt], w=[pv], signal=(ci == nch - 1))
            if stage < 5:
                continue
            st = ost[(qb // 4) % 2]
            so = (qb % 4) * 128
            S.op("dve", lambda e: e.tensor_tensor(out=rec[:], in0=pv[0:64, 256:512], in1=esink[:], op=ALU.add),
                 r=[pv, esink], w=[rec])
            S.op("dve", lambda e: e.reciprocal(out=rec[:], in_=rec[:]), r=[rec], w=[rec])
            S.op("dve", lambda e: e.tensor_tensor(out=st[:, :, so:so + 128],
                                                  in0=pv[0:64, 0:256].rearrange("p (h n) -> p h n", h=2),
                                                  in1=rec[:, :].rearrange("p (h n) -> p h n", h=2), op=ALU.mult),
                 r=[pv, rec], w=[st])
            if qb % 4 == 3 or qb == NQB + 1:
                g0 = (qb // 4) * 512
                wdt = so + 128
                if stage >= 6:
                    S.dma("sp", oview[:, :, g0:g0 + wdt], st[:, :, 0:wdt], r=[st])
            self.att_out_toks = ost + list(self.ps) + PT


    def scan_setup(self):
        S, d, nc = self.S, self.d, self.nc
        if hasattr(self, "scan_ready"):
            return
        self.scan_ready = True
        di = lambda n, sh, dt: nc.dram_tensor(n, list(sh), dt, kind="ExternalInput").ap()
        do = lambda n, sh, dt: nc.dram_tensor(n, list(sh), dt, kind="ExternalOutput").ap()
        d.update({
            "gqT": di("gqT", [64, TB], BF16), "gkT": di("gkT", [64, TB], BF16), "gaT": di("gaT", [32, TB], BF16),
            "gk": di("gk", [128, NKT * 64], BF16), "gv": di("gv", [128, NKT * 128], BF16),
            "wg": di("wg", [33, 128], F32),
            "rqT": di("rqT", [64, TB], BF16), "rkT": di("rkT", [64, TB], BF16), "rv": di("rv", [128, NKT * 128], BF16),
            "rld": di("rld", [128, 2], F32),
            "ogT": do("ogT", [128, TB], BF16), "orT": do("orT", [128, TB], BF16),
        })
        k = self.k
        k["SM"] = S.sbuf("k_SM", [128, 2, 512], BF16)
        S.dma("sp", k["SM"][:], d["SM"], w=[k["SM"]])
        k["TRI"] = S.sbuf("k_TRI", [128, 4, 128], BF16)
        S.dma("sp", k["TRI"][:], d["TRI"], w=[k["TRI"]])
        k["IDX"] = S.sbuf("k_IDX", [64, 2, 512], F32)
        S.dma("sp", k["IDX"][:], d["IDX"], w=[k["IDX"]])
        k["PIDX"] = S.sbuf("k_PIDX", [128, 2], F32)
        S.dma("sp", k["PIDX"][:], d["PIDX"], w=[k["PIDX"]])
        self.onef = S.sbuf("c_onef", [128, 1], F32)
        S.op("pool", lambda e: e.memset(self.onef[:], 1.0), w=[self.onef])
        self.acc = S.sbuf("sc_acc", [128, TB], F32)
        self.Sall = [S.sbuf("sc_Sall%d" % i, [64, 9, 128], F32) for i in range(3)]
        self.Sbf = [S.sbuf("sc_Sbf%d" % i, [64, 8, 128], BF16) for i in range(2)]
        self.qin = [S.sbuf("sc_qin%d" % i, [64, 512], BF16) for i in range(2)]
        self.kin = [S.sbuf("sc_kin%d" % i, [64, 512], BF16) for i in range(2)]
        self.kend = [S.sbuf("sc_kend%d" % i, [128, 4, 64], BF16) for i in range(2)]
        self.attm = [S.sbuf("sc_attm%d" % i, [128, 512], BF16) for i in range(2)]
        self.Eq = [S.sbuf("sc_Eq%d" % i, [64, 512], F32) for i in range(2)]
        self.Ek = S.sbuf("sc_Ek", [64, 512], F32)
        self.Eend = S.sbuf("sc_Eend", [128, 4, 64], F32)
        self.sp = S.sbuf("sc_sp", [128, 4, 128], F32)
        self.gaf = self.tmp[2]
        self.gi = 0

    def scan_groups(self, z):
        lat = [(g * 512, 512) for g in range(T // 512)]
        return [(T, L)] + (lat if z == 0 else lat[::-1])

    def chunk_front(self, z, c0, n, qin, kin, kend, kend_tok, vtm, dec_fn, dec_toks, outb):
        S, k = self.S, self.k
        A = self.ps[0]
        Bk, Ck = self.ds_banks[self.gi % 2]
        nt = n // 128
        nchunk = 2 * nt
        t0 = c0 // 128
        gi = self.gi
        self.gi += 1
        attm = self.attm[gi % 2]
        Sall, Snext = self.Sall[gi % 3], self.Sall[(gi + 1) % 3]
        vv = vtm.t[:, 0:NKT * 128].rearrange("p (i d) -> p i d", d=128)
        for p in range(nt):
            S.op("pe", lambda e: e.matmul(A[:, p * 128:(p + 1) * 128], kin[:, p * 128:(p + 1) * 128],
                                          qin[:, p * 128:(p + 1) * 128], start=(p == 0), stop=(p == nt - 1),
                                          skip_group_check=True), r=[kin, qin], w=[A], signal=(p == nt - 1))
        S.op("dve", lambda e: e.tensor_tensor(out=attm[:, :n], in0=A[:, :n], in1=k["SM"][:, z, :n], op=ALU.mult),
             r=[A, k["SM"]], w=[attm])
        for cc in range(nchunk):
            p, hf = cc // 2, cc % 2
            bank = Bk if hf == 0 else Ck
            S.op("pe", lambda e: e.matmul(bank[0:64, p * 128:(p + 1) * 128], kend[hf * 64:(hf + 1) * 64, p, :],
                                          vv[hf * 64:(hf + 1) * 64, t0 + p, :], start=(p == 0), stop=(p == nt - 1),
                                          skip_group_check=True), r=[kend_tok, vtm], w=[bank], signal=(p == nt - 1))
        order = list(range(nchunk)) if z == 0 else list(range(nchunk - 1, -1, -1))
        pos = {}
        for i, cc in enumerate(order):
            pos[cc] = i
            p, hf = cc // 2, cc % 2
            bank = Bk if hf == 0 else Ck
            last = (i == nchunk - 1)
            o_ = Snext[:, 0, :] if last else Sall[:, i + 1, :]
            S.op("dve", lambda e: e.scalar_tensor_tensor(out=o_, in0=Sall[:, i, :], scalar=dec_fn(cc),
                                                         in1=bank[0:64, p * 128:(p + 1) * 128],
                                                         op0=ALU.mult, op1=ALU.add),
                 r=[Sall, bank] + dec_toks, w=[Snext if last else Sall])
        return dict(z=z, c0=c0, n=n, nt=nt, nchunk=nchunk, t0=t0, gi=gi, attm=attm, Sall=Sall, qin=qin, vtm=vtm,
                    vv=vv, pos=pos, outb=outb)

    def chunk_back(self, c):
        if c is None:
            return
        S = self.S
        Dk = self.ps[3]
        z, c0, n, nt, nchunk, t0, gi = c["z"], c["c0"], c["n"], c["nt"], c["nchunk"], c["t0"], c["gi"]
        attm, Sall, qin, vtm, vv, pos, outb = c["attm"], c["Sall"], c["qin"], c["vtm"], c["vv"], c["pos"], c["outb"]
        Sbf = self.Sbf[gi % 2]
        S.op("act", lambda e: e.activation(out=Sbf[:, 0:nchunk, :], in_=Sall[:, 0:nchunk, :], func=AF.Copy),
             r=[Sall], w=[Sbf])
        nmm = nt * 3
        mi = 0
        for p in range(nt):
            S.op("pe", lambda e: e.matmul(Dk[:, p * 128:(p + 1) * 128], vv[:, t0 + p, :], attm[:, p * 128:(p + 1) * 128],
                                          start=(mi == 0), stop=False, skip_group_check=True),
                 r=[vtm, attm], w=[Dk], signal=False)
            mi += 1
            for hf in range(2):
                cc = 2 * p + hf
                S.op("pe", lambda e: e.matmul(Dk[:, cc * 64:(cc + 1) * 64], Sbf[:, pos[cc], :],
                                              qin[:, cc * 64:(cc + 1) * 64], start=False, stop=(mi == nmm - 1),
                                              skip_group_check=True),
                     r=[Sbf, qin], w=[Dk], signal=(mi == nmm - 1))
                mi += 1
        if z == 0:
            S.op("act", lambda e: e.activation(out=self.acc[:, c0:c0 + n], in_=Dk[:, :n], func=AF.Copy),
                 r=[Dk], w=[self.acc])
        else:
            S.op("dve", lambda e: e.tensor_tensor(out=outb[:, c0:c0 + n], in0=Dk[:, :n], in1=self.acc[:, c0:c0 + n],
                                                  op=ALU.add), r=[Dk, self.acc], w=[outb])

    def scan_init_state(self):
        S = self.S
        Sall = self.Sall[self.gi % 3]
        S.op("pool", lambda e: e.memset(Sall[:, 0, :], 0.0), w=[Sall])

    def gla(self):
        S, k, d = self.S, self.k, self.d
        self.scan_setup()
        self.gen = "gla"
        S.barrier()
        qT, kT, vtm, ktm, gaT, outb = self.R[0], self.R[1], self.R[2], self.R[3], self.R[4], self.R[5]
        S.dma("sp", qT[0:64, :], d["gqT"], w=[qT])
        S.dma("sp", kT[0:64, :], d["gkT"], w=[kT])
        S.dma("sp", gaT[0:32, :], d["gaT"], w=[gaT])
        S.dma("sp", vtm[:, 0:NKT * 128], d["gv"], w=[vtm])
        S.dma("sp", ktm[:, 0:NKT * 64], d["gk"], w=[ktm])
        kk = ktm.t[:, 0:NKT * 64].rearrange("p (i d) -> p i d", d=64)
        S.op("pool", lambda e: e.memset(gaT[32:33, :], 1.0), w=[gaT])
        wgb = S.sbuf("gla_wgb", [33, 128], BF16)
        S.dma("pool", wgb[:], d["wg"], w=[wgb])
        trib = k["TRI"]
        spb = self.tbf[2].t[:, :].rearrange("p (i c) -> p i c", c=128)
        spt = self.tbf[2]
        E, Fk, Gk = self.ps[4], self.ps[5], self.ps[4]
        self.ds_banks = [(self.ps[1], self.ps[2]), (self.ps[6], self.ps[7])]
        ex = self.tmp[3]
        pend = None
        for z in range(2):
            self.scan_init_state()
            for (c0, n) in self.scan_groups(z):
                nt = n // 128
                t0 = c0 // 128
                gi = self.gi
                qin, kin, kend, Eq = self.qin[gi % 2], self.kin[gi % 2], self.kend[gi % 2], self.Eq[gi % 2]
                for p in range(nt):
                    S.op("pe", lambda e: e.matmul(E[:, p * 128:(p + 1) * 128], gaT[0:33, c0 + p * 128:c0 + (p + 1) * 128],
                                                  wgb[:, :], start=(p == 0), stop=(p == nt - 1), skip_group_check=True),
                         r=[gaT, wgb], w=[E], signal=(p == nt - 1))
                S.op("act", lambda e: e.activation(out=ex[:, :n], in_=E[:, :n], func=AF.Exp, scale=-1.0),
                     r=[E], w=[ex])
                S.op("act", lambda e: e.activation(out=spb[:, 0:nt, :], in_=ex[:, :n].rearrange("p (i c) -> p i c", c=128),
                                                   func=AF.Ln, bias=self.onef[:, 0:1], scale=1.0),
                     r=[ex, self.onef], w=[spt])
                for p in range(nt):
                    S.op("pe", lambda e: e.matmul(Fk[0:64, p * 128:(p + 1) * 128], spb[:, p, z * 64:(z + 1) * 64],
                                                  trib[:, z, :], start=(p == 0), stop=(p == nt - 1),
                                                  skip_group_check=True),
                         r=[spt, trib], w=[Fk], signal=(p == nt - 1))
                S.op("act", lambda e: e.activation(out=Eq[:, :n], in_=Fk[0:64, :n], func=AF.Exp), r=[Fk], w=[Eq])
                S.op("act", lambda e: e.activation(out=self.Ek[:, :n], in_=Fk[0:64, :n], func=AF.Exp, scale=-1.0),
                     r=[Fk], w=[self.Ek])
                S.op("dve", lambda e: e.scalar_tensor_tensor(out=qin[:, :n], in0=qT[0:64, c0:c0 + n], scalar=0.125,
                                                             in1=Eq[:, :n], op0=ALU.mult, op1=ALU.mult),
                     r=[qT, Eq], w=[qin])
                S.op("dve", lambda e: e.tensor_tensor(out=kin[:, :n], in0=kT[0:64, c0:c0 + n], in1=self.Ek[:, :n],
                                                      op=ALU.mult), r=[kT, self.Ek], w=[kin])
                for p in range(nt):
                    S.op("pe", lambda e: e.matmul(Gk[:, p * 64:(p + 1) * 64], trib[:, 2 + z, :],
                                                  spb[:, p, z * 64:(z + 1) * 64], start=(p == 0), stop=(p == nt - 1),
                                                  skip_group_check=True),
                         r=[spt, trib], w=[Gk], signal=(p == nt - 1))
                S.op("act", lambda e: e.activation(out=self.Eend[:, 0:nt, :],
                                                   in_=Gk[:, 0:nt * 64].rearrange("p (i c) -> p i c", c=64),
                                                   func=AF.Exp), r=[Gk], w=[self.Eend])
                S.op("dve", lambda e: e.tensor_tensor(out=kend[:, 0:nt, :], in0=kk[:, t0:t0 + nt, :],
                                                      in1=self.Eend[:, 0:nt, :], op=ALU.mult),
                     r=[ktm, self.Eend], w=[kend])
                col = (lambda cc: cc * 64 + 63) if z == 0 else (lambda cc: cc * 64)
                cur = self.chunk_front(z, c0, n, qin, kin, kend, kend, vtm,
                                       lambda cc, Eq=Eq, col=col: Eq[:, col(cc):col(cc) + 1], [Eq], outb)
                self.chunk_back(pend)
                pend = cur
        self.chunk_back(pend)
        S.dma("sp", d["ogT"], outb[:, :], r=[outb])
        self.gla_out_toks = [outb]

    def ret(self):
        S, k, d = self.S, self.k, self.d
        self.scan_setup()
        self.gen = "ret"
        qT, kT, vtm, outb = self.R[0], self.R[1], self.R[2], self.R[5]
        S.dma("sp", qT[0:64, :], d["rqT"], w=[qT])
        S.dma("sp", kT[0:64, :], d["rkT"], w=[kT])
        S.dma("sp", vtm[:, 0:NKT * 128], d["rv"], w=[vtm])
        rld = S.sbuf("ret_rld", [128, 2], F32)
        S.dma("sp", rld[:], d["rld"], w=[rld])
        nlg = S.sbuf("ret_nlg", [128, 2], F32)
        lg = S.sbuf("ret_lg", [128, 2], F32)
        S.op("act", lambda e: e.activation(out=nlg[:], in_=rld[:], func=AF.Exp), r=[rld], w=[nlg])
        S.op("dve", lambda e: e.tensor_scalar(out=lg[:], in0=nlg[:], scalar1=-1.0, scalar2=None, op0=ALU.mult),
             r=[nlg], w=[lg])
        EqR = self.Eq
        spv = Tile(self.sp.t[0:64, :, :].rearrange("p a b -> p (a b)"), "spv")
        spv.b = self.sp.b
        EkR = [self.Ek, spv]
        EendR = S.sbuf("ret_EendR", [128, 2], F32)
        decR = S.sbuf("ret_decR", [64, 2], F32)
        for z in range(2):
            S.op("act", lambda e: e.activation(out=EqR[z][:], in_=k["IDX"][:, z, :], func=AF.Exp,
                                               scale=lg[0:64, z:z + 1]), r=[k["IDX"], lg], w=[EqR[z]])
            S.op("act", lambda e: e.activation(out=EkR[z][:], in_=k["IDX"][:, z, :], func=AF.Exp,
                                               scale=nlg[0:64, z:z + 1]), r=[k["IDX"], nlg], w=[EkR[z]])
            S.op("act", lambda e: e.activation(out=EendR[:, z:z + 1], in_=k["PIDX"][:, z:z + 1], func=AF.Exp,
                                               scale=lg[:, z:z + 1]), r=[k["PIDX"], lg], w=[EendR])
        S.op("act", lambda e: e.activation(out=decR[:], in_=lg[0:64, :], func=AF.Exp, scale=64.0), r=[lg], w=[decR])
        Hk = self.ps[7]
        self.ds_banks = [(self.ps[1], self.ps[2]), (self.ps[4], self.ps[5])]
        rq_r, rk_r = self.tbf[0], self.tbf[1]
        pend = None
        for z in range(2):
            self.scan_init_state()
            for (c0, n) in self.scan_groups(z):
                nt = n // 128
                gi = self.gi
                qin, kin, kend = self.qin[gi % 2], self.kin[gi % 2], self.kend[gi % 2]
                if z == 0:
                    if c0 < T:
                        self.rope_block(qT, qT, c0, n, 64, src_c0=c0)
                        self.rope_block(kT, kT, c0, n, 64, scale=0.125, src_c0=c0)
                    else:
                        S.op("act", lambda e: e.activation(out=kT[0:64, c0:c0 + n], in_=kT[0:64, c0:c0 + n],
                                                           func=AF.Copy, scale=0.125), r=[kT], w=[kT])
                S.op("dve", lambda e: e.tensor_tensor(out=qin[:, :n], in0=qT[0:64, c0:c0 + n], in1=EqR[z][:, :n],
                                                      op=ALU.mult), r=[qT, EqR[z]], w=[qin])
                S.op("dve", lambda e: e.tensor_tensor(out=kin[:, :n], in0=kT[0:64, c0:c0 + n], in1=EkR[z][:, :n],
                                                      op=ALU.mult), r=[kT, EkR[z]], w=[kin])
                for p in range(nt):
                    S.op("pe", lambda e: e.matmul(Hk[:, p * 64:(p + 1) * 64], kT[0:64, c0 + p * 128:c0 + (p + 1) * 128],
                                                  k["IDN"][0:64, 0:64], start=(p == 0), stop=(p == nt - 1),
                                                  skip_group_check=True),
                         r=[kT, k["IDN"]], w=[Hk], signal=(p == nt - 1))
                S.op("dve", lambda e: e.tensor_scalar(out=kend[:, 0:nt, :],
                                                      in0=Hk[:, 0:nt * 64].rearrange("p (i c) -> p i c", c=64),
                                                      scalar1=EendR[:, z:z + 1], scalar2=None, op0=ALU.mult),
                     r=[Hk, EendR], w=[kend])
                cur = self.chunk_front(z, c0, n, qin, kin, kend, kend, vtm, lambda cc, z=z: decR[:, z:z + 1],
                                       [decR], outb)
                self.chunk_back(pend)
                pend = cur
        self.chunk_back(pend)
        S.dma("sp", d["orT"], outb[:, :], r=[outb])
        self.ret_out_toks = [outb]


def build_p2(which=("attn", "gla", "ret"), stage=9, nqb=None):
    P = P2()
    outs = []
    if "attn" in which:
        P.attention(stage, nqb)
        outs += P.att_out_toks
    if "gla" in which:
        P.gla()
        outs += P.gla_out_toks
    if "ret" in which:
        P.ret()
        outs += P.ret_out_toks
    P.S.finish(outs, "sp")
    return P.nc, P.S


def tm_layout(a):
    n = a.shape[1]
    return np.ascontiguousarray(a.reshape(-1, 128, n).transpose(1, 0, 2).reshape(128, -1))


_PROG = {}


def _prog(name):
    if name not in _PROG:
        if name == "p1":
            _PROG[name] = build_p1()[0]
        elif name == "p2":
            _PROG[name] = build_p2()[0]
        else:
            _PROG[name] = build_p3()[0]
    return _PROG[name]


def _run(name, maps):
    res = run_bass_kernel_spmd(_prog(name), maps, core_ids=list(range(NCORE)))
    return res.results


def p2_inmaps(inp, l, zfm_b, ztm_b, consts, cos, sin):
    maps = []
    gw, gb_ = inp["gla_gate_w"][l], inp["gla_gate_b"][l]
    for c in range(NCORE):
        b, hh = c // 4, c % 4
        kvh = hh // 2
        zf, zt = zfm_b[b], ztm_b[b]

        def fm(name, o, n):
            return np.ascontiguousarray(zf[FMO[name] + o:FMO[name] + o + n])

        def tm(name, o, n):
            return tm_layout(zt[:, TMO[name] + o:TMO[name] + o + n])
        ak = fm("ak", kvh * 64, 64)
        anp = np.zeros((128, 4), np.float32)
        anp[:, 0] = np.tile(inp["attn_q_norm_g"][l], 2)
        anp[:, 1] = np.tile(inp["attn_k_norm_g"][l], 2)
        anp[:, 2] = inp["attn_sink"][l][2 * hh]
        anp[:, 3] = inp["attn_sink"][l][2 * hh + 1]
        wg = np.zeros((33, 128), np.float32)
        wg[0:16, 0:64] = gw[0][:, hh * 64:(hh + 1) * 64]
        wg[16:32, 64:128] = gw[1][:, hh * 64:(hh + 1) * 64]
        wg[32, 0:64] = gb_[0, hh * 64:(hh + 1) * 64]
        wg[32, 64:128] = gb_[1, hh * 64:(hh + 1) * 64]
        m = {
            "aqT": fm("aq", 2 * hh * 64, 128), "akT": np.ascontiguousarray(np.concatenate([ak, ak], 0)),
            "av": tm("av", kvh * 64, 64), "cosT": cos, "sinT": sin, "anp": anp,
            "gqT": fm("gq", hh * 64, 64), "gkT": fm("gk", hh * 64, 64), "gaT": fm("ga", 0, 32),
            "gk": tm("gk", hh * 64, 64), "gv": tm("gv", hh * 128, 128), "wg": wg,
            "rqT": fm("rq", hh * 64, 64), "rkT": fm("rk", hh * 64, 64), "rv": tm("rv", hh * 128, 128),
            "rld": np.ascontiguousarray(np.broadcast_to(inp["ret_log_decay"][l][:, hh], (128, 2))).astype(np.float32),
        }
        m.update(consts)
        maps.append(m)
    return maps


def kernel(**inp):
    inp = {k: np.asarray(v) for k, v in inp.items()}
    x = inp["x"].astype(np.float32)
    ctx = inp["ctx"].astype(np.float32)
    consts = p2_consts_np()
    cos, sin = rope_tables_np()
    for l in range(DEPTH):
        r1 = _run("p1", p1_inmaps(inp, l, x, ctx))
        zfm_b, ztm_b = [], []
        for b in range(B):
            zf = [np.asarray(r1[b * 4 + s]["zfm"]) for s in range(4)]
            zt = [np.asarray(r1[b * 4 + s]["ztm"]) for s in range(4)]
            zfm_b.append(np.concatenate([z[:, :TS] for z in zf] + [z[:, TS:] for z in zf], axis=1))
            ztm_b.append(np.concatenate([z[:TS] for z in zt] + [z[TS:] for z in zt], axis=0))
        modT_list = [np.asarray(r1[c]["modT"]) for c in range(NCORE)]
        del r1
        r2 = _run("p2", p2_inmaps(inp, l, zfm_b, ztm_b, consts, cos, sin))
        o_b = []
        for b in range(B):
            oa = np.concatenate([np.asarray(r2[b * 4 + hh]["oaT"]) for hh in range(4)], axis=0)
            og = np.concatenate([np.asarray(r2[b * 4 + hh]["ogT"]) for hh in range(4)], axis=0)
            orr = np.concatenate([np.asarray(r2[b * 4 + hh]["orT"]) for hh in range(4)], axis=0)
            o_b.append(np.ascontiguousarray(np.concatenate([oa, og, orr], axis=0).T))
        del r2
        o_all = np.stack(o_b, 0)
        g_all = np.stack([np.ascontiguousarray(
            np.concatenate([zfm_b[b][FMO["gr"]:FMO["gr"] + 512], zfm_b[b][FMO["rg"]:FMO["rg"] + 512],
                            zfm_b[b][FMO["mg"]:FMO["mg"] + 3072]], axis=0).T) for b in range(B)], 0)
        del zfm_b, ztm_b
        r3 = _run("p3", p3_inmaps(inp, l, x, ctx, modT_list, o_all[:, :T], o_all[:, T:], g_all[:, :T], g_all[:, T:]))
        xn = np.empty_like(x)
        cn = np.empty_like(ctx)
        for c in range(NCORE):
            b, seg = c // 4, c % 4
            xo = np.asarray(r3[c]["xo"])
            xn[b, seg * TS:(seg + 1) * TS] = xo[:, :TS].T
            cn[b, seg * LS:(seg + 1) * LS] = xo[:, TS:].T
        x, ctx = xn, cn
    return x.astype(np.float32)
```

```python
import numpy as np
import ml_dtypes
import concourse.bass as bass
import concourse.mybir as mybir
from concourse.bass_utils import run_bass_kernel_spmd

F32 = mybir.dt.float32
BF16 = mybir.dt.bfloat16
AF = mybir.ActivationFunctionType
ALU = mybir.AluOpType
AX = mybir.AxisListType

D = 1024
B = 2
T = 8192
L = 256
DEPTH = 4
NCORE = 8
TS = T // 4
LS = L // 4
NT = TS + LS
TB = T + L
D_FF = 2816
D_IN = 6944
EPS = 1e-6
GRID_W = 64

OFF = {}
_o = 0
for _n, _s in (("aq", 512), ("ak", 128), ("av", 128), ("gq", 256), ("gk", 256), ("gv", 512), ("gr", 512),
               ("ga", 32), ("rq", 256), ("rk", 256), ("rv", 512), ("rg", 512), ("mg", 3072)):
    OFF[_n] = (_o, _s)
    _o += _s
assert _o == D_IN


class Buf:
    __slots__ = ("name", "w", "r")

    def __init__(self, name):
        self.name = name
        self.w = None
        self.r = {}


class Tile:
    def __init__(self, t, name):
        self.t = t
        self.b = Buf(name)

    def __getitem__(self, idx):
        return self.t[idx]


class Sched:
    def __init__(self, nc):
        self.nc = nc
        self.eng = {"pe": nc.tensor, "act": nc.scalar, "dve": nc.vector, "pool": nc.gpsimd, "sp": nc.sync}
        self.sem = {}
        self.cnt = {}
        self.seen = {k: {} for k in self.eng}
        self.nwait = 0
        self.nins = 0
        for k in ("pe", "act", "dve", "pool"):
            self.sem[k] = nc.alloc_semaphore("s_" + k)
            self.cnt[k] = 0
        self._uid = 0

    def sbuf(self, name, shape, dtype):
        return Tile(self.nc.alloc_sbuf_tensor("sb_" + name, list(shape), dtype), name)

    def psum(self, name, shape, dtype=F32):
        return Tile(self.nc.alloc_psum_tensor("pp_" + name, list(shape), dtype), name)

    @staticmethod
    def _b(x):
        return x.b if isinstance(x, Tile) else x

    def _deps(self, ek, reads, writes):
        deps = {}

        def add(m):
            if m is None:
                return
            k, v = m
            if deps.get(k, 0) < v:
                deps[k] = v
        for b in reads:
            add(self._b(b).w)
        for b in writes:
            b = self._b(b)
            add(b.w)
            for k, v in b.r.items():
                add((k, v))
        if ek == "pe":
            deps.pop("pe", None)
        e = self.eng[ek]
        seen = self.seen[ek]
        for k, v in deps.items():
            if seen.get(k, 0) < v:
                e.wait_ge(self.sem[k], v)
                seen[k] = v
                self.nwait += 1

    def _mark(self, mark, reads, writes):
        k, v = mark
        for b in writes:
            b = self._b(b)
            b.w = mark
            b.r = {}
        for b in reads:
            b = self._b(b)
            if b.r.get(k, 0) < v:
                b.r[k] = v

    def op(self, ek, fn, r=(), w=(), signal=True):
        self._deps(ek, r, w)
        ins = fn(self.eng[ek])
        self.nins += 1
        if signal:
            self.cnt[ek] += 1
            ins.then_inc(self.sem[ek], 1)
            mark = (ek, self.cnt[ek])
        else:
            assert ek == "pe"
            mark = (ek, self.cnt[ek] + 1)
        self._mark(mark, r, w)
        return ins

    def dma(self, qk, out, in_, r=(), w=(), key=None):
        self._deps(qk, r, w)
        tok = self._b(w[0]) if len(w) else self._b(r[0])
        k = ("dma", key or tok.name)
        if k not in self.sem:
            self._uid += 1
            self.sem[k] = self.nc.alloc_semaphore("sd%d" % self._uid)
            self.cnt[k] = 0
        self.cnt[k] += 16
        self.eng[qk].dma_start(out=out, in_=in_).then_inc(self.sem[k], 16)
        self.nins += 1
        self._mark((k, self.cnt[k]), r, w)

    def barrier(self):
        for ek, e in self.eng.items():
            for k, v in self.cnt.items():
                if v > 0 and self.seen[ek].get(k, 0) < v:
                    e.wait_ge(self.sem[k], v)
                    self.seen[ek][k] = v

    def finish(self, toks, ek="sp"):
        self._deps(ek, [], toks)
        e = self.eng[ek]
        for k, v in self.cnt.items():
            if v > 0 and self.seen[ek].get(k, 0) < v:
                e.wait_ge(self.sem[k], v)
                self.seen[ek][k] = v


def token_blocks():
    blks = [(i * 512, 512, 0) for i in range(TS // 512)]
    blks.append((TS, LS, 1))
    return blks


def token_tiles():
    tl = [(i * 128, 128) for i in range(TS // 128)]
    tl.append((TS, LS))
    return tl


FM_GROUPS = [
    (0, 512), (512, 128), (768, 512), (1792, 512), (2304, 32), (2336, 512), (3360, 512),
    (3872, 512), (4384, 512), (4896, 512), (5408, 512), (5920, 512), (6432, 512)]
FM_ROWS = sum(n for _, n in FM_GROUPS)
FMO = {"aq": 0, "ak": 512, "gq": 640, "gk": 896, "gr": 1152, "ga": 1664, "rq": 1696, "rk": 1952,
       "rg": 2208, "mg": 2720}
TM_GROUPS = [(640, 128), (1280, 512), (2848, 512), (1024, 256), (2592, 256)]
TM_COLS = sum(n for _, n in TM_GROUPS)
TMO = {"av": 0, "gv": 128, "rv": 640, "gk": 1152, "rk": 1408}


class Consts:
    def __init__(self, S):
        self.ones_bf = S.sbuf("ones_bf", [128, 128], BF16)
        S.op("pool", lambda e: e.memset(self.ones_bf[:], 1.0), w=[self.ones_bf])
        self.deps = S.sbuf("c_deps", [128, 1], F32)
        S.op("pool", lambda e: e.memset(self.deps[:], float(D * EPS)), w=[self.deps])


def emit_mod(S, cs_d, ada_w_d, ada_b_d, modT, ps):
    nc = S.nc
    cs = S.sbuf("cs", [128, 8, 2], F32)
    sg = S.sbuf("cs_sg", [128, 8, 2], F32)
    ab = S.sbuf("ada_b", [128, 48], F32)
    S.dma("sp", cs[:], cs_d, w=[cs])
    S.dma("sp", ab[:], ada_b_d, w=[ab])
    S.op("act", lambda e: e.activation(out=sg[:], in_=cs[:], func=AF.Sigmoid), r=[cs], w=[sg])
    csb = S.sbuf("cs_bf", [128, 8, 2], BF16)
    S.op("dve", lambda e: e.tensor_tensor(out=csb[:], in0=cs[:], in1=sg[:], op=ALU.mult), r=[sg, cs], w=[csb])
    GW = 768
    wb = [S.sbuf("adaw%d" % i, [128, 8, GW], BF16) for i in range(2)]
    for g in range(6144 // GW):
        wt = wb[g % 2]
        S.dma("pool", wt[:], ada_w_d[:, g * GW:(g + 1) * GW].rearrange("(c p) n -> p c n", p=128), w=[wt])
        for fc in range(GW // 128):
            ch = g * (GW // 128) + fc
            for kc in range(8):
                S.op("pe", lambda e: e.matmul(ps[:, ch * 2:ch * 2 + 2], wt[:, kc, fc * 128:(fc + 1) * 128],
                                              csb[:, kc, :], start=(kc == 0), stop=(kc == 7)),
                     r=[wt, csb], w=[ps], signal=(kc == 7))
    for j in range(2):
        S.op("dve", lambda e: e.tensor_tensor(out=modT[:, :, j], in0=ps[:, j:96:2], in1=ab[:], op=ALU.add),
             r=[ps, ab], w=[modT])


def emit_norm_mod(S, C, xT, hT, htoks, gmod, modT, part_sh, ps_ss, name, blocks=None, ps2=None, bufs=None):
    blocks = blocks if blocks is not None else token_blocks()
    if bufs is None:
        sqs = [S.sbuf(name + "_sq%d" % i, [128, 8, 512], BF16) for i in range(1)]
        rstds = [S.sbuf(name + "_rstd%d" % i, [128, 512], F32) for i in range(2)]
        tmp = [S.sbuf(name + "_tmp%d" % i, [128, 512], F32) for i in range(2)]
    else:
        sqs, rstds, tmp = bufs
    pss = [ps_ss, ps2 if ps2 is not None else ps_ss]

    def stats(bi):
        t0, nb, j = blocks[bi]
        sq, rstd, ps = sqs[0], rstds[bi % 2], pss[bi % 2]
        S.op("act", lambda e: e.activation(out=sq[:, :, :nb], in_=xT[:, :, t0:t0 + nb], func=AF.Square),
             r=[xT], w=[sq])
        for c in range(8):
            S.op("pe", lambda e: e.matmul(ps[:, :nb], C.ones_bf[:], sq[:, c, :nb], start=(c == 0), stop=(c == 7)),
                 r=[C.ones_bf, sq], w=[ps], signal=(c == 7))
        S.op("act", lambda e: e.activation(out=rstd[:, :nb], in_=ps[:, :nb], func=AF.Ln,
                                           bias=C.deps[:, 0:1], scale=1.0), r=[ps, C.deps], w=[rstd])
        S.op("act", lambda e: e.activation(out=rstd[:, :nb], in_=rstd[:, :nb], func=AF.Exp, scale=-0.5),
             r=[rstd], w=[rstd])

    def apply(bi):
        t0, nb, j = blocks[bi]
        rstd = rstds[bi % 2]
        for c in range(8):
            tt = tmp[c % 2]
            S.op("dve", lambda e: e.scalar_tensor_tensor(out=tt[:, :nb], in0=xT[:, c, t0:t0 + nb],
                                                         scalar=gmod[:, c, j:j + 1], in1=rstd[:, :nb],
                                                         op0=ALU.mult, op1=ALU.mult),
                 r=[xT, gmod, rstd], w=[tt])
            S.op("act", lambda e: e.activation(out=hT[:, c, t0:t0 + nb], in_=tt[:, :nb], func=AF.Identity,
                                               bias=modT[:, part_sh * 8 + c, j:j + 1], scale=1.0),
                 r=[tt, modT], w=[htoks[bi] if isinstance(htoks, list) else htoks])
    stats(0)
    for bi in range(len(blocks)):
        if bi + 1 < len(blocks):
            stats(bi + 1)
        apply(bi)


def emit_gmod(S, gmod, g_d, modT, part_sc, name):
    g = S.sbuf(name + "_g", [128, 8], F32)
    S.dma("sp", g[:], g_d, w=[g])
    for j in range(2):
        S.op("dve", lambda e: e.tensor_scalar(out=gmod[:, :, j], in0=modT[:, part_sc * 8:(part_sc + 1) * 8, j],
                                              scalar1=1.0, scalar2=float(np.sqrt(D)), op0=ALU.add, op1=ALU.mult),
             r=[modT], w=[gmod])
        S.op("dve", lambda e: e.tensor_tensor(out=gmod[:, :, j], in0=gmod[:, :, j], in1=g[:], op=ALU.mult),
             r=[gmod, g], w=[gmod])


def build_p1():
    nc = bass.Bass("TRN2", target_bir_lowering=False)
    S = Sched(nc)
    xT_d = nc.dram_tensor("xT", [D, NT], F32, kind="ExternalInput").ap()
    cs_d = nc.dram_tensor("cs", [128, 8, 2], F32, kind="ExternalInput").ap()
    ada_w_d = nc.dram_tensor("ada_w", [D, 6 * D], F32, kind="ExternalInput").ap()
    ada_b_d = nc.dram_tensor("ada_b", [128, 48], F32, kind="ExternalInput").ap()
    g1_d = nc.dram_tensor("norm1_g", [128, 8], F32, kind="ExternalInput").ap()
    w_in_d = nc.dram_tensor("w_in", [D, D_IN], F32, kind="ExternalInput").ap()
    zfm_d = nc.dram_tensor("zfm", [FM_ROWS, NT], BF16, kind="ExternalOutput").ap()
    ztm_d = nc.dram_tensor("ztm", [NT, TM_COLS], BF16, kind="ExternalOutput").ap()
    mod_d = nc.dram_tensor("modT", [128, 96], F32, kind="ExternalOutput").ap()

    C = Consts(S)
    xT = S.sbuf("xT", [128, 8, NT], F32)
    hT = S.sbuf("hT", [128, 8, NT], BF16)
    htoks = [Buf("hT%d" % i) for i in range(len(token_blocks()))]
    modT = S.sbuf("modT", [128, 48, 2], F32)
    gmod = S.sbuf("gmod1", [128, 8, 2], F32)
    ps_mod = S.psum("ps_mod", [128, 512])
    ps_ss = S.psum("ps_ss", [128, 512])
    ps_o = [S.psum("ps_o%d" % i, [128, 512]) for i in range(4)]

    S.dma("sp", xT[:], xT_d.rearrange("(c p) t -> p c t", p=128), w=[xT])
    emit_mod(S, cs_d, ada_w_d, ada_b_d, modT, ps_mod)
    S.dma("sp", mod_d.rearrange("p (c j) -> p c j", j=2), modT[:], r=[modT])
    emit_gmod(S, gmod, g1_d, modT, 1, "n1")
    emit_norm_mod(S, C, xT, hT, htoks, gmod, modT, 0, ps_ss, "n1", ps2=ps_mod)

    wbufs = [S.sbuf("w%d" % i, [128, 8, 512], BF16) for i in range(3)]
    stg_fm = [S.sbuf("stgfm%d" % i, [128, NT], BF16) for i in range(3)]
    stg_tm = [S.sbuf("stgtm%d" % i, [128, 17, 512], BF16) for i in range(1)]
    gi = 0
    pi = 0
    ei = 0
    si = 0
    row0 = 0
    blks = token_blocks()
    for (c0, n) in FM_GROUPS:
        wt = wbufs[gi % 3]
        gi += 1
        S.dma("pool", wt[:, :, :n], w_in_d[:, c0:c0 + n].rearrange("(c p) n -> p c n", p=128), w=[wt])
        for f0 in range(0, n, 128):
            m = min(128, n - f0)
            stg = stg_fm[si % 3]
            si += 1
            for bi, (t0, nb, j) in enumerate(blks):
                ps = ps_o[pi % 4]
                pi += 1
                for kc in range(8):
                    S.op("pe", lambda e: e.matmul(ps[:m, :nb], wt[:, kc, f0:f0 + m], hT[:, kc, t0:t0 + nb],
                                                  start=(kc == 0), stop=(kc == 7)),
                         r=[wt, htoks[bi]], w=[ps], signal=(kc == 7))
                if ei % 2 == 0:
                    S.op("act", lambda e: e.activation(out=stg[:m, t0:t0 + nb], in_=ps[:m, :nb], func=AF.Copy),
                         r=[ps], w=[stg])
                else:
                    S.op("dve", lambda e: e.tensor_copy(out=stg[:m, t0:t0 + nb], in_=ps[:m, :nb]), r=[ps], w=[stg])
                ei += 1
            S.dma("sp", zfm_d[row0 + f0:row0 + f0 + m, :], stg[:m, :], r=[stg])
        row0 += n
    col0 = 0
    tiles = token_tiles()
    for gidx, (c0, n) in enumerate(TM_GROUPS):
        wt = wbufs[gi % 3]
        gi += 1
        S.dma("pool", wt[:, :, :n], w_in_d[:, c0:c0 + n].rearrange("(c p) n -> p c n", p=128), w=[wt])
        stg = stg_tm[0]
        for ti, (t0, nt) in enumerate(tiles):
            ps = ps_o[pi % 4]
            pi += 1
            bi = min(t0 // 512, len(blks) - 1)
            for kc in range(8):
                S.op("pe", lambda e: e.matmul(ps[:nt, :n], hT[:, kc, t0:t0 + nt], wt[:, kc, :n],
                                              start=(kc == 0), stop=(kc == 7)),
                     r=[wt, htoks[bi]], w=[ps], signal=(kc == 7))
            if ei % 2 == 0:
                S.op("act", lambda e: e.activation(out=stg[:nt, ti, :n], in_=ps[:nt, :n], func=AF.Copy),
                     r=[ps], w=[stg])
            else:
                S.op("dve", lambda e: e.tensor_copy(out=stg[:nt, ti, :n], in_=ps[:nt, :n]), r=[ps], w=[stg])
            ei += 1
        S.dma("sp", ztm_d[0:TS, col0:col0 + n].rearrange("(i p) n -> p i n", p=128), stg[:, 0:16, :n], r=[stg])
        S.dma("sp", ztm_d[TS:NT, col0:col0 + n], stg[:LS, 16, :n], r=[stg])
        col0 += n
    S.finish(stg_fm + stg_tm + [modT], "sp")
    return nc, S


def pc(v, nchunk):
    return np.ascontiguousarray(np.asarray(v).reshape(nchunk, 128).T)


def p1_inmaps(inp, l, x, ctx):
    maps = []
    for c in range(NCORE):
        b, seg = c // 4, c % 4
        xt = np.concatenate([x[b, seg * TS:(seg + 1) * TS], ctx[b, seg * LS:(seg + 1) * LS]], axis=0).T
        cs = np.stack([inp["c"][b], inp["c_ctx"]], axis=1)
        cs = np.ascontiguousarray(cs.reshape(8, 128, 2).transpose(1, 0, 2))
        maps.append({
            "xT": np.ascontiguousarray(xt, dtype=np.float32),
            "cs": cs.astype(np.float32),
            "ada_w": inp["ada_w"][l],
            "ada_b": pc(inp["ada_b"][l], 48),
            "norm1_g": pc(inp["norm1_g"][l], 8),
            "w_in": inp["w_in"][l],
        })
    return maps


NL3 = TS + 2
NC3 = LS + 2
NT3 = NL3 + NC3
BLK3 = [(0, 512, 0), (512, 512, 0), (1024, 512, 0), (1536, 512, 0), (2048, 2, 0), (NL3, NC3, 1)]
FFN_GROUPS = [[(1, 352, 0), (353, 352, 0)], [(705, 352, 0), (1057, 352, 0)],
              [(1409, 352, 0), (1761, 288, 0), (NL3 + 1, LS, 1)]]
GW3 = 770


def build_p3():
    nc = bass.Bass("TRN2", target_bir_lowering=False)
    S = Sched(nc)
    xT_d = nc.dram_tensor("xT", [D, NT3], F32, kind="ExternalInput").ap()
    mod_d = nc.dram_tensor("modT", [128, 96], F32, kind="ExternalInput").ap()
    g2_d = nc.dram_tensor("norm2_g", [128, 8], F32, kind="ExternalInput").ap()
    og_d = nc.dram_tensor("onorm_g", [128, 2], F32, kind="ExternalInput").ap()
    hm_d = nc.dram_tensor("hmask", [128, 4], F32, kind="ExternalInput").ap()
    oT_d = nc.dram_tensor("oT", [3 * 512, NT3], BF16, kind="ExternalInput").ap()
    gT_d = nc.dram_tensor("gT", [4096, NT3], BF16, kind="ExternalInput").ap()
    wb_d = nc.dram_tensor("w_branch", [3, 512, D], F32, kind="ExternalInput").ap()
    wo_d = nc.dram_tensor("w_out", [D, D], F32, kind="ExternalInput").ap()
    up_d = nc.dram_tensor("ffn_up", [D, 2 * D_FF], F32, kind="ExternalInput").ap()
    cw_d = nc.dram_tensor("conv_w", [128, 3, 44], F32, kind="ExternalInput").ap()
    cb_d = nc.dram_tensor("conv_b", [128, 44], F32, kind="ExternalInput").ap()
    dn_d = nc.dram_tensor("ffn_down", [D_FF, D], F32, kind="ExternalInput").ap()
    xo_d = nc.dram_tensor("xo", [D, NT], F32, kind="ExternalOutput").ap()
    xm_d = nc.dram_tensor("xmid_scratch", [D, NT3], F32, kind="Internal").ap()

    C = Consts(S)
    NX = 8 * NT3 * 2
    big = S.sbuf("big", [128, NX + 8 * NT3], BF16)
    xT = Tile(big.t[:, 0:NX].bitcast(F32).rearrange("p (c t) -> p c t", c=8), "xTv")
    mTt = Tile(big.t[:, NX:NX + 8 * NT3].rearrange("p (c t) -> p c t", c=8), "mTv")
    mT = mTt.t
    arenaA = S.sbuf("arenaA", [128, 12 * NT3], BF16)
    yb = arenaA.t[:, :].rearrange("p (c t) -> p c t", c=12)
    h2T = arenaA.t[:, 0:8 * NT3].rearrange("p (c t) -> p c t", c=8)
    scr = S.sbuf("scr", [128, 3 * NT3], BF16)
    mgt = [Tile(scr.t[:, i * NT3:(i + 1) * NT3], "mgt%d" % i) for i in range(3)]
    gate = mgt[0:2]
    modT = S.sbuf("modT", [128, 48, 2], F32)
    gmod = S.sbuf("gmod2", [128, 8, 2], F32)
    ong = S.sbuf("ong", [128, 2], F32)
    hm = S.sbuf("hm", [128, 4], F32)
    cw = S.sbuf("cw", [128, 3, 44], F32)
    cb = S.sbuf("cb", [128, 44], F32)
    c128 = S.sbuf("c_128eps", [128, 1], F32)
    S.op("pool", lambda e: e.memset(c128[:], float(128 * EPS)), w=[c128])
    ps = [S.psum("ps%d" % i, [128, 512]) for i in range(8)]

    S.dma("sp", xT[:], xT_d.rearrange("(c p) t -> p c t", p=128), w=[xT])
    S.dma("sp", modT[:], mod_d.rearrange("p (c j) -> p c j", j=2), w=[modT])
    S.dma("sp", ong[:], og_d, w=[ong])
    S.dma("sp", hm[:], hm_d, w=[hm])
    S.dma("sp", cw[:], cw_d, w=[cw])
    S.dma("sp", cb[:], cb_d, w=[cb])
    S.op("dve", lambda e: e.tensor_scalar(out=ong[:], in0=ong[:], scalar1=float(np.sqrt(128.0)), scalar2=None,
                                          op0=ALU.mult), r=[ong], w=[ong])
    S.dma("sp", yb, oT_d.rearrange("(c p) t -> p c t", p=128), w=[arenaA])
    sqs = [S.sbuf("sq3_%d" % i, [128, 512], BF16) for i in range(2)]
    rstds = [S.sbuf("rstd3_%d" % i, [128, 512], F32) for i in range(2)]
    tmpf = [S.sbuf("tmpf%d" % i, [128, 512], F32) for i in range(2)]
    sgt = [S.sbuf("sgt%d" % i, [128, 512], F32) for i in range(2)]
    ybt = [Buf("yb%d" % i) for i in range(12)]
    pi = 0
    its = []
    for z in (1, 2):
        for hc in range(4):
            for (t0, nb, j) in BLK3:
                its.append((z, hc, t0, nb))
    gts = {}

    def s1_front(i):
        z, hc, t0, nb = its[i]
        ci = z * 4 + hc
        if (z, hc) not in gts:
            gt = gate[(z * 4 + hc) % 2]
            S.dma("sp", gt[:], gT_d[(z - 1) * 512 + hc * 128:(z - 1) * 512 + (hc + 1) * 128, :], w=[gt])
            S.op("act", lambda e: e.activation(out=gt[:], in_=gt[:], func=AF.Silu), r=[gt], w=[gt])
            gts[(z, hc)] = gt
        p_ = ps[i % 4]
        sq, rstd = sqs[i % 2], rstds[i % 2]
        S.op("pool", lambda e: e.tensor_tensor(out=sq[:, :nb], in0=yb[:, ci, t0:t0 + nb],
                                               in1=yb[:, ci, t0:t0 + nb], op=ALU.mult),
             r=[arenaA, ybt[ci]], w=[sq])
        S.op("pe", lambda e: e.matmul(p_[:, :nb], C.ones_bf[:], sq[:, :nb], start=True, stop=True),
             r=[sq, C.ones_bf], w=[p_])
        S.op("act", lambda e: e.activation(out=rstd[:, :nb], in_=p_[:, :nb], func=AF.Ln,
                                           bias=c128[:, 0:1], scale=1.0), r=[p_, c128], w=[rstd])
        S.op("act", lambda e: e.activation(out=rstd[:, :nb], in_=rstd[:, :nb], func=AF.Exp, scale=-0.5),
             r=[rstd], w=[rstd])

    def s1_back(i):
        z, hc, t0, nb = its[i]
        ci = z * 4 + hc
        gt = gts[(z, hc)]
        rstd = rstds[i % 2]
        tf = tmpf[i % 2]
        S.op("dve", lambda e: e.scalar_tensor_tensor(out=tf[:, :nb], in0=yb[:, ci, t0:t0 + nb],
                                                     scalar=ong[:, z - 1:z], in1=rstd[:, :nb],
                                                     op0=ALU.mult, op1=ALU.mult),
             r=[arenaA, ybt[ci], ong, rstd], w=[tf])
        S.op("dve", lambda e: e.tensor_tensor(out=yb[:, ci, t0:t0 + nb], in0=tf[:, :nb],
                                              in1=gt[:, t0:t0 + nb], op=ALU.mult),
             r=[tf, gt, arenaA], w=[ybt[ci]])
    s1_front(0)
    for i in range(len(its)):
        if i + 1 < len(its):
            s1_front(i + 1)
        s1_back(i)
    pi = 4
    ybr = [ybt[i] for i in range(12)]
    wbr = [S.sbuf("wbr%d" % i, [128, 4, 128], BF16) for i in range(6)]
    wi = 0
    for oc in range(8):
        wts = []
        for z in range(3):
            wt = wbr[wi % 6]
            wi += 1
            mg = mgt[z]
            S.dma("pool", wt[:], wb_d[z, :, oc * 128:(oc + 1) * 128].rearrange("(c p) n -> p c n", p=128), w=[wt])
            S.dma("sp", mg[:], gT_d[1024 + z * 1024 + oc * 128:1024 + z * 1024 + (oc + 1) * 128, :], w=[mg])
            S.op("act", lambda e: e.activation(out=mg[:], in_=mg[:], func=AF.Sigmoid), r=[mg], w=[mg])
            wts.append(wt)
        for (t0, nb, j) in BLK3:
            pz = []
            for z in range(3):
                p_ = ps[2 + pi % 6]
                pi += 1
                for kc in range(4):
                    S.op("pe", lambda e: e.matmul(p_[:, :nb], wts[z][:, kc, :], yb[:, z * 4 + kc, t0:t0 + nb],
                                                  start=(kc == 0), stop=(kc == 3)),
                         r=[wts[z], arenaA] + ybr[z * 4:z * 4 + 4], w=[p_], signal=(kc == 3))
                pz.append(p_)
            tA, tB = tmpf[0], tmpf[1]
            S.op("dve", lambda e: e.tensor_tensor(out=tA[:, :nb], in0=pz[0][:, :nb], in1=mgt[0][:, t0:t0 + nb],
                                                  op=ALU.mult), r=[pz[0], mgt[0]], w=[tA])
            S.op("dve", lambda e: e.tensor_tensor(out=tB[:, :nb], in0=pz[1][:, :nb], in1=mgt[1][:, t0:t0 + nb],
                                                  op=ALU.mult), r=[pz[1], mgt[1]], w=[tB])
            S.op("pool", lambda e: e.tensor_tensor(out=tA[:, :nb], in0=tA[:, :nb], in1=tB[:, :nb], op=ALU.add),
                 r=[tA, tB], w=[tA])
            S.op("dve", lambda e: e.tensor_tensor(out=tB[:, :nb], in0=pz[2][:, :nb], in1=mgt[2][:, t0:t0 + nb],
                                                  op=ALU.mult), r=[pz[2], mgt[2]], w=[tB])
            S.op("pool", lambda e: e.tensor_tensor(out=mT[:, oc, t0:t0 + nb], in0=tA[:, :nb], in1=tB[:, :nb],
                                                   op=ALU.add), r=[tA, tB], w=[mTt])
    wo = [S.sbuf("wo%d" % i, [128, 8, 128], BF16) for i in range(2)]
    for oc in range(8):
        wt = wo[oc % 2]
        S.dma("pool", wt[:], wo_d[:, oc * 128:(oc + 1) * 128].rearrange("(c p) n -> p c n", p=128), w=[wt])
        for (t0, nb, j) in BLK3:
            p_ = ps[2 + pi % 4]
            pi += 1
            for kc in range(8):
                S.op("pe", lambda e: e.matmul(p_[:, :nb], wt[:, kc, :], mT[:, kc, t0:t0 + nb],
                                              start=(kc == 0), stop=(kc == 7)),
                     r=[wt, mTt], w=[p_], signal=(kc == 7))
            S.op("dve", lambda e: e.scalar_tensor_tensor(out=xT[:, oc, t0:t0 + nb], in0=p_[:, :nb],
                                                         scalar=modT[:, 16 + oc, j:j + 1], in1=xT[:, oc, t0:t0 + nb],
                                                         op0=ALU.mult, op1=ALU.add),
                 r=[p_, modT, xT], w=[xT])
    S.dma("sp", xm_d.rearrange("(c p) t -> p c t", p=128), xT[:], r=[xT])
    emit_gmod(S, gmod, g2_d, modT, 4, "n2")
    h2tile = Tile(h2T, "h2v")
    h2tile.b = arenaA.b
    sq8t = Tile(scr.t[:, 0:8 * 512].rearrange("p (c t) -> p c t", c=8), "sq8v")
    S.barrier()
    emit_norm_mod(S, C, xT, h2tile, arenaA, gmod, modT, 3, ps[0], "n2", blocks=BLK3, ps2=ps[1],
                  bufs=([sq8t], rstds, tmpf))
    for i, col in enumerate((0, NL3 - 1, NL3, NT3 - 1)):
        S.op("dve", lambda e: e.tensor_scalar(out=h2T[:, :, col:col + 1], in0=h2T[:, :, col:col + 1],
                                              scalar1=hm[:, i:i + 1], scalar2=None, op0=ALU.mult),
             r=[arenaA, hm], w=[arenaA])
    S.barrier()
    gFt = Buf("gF")
    gF = big.t[:, 0:22 * NT].rearrange("p (c t) -> p c t", c=22)
    spare = arenaA.t[:, 8 * NT3:12 * NT3]
    wupA = [Tile(spare[:, i * 4096:(i + 1) * 4096].rearrange("p (c n) -> p c n", c=8), "wupA%d" % i) for i in range(2)]
    wupB = [S.sbuf("wupB%d" % i, [128, 8, 512], BF16) for i in range(2)]
    wdn = [Tile(scr.t[:, i * 2816:(i + 1) * 2816].rearrange("p (c n) -> p c n", c=22), "wdn%d" % i) for i in range(2)]
    ua = tmpf
    ub = [S.sbuf("ub%d" % i, [128, 512], F32) for i in range(2)]
    xmt = sgt
    blocks = [b_ for grp in FFN_GROUPS for b_ in grp]
    gcol = []
    o = 0
    for (c0, n, j) in blocks:
        gcol.append(o)
        o += n
    assert o == NT
    ui = 0
    for fg in range(6):
        nf = 4 if fg < 5 else 2
        wa, wb_ = wupA[fg % 2], wupB[fg % 2]
        S.dma("pool", wa[:, :, 0:nf * 128], up_d[:, fg * 512:fg * 512 + nf * 128].rearrange("(c p) n -> p c n", p=128),
              w=[wa])
        S.dma("pool", wb_[:, :, 0:nf * 128],
              up_d[:, D_FF + fg * 512:D_FF + fg * 512 + nf * 128].rearrange("(c p) n -> p c n", p=128), w=[wb_])
        for fl in range(nf):
            f = fg * 4 + fl
            for bi, (c0, n, j) in enumerate(blocks):
                pa = ps[pi % 4]
                pb = ps[4 + pi % 4]
                pi += 1
                for wt, p_ in ((wa, pa), (wb_, pb)):
                    for kc in range(8):
                        S.op("pe", lambda e: e.matmul(p_[:, :n + 2], wt[:, kc, fl * 128:(fl + 1) * 128],
                                                      h2T[:, kc, c0 - 1:c0 + n + 1], start=(kc == 0), stop=(kc == 7)),
                             r=[wt, arenaA], w=[p_], signal=(kc == 7))
                a_ = ua[ui % 2]
                b_ = ub[ui % 2]
                ui += 1
                for half, p_, u_ in ((0, pa, a_), (1, pb, b_)):
                    ch = half * 22 + f
                    S.op("act", lambda e: e.activation(out=u_[:, :n], in_=p_[:, 1:n + 1], func=AF.Identity,
                                                       bias=cb[:, ch:ch + 1], scale=cw[:, 1, ch:ch + 1]),
                         r=[p_, cb, cw], w=[u_])
                    S.op("dve", lambda e: e.scalar_tensor_tensor(out=u_[:, :n], in0=p_[:, 0:n],
                                                                 scalar=cw[:, 0, ch:ch + 1], in1=u_[:, :n],
                                                                 op0=ALU.mult, op1=ALU.add),
                         r=[p_, cw, u_], w=[u_])
                    S.op("dve", lambda e: e.scalar_tensor_tensor(out=u_[:, :n], in0=p_[:, 2:n + 2],
                                                                 scalar=cw[:, 2, ch:ch + 1], in1=u_[:, :n],
                                                                 op0=ALU.mult, op1=ALU.add),
                         r=[p_, cw, u_], w=[u_])
                sg = sgt[ui % 2]
                S.op("act", lambda e: e.activation(out=sg[:, :n], in_=a_[:, :n], func=AF.Silu), r=[a_], w=[sg])
                S.op("pool", lambda e: e.tensor_tensor(out=gF[:, f, gcol[bi]:gcol[bi] + n], in0=sg[:, :n],
                                                       in1=b_[:, :n], op=ALU.mult), r=[sg, b_], w=[gFt])
    xo_v = xo_d.rearrange("(c p) t -> p c t", p=128)
    xm_v = xm_d.rearrange("(c p) t -> p c t", p=128)
    xi = 0
    for oc in range(8):
        wt = wdn[oc % 2]
        S.dma("pool", wt[:], dn_d[:, oc * 128:(oc + 1) * 128].rearrange("(c p) n -> p c n", p=128), w=[wt])
        for bi, (c0, n, j) in enumerate(blocks):
            p_ = ps[pi % 8]
            pi += 1
            xm = xmt[xi % 2]
            xi += 1
            S.dma("sp", xm[:, :n], xm_v[:, oc, c0:c0 + n], w=[xm])
            for f in range(22):
                S.op("pe", lambda e: e.matmul(p_[:, :n], wt[:, f, :], gF[:, f, gcol[bi]:gcol[bi] + n],
                                              start=(f == 0), stop=(f == 21)),
                     r=[wt, gFt], w=[p_], signal=(f == 21))
            S.op("dve", lambda e: e.scalar_tensor_tensor(out=xm[:, :n], in0=p_[:, :n],
                                                         scalar=modT[:, 40 + oc, j:j + 1], in1=xm[:, :n],
                                                         op0=ALU.mult, op1=ALU.add),
                 r=[p_, modT, xm], w=[xm])
            S.dma("sp", xo_v[:, oc, gcol[bi]:gcol[bi] + n], xm[:, :n], r=[xm])
    S.finish(xmt, "sp")
    return nc, S


def halo_cols(a, b, seg, n, tot):
    lo, hi = seg * n - 1, (seg + 1) * n + 1
    out = np.zeros((hi - lo,) + a.shape[2:], a.dtype)
    s, e = max(lo, 0), min(hi, tot)
    out[s - lo:e - lo] = a[b, s:e]
    return out


def p3_inmaps(inp, l, x, ctx, modT_list, o_lat, o_ctx, g_lat, g_ctx):
    maps = []
    cwl = inp["ffn_conv_w"][l]
    cw = np.ascontiguousarray(cwl.reshape(3, 44, 128).transpose(2, 0, 1))
    for c in range(NCORE):
        b, seg = c // 4, c % 4

        def cols(al, ac):
            return np.ascontiguousarray(np.concatenate([halo_cols(al, b, seg, TS, T), halo_cols(ac, b, seg, LS, L)], 0).T)
        hm = np.array([seg > 0, seg < 3, seg > 0, seg < 3], np.float32)
        maps.append({
            "xT": cols(x, ctx).astype(np.float32),
            "modT": modT_list[c],
            "norm2_g": pc(inp["norm2_g"][l], 8),
            "onorm_g": np.ascontiguousarray(np.stack([inp["gla_out_norm_g"][l], inp["ret_out_norm_g"][l]], 1)),
            "hmask": np.ascontiguousarray(np.broadcast_to(hm, (128, 4))),
            "oT": cols(o_lat, o_ctx),
            "gT": cols(g_lat, g_ctx),
            "w_branch": inp["w_branch"][l],
            "w_out": inp["w_out"][l],
            "ffn_up": inp["ffn_up"][l],
            "conv_w": cw,
            "conv_b": pc(inp["ffn_conv_b"][l], 44),
            "ffn_down": inp["ffn_down"][l],
        })
    return maps


NQB = T // 128
NKT = TB // 128
NEG = -30000.0


def p2_consts_np():
    bf = ml_dtypes.bfloat16
    c = {}
    bo = np.zeros((128, 128), np.float32)
    bo[:64, :64] = 1
    bo[64:, 64:] = 1
    c["BO"] = bo.astype(bf)
    R = np.zeros((64, 64), np.float32)
    for base in (0, 32):
        for f in range(16):
            R[base + f, base + 16 + f] = -1.0
            R[base + 16 + f, base + f] = 1.0
    rp = np.zeros((128, 128), np.float32)
    rp[:64, :64] = R.T
    rp[64:, 64:] = R.T
    c["RP"] = rp.astype(bf)
    c["IDN"] = np.eye(128, dtype=np.float32).astype(bf)
    s = np.arange(128)[:, None]
    i = np.arange(128)[None, :]
    mb = np.zeros((128, 2, 2, 128), np.float32)
    mb[:, 0] = np.where(s >= i, 0.0, NEG)[:, None, :]
    mb[:, 1] = np.where(s <= i, 0.0, NEG)[:, None, :]
    c["MB"] = mb.reshape(128, 2, 256).astype(bf)
    same = (s // 64) == (i // 64)
    sm = np.zeros((128, 2, 4, 128), np.float32)
    sm[:, 0] = (same & (s <= i))[:, None, :]
    sm[:, 1] = (same & (s >= i))[:, None, :]
    c["SM"] = sm.reshape(128, 2, 512).astype(bf)
    tri = np.zeros((128, 4, 128), np.float32)
    tri[:, 0] = same & (s <= i)
    tri[:, 1] = same & (s >= i)
    tri[:, 2] = same & (s > i)
    tri[:, 3] = same & (s < i)
    c["TRI"] = (tri * (-1.0 / 16.0)).astype(bf)
    t = np.arange(512) % 64
    idx = np.zeros((64, 2, 512), np.float32)
    idx[:, 0] = (t + 1)[None, :]
    idx[:, 1] = (64 - t)[None, :]
    c["IDX"] = idx
    p = np.arange(128) % 64
    c["PIDX"] = np.stack([63 - p, p], 1).astype(np.float32)
    return c


def rope_tables_np():
    tt = np.arange(T)
    row = (tt // GRID_W).astype(np.float32)
    col = (tt % GRID_W).astype(np.float32)
    inv = (10000.0 ** (-np.arange(16, dtype=np.float32) * 2.0 / 32.0)).astype(np.float32)
    ang = np.zeros((64, T), np.float32)
    for d in range(64):
        pos = row if d < 32 else col
        ang[d] = pos * inv[d % 16]
    cos = np.cos(ang).astype(np.float32)
    sin = np.sin(ang).astype(np.float32)
    return cos, sin


class P2:
    def __init__(self):
        nc = bass.Bass("TRN2", target_bir_lowering=False)
        self.nc = nc
        S = Sched(nc)
        self.S = S
        self.C = Consts(S)
        di = lambda n, sh, dt: nc.dram_tensor(n, list(sh), dt, kind="ExternalInput").ap()
        do = lambda n, sh, dt: nc.dram_tensor(n, list(sh), dt, kind="ExternalOutput").ap()
        self.d = {
            "aqT": di("aqT", [128, TB], BF16), "akT": di("akT", [128, TB], BF16), "av": di("av", [128, NKT * 64], BF16),
            "cosT": di("cosT", [64, T], F32), "sinT": di("sinT", [64, T], F32),
            "anp": di("anp", [128, 4], F32),
            "BO": di("BO", [128, 128], BF16), "RP": di("RP", [128, 128], BF16), "IDN": di("IDN", [128, 128], BF16),
            "MB": di("MB", [128, 2, 256], BF16), "SM": di("SM", [128, 2, 512], BF16),
            "TRI": di("TRI", [128, 4, 128], BF16), "IDX": di("IDX", [64, 2, 512], F32), "PIDX": di("PIDX", [128, 2], F32),
            "oaT": do("oaT", [128, TB], BF16),
        }
        self.R = [S.sbuf("R%d" % i, [128, TB], BF16) for i in range(6)]
        self.ps = [S.psum("ps%d" % i, [128, 512]) for i in range(8)]
        self.k = {}
        for n in ("BO", "RP", "IDN"):
            self.k[n] = S.sbuf("k_" + n, [128, 128], BF16)
            S.dma("sp", self.k[n][:], self.d[n], w=[self.k[n]])
        self.k["MB"] = S.sbuf("k_MB", [128, 2, 256], BF16)
        S.dma("sp", self.k["MB"][:], self.d["MB"], w=[self.k["MB"]])
        self.c64 = S.sbuf("c_64eps", [128, 1], F32)
        S.op("pool", lambda e: e.memset(self.c64[:], float(64 * EPS)), w=[self.c64])
        self.tmp = [S.sbuf("p2tmp%d" % i, [128, 512], F32) for i in range(4)]
        self.tbf = [S.sbuf("p2tbf%d" % i, [128, 512], BF16) for i in range(4)]
        self.cs = [S.sbuf("p2cs%d" % i, [128, 2, 512], F32) for i in range(2)]

    def prep_slots(self):
        if hasattr(self, "slots"):
            return
        r3 = self.R[3].t
        def v(lo, hi, f32, name):
            ap = r3[:, lo:hi]
            if f32:
                ap = ap.bitcast(F32)
            return Tile(ap, name)
        self.slots = [
            dict(xin=self.tbf[0], sq=self.tbf[2], xg=self.tbf[3], rstd=self.tmp[0], t1=self.tmp[1], t2=self.tmp[2],
                 pa=self.ps[0], pb=self.ps[1]),
            dict(xin=self.tbf[1], sq=v(0, 512, False, "s1sq"), xg=v(512, 1024, False, "s1xg"),
                 rstd=v(1024, 2048, True, "s1rstd"), t1=v(2048, 3072, True, "s1t1"), t2=v(3072, 4096, True, "s1t2"),
                 pa=self.ps[2], pb=self.ps[3]),
        ]

    def norm_rope_block(self, src_d, dst, c0, n, gain_ap, gain_tok, roped, bi, np_=128):
        S, k = self.S, self.k
        self.prep_slots()
        sl = self.slots[bi % 2]
        xin, sq, xg, rstd = sl["xin"], sl["sq"], sl["xg"], sl["rstd"]
        ps_a, ps_b = sl["pa"], sl["pb"]
        S.dma("sp", xin[:np_, :n], src_d[:np_, c0:c0 + n], w=[xin])
        S.op("act", lambda e: e.activation(out=sq[:np_, :n], in_=xin[:np_, :n], func=AF.Square), r=[xin], w=[sq])
        S.op("pe", lambda e: e.matmul(ps_a[:np_, :n], k["BO"][:np_, :np_], sq[:np_, :n], start=True, stop=True),
             r=[k["BO"], sq], w=[ps_a])
        S.op("act", lambda e: e.activation(out=rstd[:np_, :n], in_=ps_a[:np_, :n], func=AF.Ln,
                                           bias=self.c64[:np_, 0:1], scale=1.0), r=[ps_a, self.c64], w=[rstd])
        S.op("act", lambda e: e.activation(out=rstd[:np_, :n], in_=rstd[:np_, :n], func=AF.Exp, scale=-0.5),
             r=[rstd], w=[rstd])
        if not roped:
            S.op("dve", lambda e: e.scalar_tensor_tensor(out=dst[:np_, c0:c0 + n], in0=xin[:np_, :n], scalar=gain_ap,
                                                         in1=rstd[:np_, :n], op0=ALU.mult, op1=ALU.mult),
                 r=[xin, gain_tok, rstd], w=[dst])
            return
        S.op("dve", lambda e: e.scalar_tensor_tensor(out=xg[:np_, :n], in0=xin[:np_, :n], scalar=gain_ap,
                                                     in1=rstd[:np_, :n], op0=ALU.mult, op1=ALU.mult),
             r=[xin, gain_tok, rstd], w=[xg])
        self.rope_block(xg, dst, c0, n, np_, slot=sl)

    def load_cs(self, c0, n):
        S = self.S
        cs = self.cs[(c0 // 512) % 2]
        if getattr(cs, "c0", None) == c0 and getattr(cs, "gen", None) == self.gen:
            return cs
        for hf in range(2):
            S.dma("sp", cs[hf * 64:(hf + 1) * 64, 0, :n], self.d["cosT"][:, c0:c0 + n], w=[cs])
            S.dma("sp", cs[hf * 64:(hf + 1) * 64, 1, :n], self.d["sinT"][:, c0:c0 + n], w=[cs])
        cs.c0 = c0
        cs.gen = self.gen
        return cs

    def rope_block(self, xg, dst, c0, n, np_, scale=None, src_c0=None, slot=None):
        S, k = self.S, self.k
        ps_b = self.ps[7] if src_c0 is not None else (slot["pb"] if slot else self.ps[1])
        cs = self.load_cs(c0 if src_c0 is None else src_c0, n)
        t1, t2 = (slot["t1"], slot["t2"]) if slot else (self.tmp[1], self.tmp[2])
        xs = 0 if src_c0 is None else src_c0
        xgt = xg
        xg = xg.t[:, xs:xs + n]
        S.op("pe", lambda e: e.matmul(ps_b[:np_, :n], k["RP"][:np_, :np_], xg[:np_, :n], start=True, stop=True),
             r=[k["RP"], xgt], w=[ps_b])
        S.op("dve", lambda e: e.tensor_tensor(out=t1[:np_, :n], in0=xg[:np_, :n], in1=cs[:np_, 0, :n], op=ALU.mult),
             r=[xgt, cs], w=[t1])
        S.op("dve", lambda e: e.tensor_tensor(out=t2[:np_, :n], in0=ps_b[:np_, :n], in1=cs[:np_, 1, :n], op=ALU.mult),
             r=[ps_b, cs], w=[t2])
        if scale is None:
            S.op("pool", lambda e: e.tensor_tensor(out=dst[:np_, c0:c0 + n], in0=t1[:np_, :n], in1=t2[:np_, :n],
                                                   op=ALU.add), r=[t1, t2], w=[dst])
        else:
            S.op("pool", lambda e: e.tensor_tensor(out=t1[:np_, :n], in0=t1[:np_, :n], in1=t2[:np_, :n],
                                                   op=ALU.add), r=[t1, t2], w=[t1])
            S.op("act", lambda e: e.activation(out=dst[:np_, c0:c0 + n], in_=t1[:np_, :n], func=AF.Copy,
                                               scale=float(scale)), r=[t1], w=[dst])

    def attention(self, stage=9, nqb=None):
        S, k, d = self.S, self.k, self.d
        self.gen = "attn"
        qn, kn, vt = self.R[0], self.R[1], self.R[2]
        anp = S.sbuf("anp", [128, 4], F32)
        S.dma("sp", anp[:], d["anp"], w=[anp])
        S.op("dve", lambda e: e.tensor_scalar(out=anp[:, 1:2], in0=anp[:, 1:2], scalar1=8.0, scalar2=None,
                                              op0=ALU.mult), r=[anp], w=[anp])
        es = S.sbuf("esink", [128, 2], F32)
        S.op("act", lambda e: e.activation(out=es[:], in_=anp[:, 2:4], func=AF.Exp), r=[anp], w=[es])
        esink = S.sbuf("esink_t", [64, 256], F32)
        ones_f = S.sbuf("ones_f", [64, 128], F32)
        S.op("pool", lambda e: e.memset(ones_f[:], 1.0), w=[ones_f])
        for h in range(2):
            S.op("dve", lambda e: e.tensor_scalar(out=esink[:, h * 128:(h + 1) * 128], in0=ones_f[:],
                                                  scalar1=es[0:64, h:h + 1], scalar2=None, op0=ALU.mult),
                 r=[ones_f, es], w=[esink])
        vv = vt.t[:, 0:NKT * 64].rearrange("p (i d) -> p i d", d=64)
        S.dma("sp", vt[:, 0:NKT * 64], d["av"], w=[vt])
        blocks = [(i * 512, 512, True) for i in range(T // 512)] + [(T, L, False)]
        self.att_out_toks = [qn, kn, vt]
        if stage < 1:
            return
        for bi, (c0, n, roped) in enumerate(blocks):
            self.norm_rope_block(d["aqT"], qn, c0, n, anp[:, 0:1], anp, roped, 0)
            self.norm_rope_block(d["akT"], kn, c0, n, anp[:, 1:2], anp, roped, 1)
        PT = [S.sbuf("PT%d" % i, [128, 2, 5, 128], BF16) for i in range(2)]
        ost = [S.sbuf("oast%d" % i, [64, 2, 512], BF16) for i in range(2)]
        rec = S.sbuf("arec", [64, 256], F32)
        oview = d["oaT"].rearrange("(h d) t -> d h t", h=2)
        if stage < 2:
            return
        for qi, qb in enumerate(range(NQB + 2) if nqb is None else nqb):
            if qb < NQB:
                chunks = ([(qb - 1, 0)] if qb > 0 else []) + [(qb, None)] + ([(qb + 1, 1)] if qb < NQB - 1 else []) \
                    + [(NQB, None), (NQB + 1, None)]
            else:
                chunks = [(NQB, None), (NQB + 1, None)]
            A = self.ps[0:2] if qi % 2 == 0 else self.ps[2:4]
            Bk = self.ps[4:6]
            pv = self.ps[6 + qi % 2]
            pt = PT[qi % 2]
            qc = slice(qb * 128, (qb + 1) * 128)
            nch = len(chunks)
            for h in range(2):
                for bank, sl in ((A[h], range(0, min(nch, 4))), (Bk[h], range(4, nch))):
                    mms = []
                    for ci in sl:
                        kt, mi = chunks[ci]
                        o_ = bank[:, (ci % 4) * 128:(ci % 4 + 1) * 128]
                        mms.append((o_, kn[h * 64:(h + 1) * 64, kt * 128:(kt + 1) * 128],
                                    qn[h * 64:(h + 1) * 64, qc], [kn, qn]))
                        if mi is not None:
                            mms.append((o_, k["IDN"][:], k["MB"][:, mi, 0:128], [k["IDN"], k["MB"]]))
                    for i_, (o_, l_, r_, rd) in enumerate(mms):
                        S.op("pe", lambda e: e.matmul(o_, l_, r_, start=(i_ == 0), stop=(i_ == len(mms) - 1),
                                                      skip_group_check=True),
                             r=rd, w=[bank], signal=(i_ == len(mms) - 1))
            self.att_out_toks = list(self.ps)
            if stage < 3:
                continue
            for h in range(2):
                na = min(nch, 4)
                S.op("act", lambda e: e.activation(out=pt[:, h, 0:na, :],
                                                   in_=A[h][:, 0:na * 128].rearrange("p (c n) -> p c n", n=128),
                                                   func=AF.Exp), r=[A[h]], w=[pt])
                if nch > 4:
                    S.op("act", lambda e: e.activation(out=pt[:, h, 4, :], in_=Bk[h][:, 0:128], func=AF.Exp),
                         r=[Bk[h]], w=[pt])
            self.att_out_toks = list(self.ps) + PT
            if stage < 4:
                continue
            for ci, (kt, mi) in enumerate(chunks):
                S.op("pe", lambda e: e.matmul(pv[0:64, 0:256], vv[:, kt, :], pt[:, :, ci, :],
                                              start=(ci == 0), stop=(ci == nch - 1)),
                     r=[vt, pt], w=[pv], signal=False)
            for ci, (kt, mi) in enumerate(chunks):
                S.op("pe", lambda e: e.matmul(pv[0:64, 256:512], self.C.ones_bf[:, 0:64], pt[:, :, ci, :],
                                              start=(ci == 0), stop=(ci == nch - 1)),
                     r=[self.C.ones_bf, pt], w=[pv], signal=(ci == nch - 1))
            if stage < 5:
                continue
            st = ost[(qb // 4) % 2]
            so = (qb % 4) * 128
            S.op("dve", lambda e: e.tensor_tensor(out=rec[:], in0=pv[0:64, 256:512], in1=esink[:], op=ALU.add),
                 r=[pv, esink], w=[rec])
            S.op("dve", lambda e: e.reciprocal(out=rec[:], in_=rec[:]), r=[rec], w=[rec])
            S.op("dve", lambda e: e.tensor_tensor(out=st[:, :, so:so + 128],
                                                  in0=pv[0:64, 0:256].rearrange("p (h n) -> p h n", h=2),
                                                  in1=rec[:, :].rearrange("p (h n) -> p h n", h=2), op=ALU.mult),
                 r=[pv, rec], w=[st])
            if qb % 4 == 3 or qb == NQB + 1:
                g0 = (qb // 4) * 512
                wdt = so + 128
                if stage >= 6:
                    S.dma("sp", oview[:, :, g0:g0 + wdt], st[:, :, 0:wdt], r=[st])
            self.att_out_toks = ost + list(self.ps) + PT


    def scan_setup(self):
        S, d, nc = self.S, self.d, self.nc
        if hasattr(self, "scan_ready"):
            return
        self.scan_ready = True
        di = lambda n, sh, dt: nc.dram_tensor(n, list(sh), dt, kind="ExternalInput").ap()
        do = lambda n, sh, dt: nc.dram_tensor(n, list(sh), dt, kind="ExternalOutput").ap()
        d.update({
            "gqT": di("gqT", [64, TB], BF16), "gkT": di("gkT", [64, TB], BF16), "gaT": di("gaT", [32, TB], BF16),
            "gk": di("gk", [128, NKT * 64], BF16), "gv": di("gv", [128, NKT * 128], BF16),
            "wg": di("wg", [33, 128], F32),
            "rqT": di("rqT", [64, TB], BF16), "rkT": di("rkT", [64, TB], BF16), "rv": di("rv", [128, NKT * 128], BF16),
            "rld": di("rld", [128, 2], F32),
            "ogT": do("ogT", [128, TB], BF16), "orT": do("orT", [128, TB], BF16),
        })
        k = self.k
        k["SM"] = S.sbuf("k_SM", [128, 2, 512], BF16)
        S.dma("sp", k["SM"][:], d["SM"], w=[k["SM"]])
        k["TRI"] = S.sbuf("k_TRI", [128, 4, 128], BF16)
        S.dma("sp", k["TRI"][:], d["TRI"], w=[k["TRI"]])
        k["IDX"] = S.sbuf("k_IDX", [64, 2, 512], F32)
        S.dma("sp", k["IDX"][:], d["IDX"], w=[k["IDX"]])
        k["PIDX"] = S.sbuf("k_PIDX", [128, 2], F32)
        S.dma("sp", k["PIDX"][:], d["PIDX"], w=[k["PIDX"]])
        self.onef = S.sbuf("c_onef", [128, 1], F32)
        S.op("pool", lambda e: e.memset(self.onef[:], 1.0), w=[self.onef])
        self.acc = S.sbuf("sc_acc", [128, TB], F32)
        self.Sall = [S.sbuf("sc_Sall%d" % i, [64, 9, 128], F32) for i in range(3)]
        self.Sbf = [S.sbuf("sc_Sbf%d" % i, [64, 8, 128], BF16) for i in range(2)]
        self.qin = [S.sbuf("sc_qin%d" % i, [64, 512], BF16) for i in range(2)]
        self.kin = [S.sbuf("sc_kin%d" % i, [64, 512], BF16) for i in range(2)]
        self.kend = [S.sbuf("sc_kend%d" % i, [128, 4, 64], BF16) for i in range(2)]
        self.attm = [S.sbuf("sc_attm%d" % i, [128, 512], BF16) for i in range(2)]
        self.Eq = [S.sbuf("sc_Eq%d" % i, [64, 512], F32) for i in range(2)]
        self.Ek = S.sbuf("sc_Ek", [64, 512], F32)
        self.Eend = S.sbuf("sc_Eend", [128, 4, 64], F32)
        self.sp = S.sbuf("sc_sp", [128, 4, 128], F32)
        self.gaf = self.tmp[2]
        self.gi = 0

    def scan_groups(self, z):
        lat = [(g * 512, 512) for g in range(T // 512)]
        return [(T, L)] + (lat if z == 0 else lat[::-1])

    def chunk_front(self, z, c0, n, qin, kin, kend, kend_tok, vtm, dec_fn, dec_toks, outb):
        S, k = self.S, self.k
        A = self.ps[0]
        Bk, Ck = self.ds_banks[self.gi % 2]
        nt = n // 128
        nchunk = 2 * nt
        t0 = c0 // 128
        gi = self.gi
        self.gi += 1
        attm = self.attm[gi % 2]
        Sall, Snext = self.Sall[gi % 3], self.Sall[(gi + 1) % 3]
        vv = vtm.t[:, 0:NKT * 128].rearrange("p (i d) -> p i d", d=128)
        for p in range(nt):
            S.op("pe", lambda e: e.matmul(A[:, p * 128:(p + 1) * 128], kin[:, p * 128:(p + 1) * 128],
                                          qin[:, p * 128:(p + 1) * 128], start=(p == 0), stop=(p == nt - 1),
                                          skip_group_check=True), r=[kin, qin], w=[A], signal=(p == nt - 1))
        S.op("dve", lambda e: e.tensor_tensor(out=attm[:, :n], in0=A[:, :n], in1=k["SM"][:, z, :n], op=ALU.mult),
             r=[A, k["SM"]], w=[attm])
        for cc in range(nchunk):
            p, hf = cc // 2, cc % 2
            bank = Bk if hf == 0 else Ck
            S.op("pe", lambda e: e.matmul(bank[0:64, p * 128:(p + 1) * 128], kend[hf * 64:(hf + 1) * 64, p, :],
                                          vv[hf * 64:(hf + 1) * 64, t0 + p, :], start=(p == 0), stop=(p == nt - 1),
                                          skip_group_check=True), r=[kend_tok, vtm], w=[bank], signal=(p == nt - 1))
        order = list(range(nchunk)) if z == 0 else list(range(nchunk - 1, -1, -1))
        pos = {}
        for i, cc in enumerate(order):
            pos[cc] = i
            p, hf = cc // 2, cc % 2
            bank = Bk if hf == 0 else Ck
            last = (i == nchunk - 1)
            o_ = Snext[:, 0, :] if last else Sall[:, i + 1, :]
            S.op("dve", lambda e: e.scalar_tensor_tensor(out=o_, in0=Sall[:, i, :], scalar=dec_fn(cc),
                                                         in1=bank[0:64, p * 128:(p + 1) * 128],
                                                         op0=ALU.mult, op1=ALU.add),
                 r=[Sall, bank] + dec_toks, w=[Snext if last else Sall])
        return dict(z=z, c0=c0, n=n, nt=nt, nchunk=nchunk, t0=t0, gi=gi, attm=attm, Sall=Sall, qin=qin, vtm=vtm,
                    vv=vv, pos=pos, outb=outb)

    def chunk_back(self, c):
        if c is None:
            return
        S = self.S
        Dk = self.ps[3]
        z, c0, n, nt, nchunk, t0, gi = c["z"], c["c0"], c["n"], c["nt"], c["nchunk"], c["t0"], c["gi"]
        attm, Sall, qin, vtm, vv, pos, outb = c["attm"], c["Sall"], c["qin"], c["vtm"], c["vv"], c["pos"], c["outb"]
        Sbf = self.Sbf[gi % 2]
        S.op("act", lambda e: e.activation(out=Sbf[:, 0:nchunk, :], in_=Sall[:, 0:nchunk, :], func=AF.Copy),
             r=[Sall], w=[Sbf])
        nmm = nt * 3
        mi = 0
        for p in range(nt):
            S.op("pe", lambda e: e.matmul(Dk[:, p * 128:(p + 1) * 128], vv[:, t0 + p, :], attm[:, p * 128:(p + 1) * 128],
                                          start=(mi == 0), stop=False, skip_group_check=True),
                 r=[vtm, attm], w=[Dk], signal=False)
            mi += 1
            for hf in range(2):
                cc = 2 * p + hf
                S.op("pe", lambda e: e.matmul(Dk[:, cc * 64:(cc + 1) * 64], Sbf[:, pos[cc], :],
                                              qin[:, cc * 64:(cc + 1) * 64], start=False, stop=(mi == nmm - 1),
                                              skip_group_check=True),
                     r=[Sbf, qin], w=[Dk], signal=(mi == nmm - 1))
                mi += 1
        if z == 0:
            S.op("act", lambda e: e.activation(out=self.acc[:, c0:c0 + n], in_=Dk[:, :n], func=AF.Copy),
                 r=[Dk], w=[self.acc])
        else:
            S.op("dve", lambda e: e.tensor_tensor(out=outb[:, c0:c0 + n], in0=Dk[:, :n], in1=self.acc[:, c0:c0 + n],
                                                  op=ALU.add), r=[Dk, self.acc], w=[outb])

    def scan_init_state(self):
        S = self.S
        Sall = self.Sall[self.gi % 3]
        S.op("pool", lambda e: e.memset(Sall[:, 0, :], 0.0), w=[Sall])

    def gla(self):
        S, k, d = self.S, self.k, self.d
        self.scan_setup()
        self.gen = "gla"
        S.barrier()
        qT, kT, vtm, ktm, gaT, outb = self.R[0], self.R[1], self.R[2], self.R[3], self.R[4], self.R[5]
        S.dma("sp", qT[0:64, :], d["gqT"], w=[qT])
        S.dma("sp", kT[0:64, :], d["gkT"], w=[kT])
        S.dma("sp", gaT[0:32, :], d["gaT"], w=[gaT])
        S.dma("sp", vtm[:, 0:NKT * 128], d["gv"], w=[vtm])
        S.dma("sp", ktm[:, 0:NKT * 64], d["gk"], w=[ktm])
        kk = ktm.t[:, 0:NKT * 64].rearrange("p (i d) -> p i d", d=64)
        S.op("pool", lambda e: e.memset(gaT[32:33, :], 1.0), w=[gaT])
        wgb = S.sbuf("gla_wgb", [33, 128], BF16)
        S.dma("pool", wgb[:], d["wg"], w=[wgb])
        trib = k["TRI"]
        spb = self.tbf[2].t[:, :].rearrange("p (i c) -> p i c", c=128)
        spt = self.tbf[2]
        E, Fk, Gk = self.ps[4], self.ps[5], self.ps[4]
        self.ds_banks = [(self.ps[1], self.ps[2]), (self.ps[6], self.ps[7])]
        ex = self.tmp[3]
        pend = None
        for z in range(2):
            self.scan_init_state()
            for (c0, n) in self.scan_groups(z):
                nt = n // 128
                t0 = c0 // 128
                gi = self.gi
                qin, kin, kend, Eq = self.qin[gi % 2], self.kin[gi % 2], self.kend[gi % 2], self.Eq[gi % 2]
                for p in range(nt):
                    S.op("pe", lambda e: e.matmul(E[:, p * 128:(p + 1) * 128], gaT[0:33, c0 + p * 128:c0 + (p + 1) * 128],
                                                  wgb[:, :], start=(p == 0), stop=(p == nt - 1), skip_group_check=True),
                         r=[gaT, wgb], w=[E], signal=(p == nt - 1))
                S.op("act", lambda e: e.activation(out=ex[:, :n], in_=E[:, :n], func=AF.Exp, scale=-1.0),
                     r=[E], w=[ex])
                S.op("act", lambda e: e.activation(out=spb[:, 0:nt, :], in_=ex[:, :n].rearrange("p (i c) -> p i c", c=128),
                                                   func=AF.Ln, bias=self.onef[:, 0:1], scale=1.0),
                     r=[ex, self.onef], w=[spt])
                for p in range(nt):
                    S.op("pe", lambda e: e.matmul(Fk[0:64, p * 128:(p + 1) * 128], spb[:, p, z * 64:(z + 1) * 64],
                                                  trib[:, z, :], start=(p == 0), stop=(p == nt - 1),
                                                  skip_group_check=True),
                         r=[spt, trib], w=[Fk], signal=(p == nt - 1))
                S.op("act", lambda e: e.activation(out=Eq[:, :n], in_=Fk[0:64, :n], func=AF.Exp), r=[Fk], w=[Eq])
                S.op("act", lambda e: e.activation(out=self.Ek[:, :n], in_=Fk[0:64, :n], func=AF.Exp, scale=-1.0),
                     r=[Fk], w=[self.Ek])
                S.op("dve", lambda e: e.scalar_tensor_tensor(out=qin[:, :n], in0=qT[0:64, c0:c0 + n], scalar=0.125,
                                                             in1=Eq[:, :n], op0=ALU.mult, op1=ALU.mult),
                     r=[qT, Eq], w=[qin])
                S.op("dve", lambda e: e.tensor_tensor(out=kin[:, :n], in0=kT[0:64, c0:c0 + n], in1=self.Ek[:, :n],
                                                      op=ALU.mult), r=[kT, self.Ek], w=[kin])
                for p in range(nt):
                    S.op("pe", lambda e: e.matmul(Gk[:, p * 64:(p + 1) * 64], trib[:, 2 + z, :],
                                                  spb[:, p, z * 64:(z + 1) * 64], start=(p == 0), stop=(p == nt - 1),
                                                  skip_group_check=True),
                         r=[spt, trib], w=[Gk], signal=(p == nt - 1))
                S.op("act", lambda e: e.activation(out=self.Eend[:, 0:nt, :],
                                                   in_=Gk[:, 0:nt * 64].rearrange("p (i c) -> p i c", c=64),
                                                   func=AF.Exp), r=[Gk], w=[self.Eend])
                S.op("dve", lambda e: e.tensor_tensor(out=kend[:, 0:nt, :], in0=kk[:, t0:t0 + nt, :],
                                                      in1=self.Eend[:, 0:nt, :], op=ALU.mult),
                     r=[ktm, self.Eend], w=[kend])
                col = (lambda cc: cc * 64 + 63) if z == 0 else (lambda cc: cc * 64)
                cur = self.chunk_front(z, c0, n, qin, kin, kend, kend, vtm,
                                       lambda cc, Eq=Eq, col=col: Eq[:, col(cc):col(cc) + 1], [Eq], outb)
                self.chunk_back(pend)
                pend = cur
        self.chunk_back(pend)
        S.dma("sp", d["ogT"], outb[:, :], r=[outb])
        self.gla_out_toks = [outb]

    def ret(self):
        S, k, d = self.S, self.k, self.d
        self.scan_setup()
        self.gen = "ret"
        qT, kT, vtm, outb = self.R[0], self.R[1], self.R[2], self.R[5]
        S.dma("sp", qT[0:64, :], d["rqT"], w=[qT])
        S.dma("sp", kT[0:64, :], d["rkT"], w=[kT])
        S.dma("sp", vtm[:, 0:NKT * 128], d["rv"], w=[vtm])
        rld = S.sbuf("ret_rld", [128, 2], F32)
        S.dma("sp", rld[:], d["rld"], w=[rld])
        nlg = S.sbuf("ret_nlg", [128, 2], F32)
        lg = S.sbuf("ret_lg", [128, 2], F32)
        S.op("act", lambda e: e.activation(out=nlg[:], in_=rld[:], func=AF.Exp), r=[rld], w=[nlg])
        S.op("dve", lambda e: e.tensor_scalar(out=lg[:], in0=nlg[:], scalar1=-1.0, scalar2=None, op0=ALU.mult),
             r=[nlg], w=[lg])
        EqR = self.Eq
        spv = Tile(self.sp.t[0:64, :, :].rearrange("p a b -> p (a b)"), "spv")
        spv.b = self.sp.b
        EkR = [self.Ek, spv]
        EendR = S.sbuf("ret_EendR", [128, 2], F32)
        decR = S.sbuf("ret_decR", [64, 2], F32)
        for z in range(2):
            S.op("act", lambda e: e.activation(out=EqR[z][:], in_=k["IDX"][:, z, :], func=AF.Exp,
                                               scale=lg[0:64, z:z + 1]), r=[k["IDX"], lg], w=[EqR[z]])
            S.op("act", lambda e: e.activation(out=EkR[z][:], in_=k["IDX"][:, z, :], func=AF.Exp,
                                               scale=nlg[0:64, z:z + 1]), r=[k["IDX"], nlg], w=[EkR[z]])
            S.op("act", lambda e: e.activation(out=EendR[:, z:z + 1], in_=k["PIDX"][:, z:z + 1], func=AF.Exp,
                                               scale=lg[:, z:z + 1]), r=[k["PIDX"], lg], w=[EendR])
        S.op("act", lambda e: e.activation(out=decR[:], in_=lg[0:64, :], func=AF.Exp, scale=64.0), r=[lg], w=[decR])
        Hk = self.ps[7]
        self.ds_banks = [(self.ps[1], self.ps[2]), (self.ps[4], self.ps[5])]
        rq_r, rk_r = self.tbf[0], self.tbf[1]
        pend = None
        for z in range(2):
            self.scan_init_state()
            for (c0, n) in self.scan_groups(z):
                nt = n // 128
                gi = self.gi
                qin, kin, kend = self.qin[gi % 2], self.kin[gi % 2], self.kend[gi % 2]
                if z == 0:
                    if c0 < T:
                        self.rope_block(qT, qT, c0, n, 64, src_c0=c0)
                        self.rope_block(kT, kT, c0, n, 64, scale=0.125, src_c0=c0)
                    else:
                        S.op("act", lambda e: e.activation(out=kT[0:64, c0:c0 + n], in_=kT[0:64, c0:c0 + n],
                                                           func=AF.Copy, scale=0.125), r=[kT], w=[kT])
                S.op("dve", lambda e: e.tensor_tensor(out=qin[:, :n], in0=qT[0:64, c0:c0 + n], in1=EqR[z][:, :n],
                                                      op=ALU.mult), r=[qT, EqR[z]], w=[qin])
                S.op("dve", lambda e: e.tensor_tensor(out=kin[:, :n], in0=kT[0:64, c0:c0 + n], in1=EkR[z][:, :n],
                                                      op=ALU.mult), r=[kT, EkR[z]], w=[kin])
                for p in range(nt):
                    S.op("pe", lambda e: e.matmul(Hk[:, p * 64:(p + 1) * 64], kT[0:64, c0 + p * 128:c0 + (p + 1) * 128],
                                                  k["IDN"][0:64, 0:64], start=(p == 0), stop=(p == nt - 1),
                                                  skip_group_check=True),
                         r=[kT, k["IDN"]], w=[Hk], signal=(p == nt - 1))
                S.op("dve", lambda e: e.tensor_scalar(out=kend[:, 0:nt, :],
                                                      in0=Hk[:, 0:nt * 64].rearrange("p (i c) -> p i c", c=64),
                                                      scalar1=EendR[:, z:z + 1], scalar2=None, op0=ALU.mult),
                     r=[Hk, EendR], w=[kend])
                cur = self.chunk_front(z, c0, n, qin, kin, kend, kend, vtm, lambda cc, z=z: decR[:, z:z + 1],
                                       [decR], outb)
                self.chunk_back(pend)
                pend = cur
        self.chunk_back(pend)
        S.dma("sp", d["orT"], outb[:, :], r=[outb])
        self.ret_out_toks = [outb]


def build_p2(which=("attn", "gla", "ret"), stage=9, nqb=None):
    P = P2()
    outs = []
    if "attn" in which:
        P.attention(stage, nqb)
        outs += P.att_out_toks
    if "gla" in which:
        P.gla()
        outs += P.gla_out_toks
    if "ret" in which:
        P.ret()
        outs += P.ret_out_toks
    P.S.finish(outs, "sp")
    return P.nc, P.S


def tm_layout(a):
    n = a.shape[1]
    return np.ascontiguousarray(a.reshape(-1, 128, n).transpose(1, 0, 2).reshape(128, -1))


_PROG = {}


def _prog(name):
    if name not in _PROG:
        if name == "p1":
            _PROG[name] = build_p1()[0]
        elif name == "p2":
            _PROG[name] = build_p2()[0]
        else:
            _PROG[name] = build_p3()[0]
    return _PROG[name]


def _run(name, maps):
    res = run_bass_kernel_spmd(_prog(name), maps, core_ids=list(range(NCORE)))
    return res.results


def p2_inmaps(inp, l, zfm_b, ztm_b, consts, cos, sin):
    maps = []
    gw, gb_ = inp["gla_gate_w"][l], inp["gla_gate_b"][l]
    for c in range(NCORE):
        b, hh = c // 4, c % 4
        kvh = hh // 2
        zf, zt = zfm_b[b], ztm_b[b]

        def fm(name, o, n):
            return np.ascontiguousarray(zf[FMO[name] + o:FMO[name] + o + n])

        def tm(name, o, n):
            return tm_layout(zt[:, TMO[name] + o:TMO[name] + o + n])
        ak = fm("ak", kvh * 64, 64)
        anp = np.zeros((128, 4), np.float32)
        anp[:, 0] = np.tile(inp["attn_q_norm_g"][l], 2)
        anp[:, 1] = np.tile(inp["attn_k_norm_g"][l], 2)
        anp[:, 2] = inp["attn_sink"][l][2 * hh]
        anp[:, 3] = inp["attn_sink"][l][2 * hh + 1]
        wg = np.zeros((33, 128), np.float32)
        wg[0:16, 0:64] = gw[0][:, hh * 64:(hh + 1) * 64]
        wg[16:32, 64:128] = gw[1][:, hh * 64:(hh + 1) * 64]
        wg[32, 0:64] = gb_[0, hh * 64:(hh + 1) * 64]
        wg[32, 64:128] = gb_[1, hh * 64:(hh + 1) * 64]
        m = {
            "aqT": fm("aq", 2 * hh * 64, 128), "akT": np.ascontiguousarray(np.concatenate([ak, ak], 0)),
            "av": tm("av", kvh * 64, 64), "cosT": cos, "sinT": sin, "anp": anp,
            "gqT": fm("gq", hh * 64, 64), "gkT": fm("gk", hh * 64, 64), "gaT": fm("ga", 0, 32),
            "gk": tm("gk", hh * 64, 64), "gv": tm("gv", hh * 128, 128), "wg": wg,
            "rqT": fm("rq", hh * 64, 64), "rkT": fm("rk", hh * 64, 64), "rv": tm("rv", hh * 128, 128),
            "rld": np.ascontiguousarray(np.broadcast_to(inp["ret_log_decay"][l][:, hh], (128, 2))).astype(np.float32),
        }
        m.update(consts)
        maps.append(m)
    return maps


def kernel(**inp):
    inp = {k: np.asarray(v) for k, v in inp.items()}
    x = inp["x"].astype(np.float32)
    ctx = inp["ctx"].astype(np.float32)
    consts = p2_consts_np()
    cos, sin = rope_tables_np()
    for l in range(DEPTH):
        r1 = _run("p1", p1_inmaps(inp, l, x, ctx))
        zfm_b, ztm_b = [], []
        for b in range(B):
            zf = [np.asarray(r1[b * 4 + s]["zfm"]) for s in range(4)]
            zt = [np.asarray(r1[b * 4 + s]["ztm"]) for s in range(4)]
            zfm_b.append(np.concatenate([z[:, :TS] for z in zf] + [z[:, TS:] for z in zf], axis=1))
            ztm_b.append(np.concatenate([z[:TS] for z in zt] + [z[TS:] for z in zt], axis=0))
        modT_list = [np.asarray(r1[c]["modT"]) for c in range(NCORE)]
        del r1
        r2 = _run("p2", p2_inmaps(inp, l, zfm_b, ztm_b, consts, cos, sin))
        o_b = []
        for b in range(B):
            oa = np.concatenate([np.asarray(r2[b * 4 + hh]["oaT"]) for hh in range(4)], axis=0)
            og = np.concatenate([np.asarray(r2[b * 4 + hh]["ogT"]) for hh in range(4)], axis=0)
            orr = np.concatenate([np.asarray(r2[b * 4 + hh]["orT"]) for hh in range(4)], axis=0)
            o_b.append(np.ascontiguousarray(np.concatenate([oa, og, orr], axis=0).T))
        del r2
        o_all = np.stack(o_b, 0)
        g_all = np.stack([np.ascontiguousarray(
            np.concatenate([zfm_b[b][FMO["gr"]:FMO["gr"] + 512], zfm_b[b][FMO["rg"]:FMO["rg"] + 512],
                            zfm_b[b][FMO["mg"]:FMO["mg"] + 3072]], axis=0).T) for b in range(B)], 0)
        del zfm_b, ztm_b
        r3 = _run("p3", p3_inmaps(inp, l, x, ctx, modT_list, o_all[:, :T], o_all[:, T:], g_all[:, :T], g_all[:, T:]))
        xn = np.empty_like(x)
        cn = np.empty_like(ctx)
        for c in range(NCORE):
            b, seg = c // 4, c % 4
            xo = np.asarray(r3[c]["xo"])
            xn[b, seg * TS:(seg + 1) * TS] = xo[:, :TS].T
            cn[b, seg * LS:(seg + 1) * LS] = xo[:, TS:].T
        x, ctx = xn, cn
    return x.astype(np.float32)
```

```python
import numpy as np
import ml_dtypes
import concourse.bass as bass
import concourse.mybir as mybir
from concourse.bass_utils import run_bass_kernel_spmd

F32 = mybir.dt.float32
BF16 = mybir.dt.bfloat16
AF = mybir.ActivationFunctionType
ALU = mybir.AluOpType
AX = mybir.AxisListType

D = 1024
B = 2
T = 8192
L = 256
DEPTH = 4
NCORE = 8
TS = T // 4
LS = L // 4
NT = TS + LS
TB = T + L
D_FF = 2816
D_IN = 6944
EPS = 1e-6
GRID_W = 64

OFF = {}
_o = 0
for _n, _s in (("aq", 512), ("ak", 128), ("av", 128), ("gq", 256), ("gk", 256), ("gv", 512), ("gr", 512),
               ("ga", 32), ("rq", 256), ("rk", 256), ("rv", 512), ("rg", 512), ("mg", 3072)):
    OFF[_n] = (_o, _s)
    _o += _s
assert _o == D_IN


class Buf:
    __slots__ = ("name", "w", "r")

    def __init__(self, name):
        self.name = name
        self.w = None
        self.r = {}


class Tile:
    def __init__(self, t, name):
        self.t = t
        self.b = Buf(name)

    def __getitem__(self, idx):
        return self.t[idx]


class Sched:
    def __init__(self, nc):
        self.nc = nc
        self.eng = {"pe": nc.tensor, "act": nc.scalar, "dve": nc.vector, "pool": nc.gpsimd, "sp": nc.sync}
        self.sem = {}
        self.cnt = {}
        self.seen = {k: {} for k in self.eng}
        self.nwait = 0
        self.nins = 0
        for k in ("pe", "act", "dve", "pool"):
            self.sem[k] = nc.alloc_semaphore("s_" + k)
            self.cnt[k] = 0
        self._uid = 0

    def sbuf(self, name, shape, dtype):
        return Tile(self.nc.alloc_sbuf_tensor("sb_" + name, list(shape), dtype), name)

    def psum(self, name, shape, dtype=F32):
        return Tile(self.nc.alloc_psum_tensor("pp_" + name, list(shape), dtype), name)

    @staticmethod
    def _b(x):
        return x.b if isinstance(x, Tile) else x

    def _deps(self, ek, reads, writes):
        deps = {}

        def add(m):
            if m is None:
                return
            k, v = m
            if deps.get(k, 0) < v:
                deps[k] = v
        for b in reads:
            add(self._b(b).w)
        for b in writes:
            b = self._b(b)
            add(b.w)
            for k, v in b.r.items():
                add((k, v))
        if ek == "pe":
            deps.pop("pe", None)
        e = self.eng[ek]
        seen = self.seen[ek]
        for k, v in deps.items():
            if seen.get(k, 0) < v:
                e.wait_ge(self.sem[k], v)
                seen[k] = v
                self.nwait += 1

    def _mark(self, mark, reads, writes):
        k, v = mark
        for b in writes:
            b = self._b(b)
            b.w = mark
            b.r = {}
        for b in reads:
            b = self._b(b)
            if b.r.get(k, 0) < v:
                b.r[k] = v

    def op(self, ek, fn, r=(), w=(), signal=True):
        self._deps(ek, r, w)
        ins = fn(self.eng[ek])
        self.nins += 1
        if signal:
            self.cnt[ek] += 1
            ins.then_inc(self.sem[ek], 1)
            mark = (ek, self.cnt[ek])
        else:
            assert ek == "pe"
            mark = (ek, self.cnt[ek] + 1)
        self._mark(mark, r, w)
        return ins

    def dma(self, qk, out, in_, r=(), w=(), key=None):
        self._deps(qk, r, w)
        tok = self._b(w[0]) if len(w) else self._b(r[0])
        k = ("dma", key or tok.name)
        if k not in self.sem:
            self._uid += 1
            self.sem[k] = self.nc.alloc_semaphore("sd%d" % self._uid)
            self.cnt[k] = 0
        self.cnt[k] += 16
        self.eng[qk].dma_start(out=out, in_=in_).then_inc(self.sem[k], 16)
        self.nins += 1
        self._mark((k, self.cnt[k]), r, w)

    def barrier(self):
        for ek, e in self.eng.items():
            for k, v in self.cnt.items():
                if v > 0 and self.seen[ek].get(k, 0) < v:
                    e.wait_ge(self.sem[k], v)
                    self.seen[ek][k] = v

    def finish(self, toks, ek="sp"):
        self._deps(ek, [], toks)
        e = self.eng[ek]
        for k, v in self.cnt.items():
            if v > 0 and self.seen[ek].get(k, 0) < v:
                e.wait_ge(self.sem[k], v)
                self.seen[ek][k] = v


def token_blocks():
    blks = [(i * 512, 512, 0) for i in range(TS // 512)]
    blks.append((TS, LS, 1))
    return blks


def token_tiles():
    tl = [(i * 128, 128) for i in range(TS // 128)]
    tl.append((TS, LS))
    return tl


FM_GROUPS = [
    (0, 512), (512, 128), (768, 512), (1792, 512), (2304, 32), (2336, 512), (3360, 512),
    (3872, 512), (4384, 512), (4896, 512), (5408, 512), (5920, 512), (6432, 512)]
FM_ROWS = sum(n for _, n in FM_GROUPS)
FMO = {"aq": 0, "ak": 512, "gq": 640, "gk": 896, "gr": 1152, "ga": 1664, "rq": 1696, "rk": 1952,
       "rg": 2208, "mg": 2720}
TM_GROUPS = [(640, 128), (1280, 512), (2848, 512), (1024, 256)]
TM_COLS = sum(n for _, n in TM_GROUPS)
TMO = {"av": 0, "gv": 128, "rv": 640, "gk": 1152, "rk": 1408}


class Consts:
    def __init__(self, S):
        self.ones_bf = S.sbuf("ones_bf", [128, 128], BF16)
        S.op("pool", lambda e: e.memset(self.ones_bf[:], 1.0), w=[self.ones_bf])
        self.deps = S.sbuf("c_deps", [128, 1], F32)
        S.op("pool", lambda e: e.memset(self.deps[:], float(D * EPS)), w=[self.deps])


def emit_mod(S, cs_d, ada_w_d, ada_b_d, modT, ps):
    nc = S.nc
    cs = S.sbuf("cs", [128, 8, 2], F32)
    sg = S.sbuf("cs_sg", [128, 8, 2], F32)
    ab = S.sbuf("ada_b", [128, 48], F32)
    S.dma("sp", cs[:], cs_d, w=[cs])
    S.dma("sp", ab[:], ada_b_d, w=[ab])
    S.op("act", lambda e: e.activation(out=sg[:], in_=cs[:], func=AF.Sigmoid), r=[cs], w=[sg])
    csb = S.sbuf("cs_bf", [128, 8, 2], BF16)
    S.op("dve", lambda e: e.tensor_tensor(out=csb[:], in0=cs[:], in1=sg[:], op=ALU.mult), r=[sg, cs], w=[csb])
    GW = 768
    wb = [S.sbuf("adaw%d" % i, [128, 8, GW], BF16) for i in range(2)]
    for g in range(6144 // GW):
        wt = wb[g % 2]
        S.dma("pool", wt[:], ada_w_d[:, g * GW:(g + 1) * GW].rearrange("(c p) n -> p c n", p=128), w=[wt])
        for fc in range(GW // 128):
            ch = g * (GW // 128) + fc
            for kc in range(8):
                S.op("pe", lambda e: e.matmul(ps[:, ch * 2:ch * 2 + 2], wt[:, kc, fc * 128:(fc + 1) * 128],
                                              csb[:, kc, :], start=(kc == 0), stop=(kc == 7)),
                     r=[wt, csb], w=[ps], signal=(kc == 7))
    for j in range(2):
        S.op("dve", lambda e: e.tensor_tensor(out=modT[:, :, j], in0=ps[:, j:96:2], in1=ab[:], op=ALU.add),
             r=[ps, ab], w=[modT])


def emit_norm_mod(S, C, xT, hT, htoks, gmod, modT, part_sh, ps_ss, name, blocks=None, ps2=None, bufs=None):
    blocks = blocks if blocks is not None else token_blocks()
    if bufs is None:
        sqs = [S.sbuf(name + "_sq%d" % i, [128, 8, 512], BF16) for i in range(1)]
        rstds = [S.sbuf(name + "_rstd%d" % i, [128, 512], F32) for i in range(2)]
        tmp = [S.sbuf(name + "_tmp%d" % i, [128, 512], F32) for i in range(2)]
    else:
        sqs, rstds, tmp = bufs
    pss = [ps_ss, ps2 if ps2 is not None else ps_ss]

    def stats(bi):
        t0, nb, j = blocks[bi]
        sq, rstd, ps = sqs[0], rstds[bi % 2], pss[bi % 2]
        S.op("act", lambda e: e.activation(out=sq[:, :, :nb], in_=xT[:, :, t0:t0 + nb], func=AF.Square),
             r=[xT], w=[sq])
        for c in range(8):
            S.op("pe", lambda e: e.matmul(ps[:, :nb], C.ones_bf[:], sq[:, c, :nb], start=(c == 0), stop=(c == 7)),
                 r=[C.ones_bf, sq], w=[ps], signal=(c == 7))
        S.op("act", lambda e: e.activation(out=rstd[:, :nb], in_=ps[:, :nb], func=AF.Ln,
                                           bias=C.deps[:, 0:1], scale=1.0), r=[ps, C.deps], w=[rstd])
        S.op("act", lambda e: e.activation(out=rstd[:, :nb], in_=rstd[:, :nb], func=AF.Exp, scale=-0.5),
             r=[rstd], w=[rstd])

    def apply(bi):
        t0, nb, j = blocks[bi]
        rstd = rstds[bi % 2]
        for c in range(8):
            tt = tmp[c % 2]
            S.op("dve", lambda e: e.scalar_tensor_tensor(out=tt[:, :nb], in0=xT[:, c, t0:t0 + nb],
                                                         scalar=gmod[:, c, j:j + 1], in1=rstd[:, :nb],
                                                         op0=ALU.mult, op1=ALU.mult),
                 r=[xT, gmod, rstd], w=[tt])
            S.op("act", lambda e: e.activation(out=hT[:, c, t0:t0 + nb], in_=tt[:, :nb], func=AF.Identity,
                                               bias=modT[:, part_sh * 8 + c, j:j + 1], scale=1.0),
                 r=[tt, modT], w=[htoks[bi] if isinstance(htoks, list) else htoks])
    stats(0)
    for bi in range(len(blocks)):
        if bi + 1 < len(blocks):
            stats(bi + 1)
        apply(bi)


def emit_gmod(S, gmod, g_d, modT, part_sc, name):
    g = S.sbuf(name + "_g", [128, 8], F32)
    S.dma("sp", g[:], g_d, w=[g])
    for j in range(2):
        S.op("dve", lambda e: e.tensor_scalar(out=gmod[:, :, j], in0=modT[:, part_sc * 8:(part_sc + 1) * 8, j],
                                              scalar1=1.0, scalar2=float(np.sqrt(D)), op0=ALU.add, op1=ALU.mult),
             r=[modT], w=[gmod])
        S.op("dve", lambda e: e.tensor_tensor(out=gmod[:, :, j], in0=gmod[:, :, j], in1=g[:], op=ALU.mult),
             r=[gmod, g], w=[gmod])


def build_p1():
    nc = bass.Bass("TRN2", target_bir_lowering=False)
    S = Sched(nc)
    xT_d = nc.dram_tensor("xT", [D, NT], F32, kind="ExternalInput").ap()
    cs_d = nc.dram_tensor("cs", [128, 8, 2], F32, kind="ExternalInput").ap()
    ada_w_d = nc.dram_tensor("ada_w", [D, 6 * D], F32, kind="ExternalInput").ap()
    ada_b_d = nc.dram_tensor("ada_b", [128, 48], F32, kind="ExternalInput").ap()
    g1_d = nc.dram_tensor("norm1_g", [128, 8], F32, kind="ExternalInput").ap()
    w_in_d = nc.dram_tensor("w_in", [D, D_IN], F32, kind="ExternalInput").ap()
    zfm_d = nc.dram_tensor("zfm", [FM_ROWS, NT], BF16, kind="ExternalOutput").ap()
    ztm_d = nc.dram_tensor("ztm", [NT, TM_COLS], BF16, kind="ExternalOutput").ap()
    mod_d = nc.dram_tensor("modT", [128, 96], F32, kind="ExternalOutput").ap()

    C = Consts(S)
    xT = S.sbuf("xT", [128, 8, NT], F32)
    hT = S.sbuf("hT", [128, 8, NT], BF16)
    htoks = [Buf("hT%d" % i) for i in range(len(token_blocks()))]
    modT = S.sbuf("modT", [128, 48, 2], F32)
    gmod = S.sbuf("gmod1", [128, 8, 2], F32)
    ps_mod = S.psum("ps_mod", [128, 512])
    ps_ss = S.psum("ps_ss", [128, 512])
    ps_o = [S.psum("ps_o%d" % i, [128, 512]) for i in range(4)]

    S.dma("sp", xT[:], xT_d.rearrange("(c p) t -> p c t", p=128), w=[xT])
    emit_mod(S, cs_d, ada_w_d, ada_b_d, modT, ps_mod)
    S.dma("sp", mod_d.rearrange("p (c j) -> p c j", j=2), modT[:], r=[modT])
    emit_gmod(S, gmod, g1_d, modT, 1, "n1")
    emit_norm_mod(S, C, xT, hT, htoks, gmod, modT, 0, ps_ss, "n1", ps2=ps_mod)

    wbufs = [S.sbuf("w%d" % i, [128, 8, 512], BF16) for i in range(3)]
    stg_fm = [S.sbuf("stgfm%d" % i, [128, NT], BF16) for i in range(2)]
    stg_tm = [S.sbuf("stgtm%d" % i, [128, 17, 512], BF16) for i in range(2)]
    gi = 0
    pi = 0
    ei = 0
    si = 0
    row0 = 0
    blks = token_blocks()
    for (c0, n) in FM_GROUPS:
        wt = wbufs[gi % 3]
        gi += 1
        S.dma("pool", wt[:, :, :n], w_in_d[:, c0:c0 + n].rearrange("(c p) n -> p c n", p=128), w=[wt])
        for f0 in range(0, n, 128):
            m = min(128, n - f0)
            stg = stg_fm[si % 2]
            si += 1
            for bi, (t0, nb, j) in enumerate(blks):
                ps = ps_o[pi % 4]
                pi += 1
                for kc in range(8):
                    S.op("pe", lambda e: e.matmul(ps[:m, :nb], wt[:, kc, f0:f0 + m], hT[:, kc, t0:t0 + nb],
                                                  start=(kc == 0), stop=(kc == 7)),
                         r=[wt, htoks[bi]], w=[ps], signal=(kc == 7))
                if ei % 2 == 0:
                    S.op("act", lambda e: e.activation(out=stg[:m, t0:t0 + nb], in_=ps[:m, :nb], func=AF.Copy),
                         r=[ps], w=[stg])
                else:
                    S.op("dve", lambda e: e.tensor_copy(out=stg[:m, t0:t0 + nb], in_=ps[:m, :nb]), r=[ps], w=[stg])
                ei += 1
            S.dma("sp", zfm_d[row0 + f0:row0 + f0 + m, :], stg[:m, :], r=[stg])
        row0 += n
    col0 = 0
    tiles = token_tiles()
    for gidx, (c0, n) in enumerate(TM_GROUPS):
        wt = wbufs[gi % 3]
        gi += 1
        S.dma("pool", wt[:, :, :n], w_in_d[:, c0:c0 + n].rearrange("(c p) n -> p c n", p=128), w=[wt])
        stg = stg_tm[gidx % 2]
        for ti, (t0, nt) in enumerate(tiles):
            ps = ps_o[pi % 4]
            pi += 1
            bi = min(t0 // 512, len(blks) - 1)
            for kc in range(8):
                S.op("pe", lambda e: e.matmul(ps[:nt, :n], hT[:, kc, t0:t0 + nt], wt[:, kc, :n],
                                              start=(kc == 0), stop=(kc == 7)),
                     r=[wt, htoks[bi]], w=[ps], signal=(kc == 7))
            if ei % 2 == 0:
                S.op("act", lambda e: e.activation(out=stg[:nt, ti, :n], in_=ps[:nt, :n], func=AF.Copy),
                     r=[ps], w=[stg])
            else:
                S.op("dve", lambda e: e.tensor_copy(out=stg[:nt, ti, :n], in_=ps[:nt, :n]), r=[ps], w=[stg])
            ei += 1
        S.dma("sp", ztm_d[0:TS, col0:col0 + n].rearrange("(i p) n -> p i n", p=128), stg[:, 0:16, :n], r=[stg])
        S.dma("sp", ztm_d[TS:NT, col0:col0 + n], stg[:LS, 16, :n], r=[stg])
        col0 += n
    S.finish(stg_fm + stg_tm + [modT], "sp")
    return nc, S


def pc(v, nchunk):
    return np.ascontiguousarray(np.asarray(v).reshape(nchunk, 128).T)


def p1_inmaps(inp, l, x, ctx):
    maps = []
    for c in range(NCORE):
        b, seg = c // 4, c % 4
        xt = np.concatenate([x[b, seg * TS:(seg + 1) * TS], ctx[b, seg * LS:(seg + 1) * LS]], axis=0).T
        cs = np.stack([inp["c"][b], inp["c_ctx"]], axis=1)
        cs = np.ascontiguousarray(cs.reshape(8, 128, 2).transpose(1, 0, 2))
        maps.append({
            "xT": np.ascontiguousarray(xt, dtype=np.float32),
            "cs": cs.astype(np.float32),
            "ada_w": inp["ada_w"][l],
            "ada_b": pc(inp["ada_b"][l], 48),
            "norm1_g": pc(inp["norm1_g"][l], 8),
            "w_in": inp["w_in"][l],
        })
    return maps


NL3 = TS + 2
NC3 = LS + 2
NT3 = NL3 + NC3
BLK3 = [(0, 512, 0), (512, 512, 0), (1024, 512, 0), (1536, 512, 0), (2048, 2, 0), (NL3, NC3, 1)]
FFN_GROUPS = [[(1, 352, 0), (353, 352, 0)], [(705, 352, 0), (1057, 352, 0)],
              [(1409, 352, 0), (1761, 288, 0), (NL3 + 1, LS, 1)]]
GW3 = 770


def build_p3():
    nc = bass.Bass("TRN2", target_bir_lowering=False)
    S = Sched(nc)
    xT_d = nc.dram_tensor("xT", [D, NT3], F32, kind="ExternalInput").ap()
    mod_d = nc.dram_tensor("modT", [128, 96], F32, kind="ExternalInput").ap()
    g2_d = nc.dram_tensor("norm2_g", [128, 8], F32, kind="ExternalInput").ap()
    og_d = nc.dram_tensor("onorm_g", [128, 2], F32, kind="ExternalInput").ap()
    hm_d = nc.dram_tensor("hmask", [128, 4], F32, kind="ExternalInput").ap()
    oT_d = nc.dram_tensor("oT", [3 * 512, NT3], BF16, kind="ExternalInput").ap()
    gT_d = nc.dram_tensor("gT", [4096, NT3], BF16, kind="ExternalInput").ap()
    wb_d = nc.dram_tensor("w_branch", [3, 512, D], F32, kind="ExternalInput").ap()
    wo_d = nc.dram_tensor("w_out", [D, D], F32, kind="ExternalInput").ap()
    up_d = nc.dram_tensor("ffn_up", [D, 2 * D_FF], F32, kind="ExternalInput").ap()
    cw_d = nc.dram_tensor("conv_w", [128, 3, 44], F32, kind="ExternalInput").ap()
    cb_d = nc.dram_tensor("conv_b", [128, 44], F32, kind="ExternalInput").ap()
    dn_d = nc.dram_tensor("ffn_down", [D_FF, D], F32, kind="ExternalInput").ap()
    xo_d = nc.dram_tensor("xo", [D, NT], F32, kind="ExternalOutput").ap()
    xm_d = nc.dram_tensor("xmid_scratch", [D, NT3], F32, kind="Internal").ap()

    C = Consts(S)
    NX = 8 * NT3 * 2
    big = S.sbuf("big", [128, NX + 8 * NT3], BF16)
    xT = Tile(big.t[:, 0:NX].bitcast(F32).rearrange("p (c t) -> p c t", c=8), "xTv")
    mTt = Tile(big.t[:, NX:NX + 8 * NT3].rearrange("p (c t) -> p c t", c=8), "mTv")
    mT = mTt.t
    arenaA = S.sbuf("arenaA", [128, 12 * NT3], BF16)
    yb = arenaA.t[:, :].rearrange("p (c t) -> p c t", c=12)
    h2T = arenaA.t[:, 0:8 * NT3].rearrange("p (c t) -> p c t", c=8)
    scr = S.sbuf("scr", [128, 3 * NT3], BF16)
    mgt = [Tile(scr.t[:, i * NT3:(i + 1) * NT3], "mgt%d" % i) for i in range(3)]
    gate = mgt[0:2]
    modT = S.sbuf("modT", [128, 48, 2], F32)
    gmod = S.sbuf("gmod2", [128, 8, 2], F32)
    ong = S.sbuf("ong", [128, 2], F32)
    hm = S.sbuf("hm", [128, 4], F32)
    cw = S.sbuf("cw", [128, 3, 44], F32)
    cb = S.sbuf("cb", [128, 44], F32)
    c128 = S.sbuf("c_128eps", [128, 1], F32)
    S.op("pool", lambda e: e.memset(c128[:], float(128 * EPS)), w=[c128])
    ps = [S.psum("ps%d" % i, [128, 512]) for i in range(8)]

    S.dma("sp", xT[:], xT_d.rearrange("(c p) t -> p c t", p=128), w=[xT])
    S.dma("sp", modT[:], mod_d.rearrange("p (c j) -> p c j", j=2), w=[modT])
    S.dma("sp", ong[:], og_d, w=[ong])
    S.dma("sp", hm[:], hm_d, w=[hm])
    S.dma("sp", cw[:], cw_d, w=[cw])
    S.dma("sp", cb[:], cb_d, w=[cb])
    S.op("dve", lambda e: e.tensor_scalar(out=ong[:], in0=ong[:], scalar1=float(np.sqrt(128.0)), scalar2=None,
                                          op0=ALU.mult), r=[ong], w=[ong])
    S.dma("sp", yb, oT_d.rearrange("(c p) t -> p c t", p=128), w=[arenaA])
    sqs = [S.sbuf("sq3_%d" % i, [128, 512], BF16) for i in range(2)]
    rstds = [S.sbuf("rstd3_%d" % i, [128, 512], F32) for i in range(2)]
    tmpf = [S.sbuf("tmpf%d" % i, [128, 512], F32) for i in range(2)]
    sgt = [S.sbuf("sgt%d" % i, [128, 512], F32) for i in range(2)]
    ybt = [Buf("yb%d" % i) for i in range(12)]
    pi = 0
    its = []
    for z in (1, 2):
        for hc in range(4):
            for (t0, nb, j) in BLK3:
                its.append((z, hc, t0, nb))
    gts = {}

    def s1_front(i):
        z, hc, t0, nb = its[i]
        ci = z * 4 + hc
        if (z, hc) not in gts:
            gt = gate[(z * 4 + hc) % 2]
            S.dma("sp", gt[:], gT_d[(z - 1) * 512 + hc * 128:(z - 1) * 512 + (hc + 1) * 128, :], w=[gt])
            S.op("act", lambda e: e.activation(out=gt[:], in_=gt[:], func=AF.Silu), r=[gt], w=[gt])
            gts[(z, hc)] = gt
        p_ = ps[i % 4]
        sq, rstd = sqs[i % 2], rstds[i % 2]
        S.op("pool", lambda e: e.tensor_tensor(out=sq[:, :nb], in0=yb[:, ci, t0:t0 + nb],
                                               in1=yb[:, ci, t0:t0 + nb], op=ALU.mult),
             r=[arenaA, ybt[ci]], w=[sq])
        S.op("pe", lambda e: e.matmul(p_[:, :nb], C.ones_bf[:], sq[:, :nb], start=True, stop=True),
             r=[sq, C.ones_bf], w=[p_])
        S.op("act", lambda e: e.activation(out=rstd[:, :nb], in_=p_[:, :nb], func=AF.Ln,
                                           bias=c128[:, 0:1], scale=1.0), r=[p_, c128], w=[rstd])
        S.op("act", lambda e: e.activation(out=rstd[:, :nb], in_=rstd[:, :nb], func=AF.Exp, scale=-0.5),
             r=[rstd], w=[rstd])

    def s1_back(i):
        z, hc, t0, nb = its[i]
        ci = z * 4 + hc
        gt = gts[(z, hc)]
        rstd = rstds[i % 2]
        tf = tmpf[i % 2]
        S.op("dve", lambda e: e.scalar_tensor_tensor(out=tf[:, :nb], in0=yb[:, ci, t0:t0 + nb],
                                                     scalar=ong[:, z - 1:z], in1=rstd[:, :nb],
                                                     op0=ALU.mult, op1=ALU.mult),
             r=[arenaA, ybt[ci], ong, rstd], w=[tf])
        S.op("dve", lambda e: e.tensor_tensor(out=yb[:, ci, t0:t0 + nb], in0=tf[:, :nb],
                                              in1=gt[:, t0:t0 + nb], op=ALU.mult),
             r=[tf, gt, arenaA], w=[ybt[ci]])
    s1_front(0)
    for i in range(len(its)):
        if i + 1 < len(its):
            s1_front(i + 1)
        s1_back(i)
    pi = 4
    ybr = [ybt[i] for i in range(12)]
    wbr = [S.sbuf("wbr%d" % i, [128, 4, 128], BF16) for i in range(6)]
    ub = [S.sbuf("ub%d" % i, [128, 512], F32) for i in range(2)]
    ysb = [Tile(sqs[i].t[:, :], "ysb%d" % i) for i in range(2)]
    for i in range(2):
        ubv = ub[i].t[:, :].bitcast(BF16)
        ysb += [Tile(ubv[:, 0:512], "ysb%d" % (2 + 2 * i)), Tile(ubv[:, 512:1024], "ysb%d" % (3 + 2 * i))]
    bcount = 0
    wi = 0
    for oc in range(8):
        wts = []
        for z in range(3):
            wt = wbr[wi % 6]
            wi += 1
            mg = mgt[z]
            S.dma("pool", wt[:], wb_d[z, :, oc * 128:(oc + 1) * 128].rearrange("(c p) n -> p c n", p=128), w=[wt])
            S.dma("sp", mg[:], gT_d[1024 + z * 1024 + oc * 128:1024 + z * 1024 + (oc + 1) * 128, :], w=[mg])
            S.op("act", lambda e: e.activation(out=mg[:], in_=mg[:], func=AF.Sigmoid), r=[mg], w=[mg])
            wts.append(wt)
        for (t0, nb, j) in BLK3:
            pz = []
            for z in range(3):
                p_ = ps[2 + pi % 6]
                pi += 1
                for kc in range(4):
                    S.op("pe", lambda e: e.matmul(p_[:, :nb], wts[z][:, kc, :], yb[:, z * 4 + kc, t0:t0 + nb],
                                                  start=(kc == 0), stop=(kc == 3)),
                         r=[wts[z], arenaA] + ybr[z * 4:z * 4 + 4], w=[p_], signal=(kc == 3))
                pz.append(p_)
            ev = []
            for z in range(3):
                e_ = ysb[(bcount * 3 + z) % 6]
                S.op("act", lambda e: e.activation(out=e_[:, :nb], in_=pz[z][:, :nb], func=AF.Copy), r=[pz[z]], w=[e_])
                ev.append(e_)
            bcount += 1
            S.op("dve", lambda e: e.tensor_tensor(out=ev[0][:, :nb], in0=ev[0][:, :nb], in1=mgt[0][:, t0:t0 + nb],
                                                  op=ALU.mult), r=[ev[0], mgt[0]], w=[ev[0]])
            S.op("dve", lambda e: e.tensor_tensor(out=ev[1][:, :nb], in0=ev[1][:, :nb], in1=mgt[1][:, t0:t0 + nb],
                                                  op=ALU.mult), r=[ev[1], mgt[1]], w=[ev[1]])
            S.op("dve", lambda e: e.tensor_tensor(out=ev[2][:, :nb], in0=ev[2][:, :nb], in1=mgt[2][:, t0:t0 + nb],
                                                  op=ALU.mult), r=[ev[2], mgt[2]], w=[ev[2]])
            S.op("dve", lambda e: e.tensor_tensor(out=ev[0][:, :nb], in0=ev[0][:, :nb], in1=ev[1][:, :nb], op=ALU.add),
                 r=[ev[0], ev[1]], w=[ev[0]])
            S.op("dve", lambda e: e.tensor_tensor(out=mT[:, oc, t0:t0 + nb], in0=ev[0][:, :nb], in1=ev[2][:, :nb],
                                                  op=ALU.add), r=[ev[0], ev[2]], w=[mTt])
    wo = [S.sbuf("wo%d" % i, [128, 8, 128], BF16) for i in range(2)]
    for oc in range(8):
        wt = wo[oc % 2]
        S.dma("pool", wt[:], wo_d[:, oc * 128:(oc + 1) * 128].rearrange("(c p) n -> p c n", p=128), w=[wt])
        for (t0, nb, j) in BLK3:
            p_ = ps[2 + pi % 4]
            pi += 1
            for kc in range(8):
                S.op("pe", lambda e: e.matmul(p_[:, :nb], wt[:, kc, :], mT[:, kc, t0:t0 + nb],
                                              start=(kc == 0), stop=(kc == 7)),
                     r=[wt, mTt], w=[p_], signal=(kc == 7))
            S.op("dve", lambda e: e.scalar_tensor_tensor(out=xT[:, oc, t0:t0 + nb], in0=p_[:, :nb],
                                                         scalar=modT[:, 16 + oc, j:j + 1], in1=xT[:, oc, t0:t0 + nb],
                                                         op0=ALU.mult, op1=ALU.add),
                 r=[p_, modT, xT], w=[xT])
    S.dma("sp", xm_d.rearrange("(c p) t -> p c t", p=128), xT[:], r=[xT])
    emit_gmod(S, gmod, g2_d, modT, 4, "n2")
    h2tile = Tile(h2T, "h2v")
    h2tile.b = arenaA.b
    sq8t = Tile(scr.t[:, 0:8 * 512].rearrange("p (c t) -> p c t", c=8), "sq8v")
    S.barrier()
    emit_norm_mod(S, C, xT, h2tile, arenaA, gmod, modT, 3, ps[0], "n2", blocks=BLK3, ps2=ps[1],
                  bufs=([sq8t], rstds, tmpf))
    for i, col in enumerate((0, NL3 - 1, NL3, NT3 - 1)):
        S.op("dve", lambda e: e.tensor_scalar(out=h2T[:, :, col:col + 1], in0=h2T[:, :, col:col + 1],
                                              scalar1=hm[:, i:i + 1], scalar2=None, op0=ALU.mult),
             r=[arenaA, hm], w=[arenaA])
    S.barrier()
    gFt = Buf("gF")
    gF = big.t[:, 0:22 * NT].rearrange("p (c t) -> p c t", c=22)
    spare = arenaA.t[:, 8 * NT3:12 * NT3]
    wupA = [Tile(spare[:, i * 4096:(i + 1) * 4096].rearrange("p (c n) -> p c n", c=8), "wupA%d" % i) for i in range(2)]
    wupB = [S.sbuf("wupB%d" % i, [128, 8, 512], BF16) for i in range(2)]
    wdn = [Tile(scr.t[:, i * 2816:(i + 1) * 2816].rearrange("p (c n) -> p c n", c=22), "wdn%d" % i) for i in range(2)]
    ua = tmpf
    xmt = sgt
    blocks = [b_ for grp in FFN_GROUPS for b_ in grp]
    gcol = []
    o = 0
    for (c0, n, j) in blocks:
        gcol.append(o)
        o += n
    assert o == NT
    ui = 0
    for fg in range(6):
        nf = 4 if fg < 5 else 2
        wa, wb_ = wupA[fg % 2], wupB[fg % 2]
        S.dma("pool", wa[:, :, 0:nf * 128], up_d[:, fg * 512:fg * 512 + nf * 128].rearrange("(c p) n -> p c n", p=128),
              w=[wa])
        S.dma("pool", wb_[:, :, 0:nf * 128],
              up_d[:, D_FF + fg * 512:D_FF + fg * 512 + nf * 128].rearrange("(c p) n -> p c n", p=128), w=[wb_])
        for fl in range(nf):
            f = fg * 4 + fl
            for bi, (c0, n, j) in enumerate(blocks):
                pa = ps[pi % 4]
                pb = ps[4 + pi % 4]
                pi += 1
                for wt, p_ in ((wa, pa), (wb_, pb)):
                    for kc in range(8):
                        S.op("pe", lambda e: e.matmul(p_[:, :n + 2], wt[:, kc, fl * 128:(fl + 1) * 128],
                                                      h2T[:, kc, c0 - 1:c0 + n + 1], start=(kc == 0), stop=(kc == 7)),
                             r=[wt, arenaA], w=[p_], signal=(kc == 7))
                a_ = ua[ui % 2]
                b_ = ub[ui % 2]
                ui += 1
                for half, p_, u_ in ((0, pa, a_), (1, pb, b_)):
                    ch = half * 22 + f
                    S.op("act", lambda e: e.activation(out=u_[:, :n], in_=p_[:, 1:n + 1], func=AF.Identity,
                                                       bias=cb[:, ch:ch + 1], scale=cw[:, 1, ch:ch + 1]),
                         r=[p_, cb, cw], w=[u_])
                    S.op("dve", lambda e: e.scalar_tensor_tensor(out=u_[:, :n], in0=p_[:, 0:n],
                                                                 scalar=cw[:, 0, ch:ch + 1], in1=u_[:, :n],
                                                                 op0=ALU.mult, op1=ALU.add),
                         r=[p_, cw, u_], w=[u_])
                    S.op("dve", lambda e: e.scalar_tensor_tensor(out=u_[:, :n], in0=p_[:, 2:n + 2],
                                                                 scalar=cw[:, 2, ch:ch + 1], in1=u_[:, :n],
                                                                 op0=ALU.mult, op1=ALU.add),
                         r=[p_, cw, u_], w=[u_])
                sg = sgt[ui % 2]
                S.op("act", lambda e: e.activation(out=sg[:, :n], in_=a_[:, :n], func=AF.Silu), r=[a_], w=[sg])
                S.op("pool", lambda e: e.tensor_tensor(out=gF[:, f, gcol[bi]:gcol[bi] + n], in0=sg[:, :n],
                                                       in1=b_[:, :n], op=ALU.mult), r=[sg, b_], w=[gFt])
    xo_v = xo_d.rearrange("(c p) t -> p c t", p=128)
    xm_v = xm_d.rearrange("(c p) t -> p c t", p=128)
    xi = 0
    for oc in range(8):
        wt = wdn[oc % 2]
        S.dma("pool", wt[:], dn_d[:, oc * 128:(oc + 1) * 128].rearrange("(c p) n -> p c n", p=128), w=[wt])
        for bi, (c0, n, j) in enumerate(blocks):
            p_ = ps[pi % 8]
            pi += 1
            xm = xmt[xi % 2]
            xi += 1
            S.dma("sp", xm[:, :n], xm_v[:, oc, c0:c0 + n], w=[xm])
            for f in range(22):
                S.op("pe", lambda e: e.matmul(p_[:, :n], wt[:, f, :], gF[:, f, gcol[bi]:gcol[bi] + n],
                                              start=(f == 0), stop=(f == 21)),
                     r=[wt, gFt], w=[p_], signal=(f == 21))
            S.op("dve", lambda e: e.scalar_tensor_tensor(out=xm[:, :n], in0=p_[:, :n],
                                                         scalar=modT[:, 40 + oc, j:j + 1], in1=xm[:, :n],
                                                         op0=ALU.mult, op1=ALU.add),
                 r=[p_, modT, xm], w=[xm])
            S.dma("sp", xo_v[:, oc, gcol[bi]:gcol[bi] + n], xm[:, :n], r=[xm])
    S.finish(xmt, "sp")
    return nc, S


def halo_cols(a, b, seg, n, tot):
    lo, hi = seg * n - 1, (seg + 1) * n + 1
    out = np.zeros((hi - lo,) + a.shape[2:], a.dtype)
    s, e = max(lo, 0), min(hi, tot)
    out[s - lo:e - lo] = a[b, s:e]
    return out


def p3_inmaps(inp, l, x, ctx, modT_list, o_lat, o_ctx, g_lat, g_ctx):
    maps = []
    cwl = inp["ffn_conv_w"][l]
    cw = np.ascontiguousarray(cwl.reshape(3, 44, 128).transpose(2, 0, 1))
    for c in range(NCORE):
        b, seg = c // 4, c % 4

        def cols(al, ac):
            return np.ascontiguousarray(np.concatenate([halo_cols(al, b, seg, TS, T), halo_cols(ac, b, seg, LS, L)], 0).T)
        hm = np.array([seg > 0, seg < 3, seg > 0, seg < 3], np.float32)
        maps.append({
            "xT": cols(x, ctx).astype(np.float32),
            "modT": modT_list[c],
            "norm2_g": pc(inp["norm2_g"][l], 8),
            "onorm_g": np.ascontiguousarray(np.stack([inp["gla_out_norm_g"][l], inp["ret_out_norm_g"][l]], 1)),
            "hmask": np.ascontiguousarray(np.broadcast_to(hm, (128, 4))),
            "oT": cols(o_lat, o_ctx),
            "gT": cols(g_lat, g_ctx),
            "w_branch": inp["w_branch"][l],
            "w_out": inp["w_out"][l],
            "ffn_up": inp["ffn_up"][l],
            "conv_w": cw,
            "conv_b": pc(inp["ffn_conv_b"][l], 44),
            "ffn_down": inp["ffn_down"][l],
        })
    return maps


NQB = T // 128
NKT = TB // 128
NEG = -30000.0


def p2_consts_np():
    bf = ml_dtypes.bfloat16
    c = {}
    bo = np.zeros((128, 128), np.float32)
    bo[:64, :64] = 1
    bo[64:, 64:] = 1
    c["BO"] = bo.astype(bf)
    R = np.zeros((64, 64), np.float32)
    for base in (0, 32):
        for f in range(16):
            R[base + f, base + 16 + f] = -1.0
            R[base + 16 + f, base + f] = 1.0
    rp = np.zeros((128, 128), np.float32)
    rp[:64, :64] = R.T
    rp[64:, 64:] = R.T
    c["RP"] = rp.astype(bf)
    c["IDN"] = np.eye(128, dtype=np.float32).astype(bf)
    s = np.arange(128)[:, None]
    i = np.arange(128)[None, :]
    mb = np.zeros((128, 2, 2, 128), np.float32)
    mb[:, 0] = np.where(s >= i, 0.0, NEG)[:, None, :]
    mb[:, 1] = np.where(s <= i, 0.0, NEG)[:, None, :]
    c["MB"] = mb.reshape(128, 2, 256).astype(bf)
    same = (s // 64) == (i // 64)
    sm = np.zeros((128, 2, 4, 128), np.float32)
    sm[:, 0] = (same & (s <= i))[:, None, :]
    sm[:, 1] = (same & (s >= i))[:, None, :]
    c["SM"] = sm.reshape(128, 2, 512).astype(bf)
    tri = np.zeros((128, 4, 128), np.float32)
    tri[:, 0] = same & (s <= i)
    tri[:, 1] = same & (s >= i)
    tri[:, 2] = same & (s > i)
    tri[:, 3] = same & (s < i)
    c["TRI"] = (tri * (-1.0 / 16.0)).astype(bf)
    t = np.arange(512) % 64
    idx = np.zeros((64, 2, 512), np.float32)
    idx[:, 0] = (t + 1)[None, :]
    idx[:, 1] = (64 - t)[None, :]
    c["IDX"] = idx
    p = np.arange(128) % 64
    c["PIDX"] = np.stack([63 - p, p], 1).astype(np.float32)
    return c


def rope_tables_np():
    tt = np.arange(T)
    row = (tt // GRID_W).astype(np.float32)
    col = (tt % GRID_W).astype(np.float32)
    inv = (10000.0 ** (-np.arange(16, dtype=np.float32) * 2.0 / 32.0)).astype(np.float32)
    ang = np.zeros((64, T), np.float32)
    for d in range(64):
        pos = row if d < 32 else col
        ang[d] = pos * inv[d % 16]
    cos = np.cos(ang).astype(np.float32)
    sin = np.sin(ang).astype(np.float32)
    return cos, sin


class P2:
    def __init__(self):
        nc = bass.Bass("TRN2", target_bir_lowering=False)
        self.nc = nc
        S = Sched(nc)
        self.S = S
        self.C = Consts(S)
        di = lambda n, sh, dt: nc.dram_tensor(n, list(sh), dt, kind="ExternalInput").ap()
        do = lambda n, sh, dt: nc.dram_tensor(n, list(sh), dt, kind="ExternalOutput").ap()
        self.d = {
            "aqT": di("aqT", [128, TB], BF16), "akT": di("akT", [128, TB], BF16), "av": di("av", [128, NKT * 64], BF16),
            "cosT": di("cosT", [64, T], F32), "sinT": di("sinT", [64, T], F32),
            "anp": di("anp", [128, 4], F32),
            "BO": di("BO", [128, 128], BF16), "RP": di("RP", [128, 128], BF16), "IDN": di("IDN", [128, 128], BF16),
            "MB": di("MB", [128, 2, 256], BF16), "SM": di("SM", [128, 2, 512], BF16),
            "TRI": di("TRI", [128, 4, 128], BF16), "IDX": di("IDX", [64, 2, 512], F32), "PIDX": di("PIDX", [128, 2], F32),
            "oaT": do("oaT", [128, TB], BF16),
        }
        self.R = [S.sbuf("R%d" % i, [128, TB], BF16) for i in range(6)]
        self.ps = [S.psum("ps%d" % i, [128, 512]) for i in range(8)]
        self.k = {}
        for n in ("BO", "RP", "IDN"):
            self.k[n] = S.sbuf("k_" + n, [128, 128], BF16)
            S.dma("sp", self.k[n][:], self.d[n], w=[self.k[n]])
        self.k["MB"] = S.sbuf("k_MB", [128, 2, 256], BF16)
        S.dma("sp", self.k["MB"][:], self.d["MB"], w=[self.k["MB"]])
        self.c64 = S.sbuf("c_64eps", [128, 1], F32)
        S.op("pool", lambda e: e.memset(self.c64[:], float(64 * EPS)), w=[self.c64])
        self.tmp = [S.sbuf("p2tmp%d" % i, [128, 512], F32) for i in range(4)]
        self.tbf = [S.sbuf("p2tbf%d" % i, [128, 512], BF16) for i in range(4)]
        self.cs = [S.sbuf("p2cs%d" % i, [128, 2, 512], F32) for i in range(2)]

    def prep_slots(self):
        if hasattr(self, "slots"):
            return
        def mk(rt, pa, pb, tag):
            def v(lo, hi, f32, name):
                ap = rt[:, lo:hi]
                if f32:
                    ap = ap.bitcast(F32)
                return Tile(ap, tag + name)
            return dict(xin=v(4096, 4608, False, "xin"), sq=v(0, 512, False, "sq"), xg=v(512, 1024, False, "xg"),
                        rstd=v(1024, 2048, True, "rstd"), t1=v(2048, 3072, True, "t1"), t2=v(3072, 4096, True, "t2"),
                        pa=pa, pb=pb)
        self.slots = [
            dict(xin=self.tbf[0], sq=self.tbf[2], xg=self.tbf[3], rstd=self.tmp[0], t1=self.tmp[1], t2=self.tmp[2],
                 pa=self.ps[0], pb=self.ps[1]),
            mk(self.R[3].t, self.ps[2], self.ps[3], "s1"),
            mk(self.R[4].t, self.ps[4], self.ps[5], "s2"),
            mk(self.R[5].t, self.ps[6], self.ps[7], "s3"),
        ]

    def norm_rope_block(self, src_d, dst, c0, n, gain_ap, gain_tok, roped, bi, np_=128):
        S, k = self.S, self.k
        self.prep_slots()
        sl = self.slots[bi % 4]
        xin, sq, xg, rstd = sl["xin"], sl["sq"], sl["xg"], sl["rstd"]
        ps_a, ps_b = sl["pa"], sl["pb"]
        S.dma("sp", xin[:np_, :n], src_d[:np_, c0:c0 + n], w=[xin])
        S.op("act", lambda e: e.activation(out=sq[:np_, :n], in_=xin[:np_, :n], func=AF.Square), r=[xin], w=[sq])
        S.op("pe", lambda e: e.matmul(ps_a[:np_, :n], k["BO"][:np_, :np_], sq[:np_, :n], start=True, stop=True),
             r=[k["BO"], sq], w=[ps_a])
        S.op("act", lambda e: e.activation(out=rstd[:np_, :n], in_=ps_a[:np_, :n], func=AF.Ln,
                                           bias=self.c64[:np_, 0:1], scale=1.0), r=[ps_a, self.c64], w=[rstd])
        S.op("act", lambda e: e.activation(out=rstd[:np_, :n], in_=rstd[:np_, :n], func=AF.Exp, scale=-0.5),
             r=[rstd], w=[rstd])
        if not roped:
            S.op("dve", lambda e: e.scalar_tensor_tensor(out=dst[:np_, c0:c0 + n], in0=xin[:np_, :n], scalar=gain_ap,
                                                         in1=rstd[:np_, :n], op0=ALU.mult, op1=ALU.mult),
                 r=[xin, gain_tok, rstd], w=[dst])
            return
        S.op("dve", lambda e: e.scalar_tensor_tensor(out=xg[:np_, :n], in0=xin[:np_, :n], scalar=gain_ap,
                                                     in1=rstd[:np_, :n], op0=ALU.mult, op1=ALU.mult),
             r=[xin, gain_tok, rstd], w=[xg])
        self.rope_block(xg, dst, c0, n, np_, slot=sl)

    def load_cs(self, c0, n):
        S = self.S
        cs = self.cs[(c0 // 512) % 2]
        if getattr(cs, "c0", None) == c0 and getattr(cs, "gen", None) == self.gen:
            return cs
        for hf in range(2):
            S.dma("sp", cs[hf * 64:(hf + 1) * 64, 0, :n], self.d["cosT"][:, c0:c0 + n], w=[cs])
            S.dma("sp", cs[hf * 64:(hf + 1) * 64, 1, :n], self.d["sinT"][:, c0:c0 + n], w=[cs])
        cs.c0 = c0
        cs.gen = self.gen
        return cs

    def rope_block(self, xg, dst, c0, n, np_, scale=None, src_c0=None, slot=None):
        S, k = self.S, self.k
        ps_b = self.ps[7] if src_c0 is not None else (slot["pb"] if slot else self.ps[1])
        cs = self.load_cs(c0 if src_c0 is None else src_c0, n)
        t1, t2 = (slot["t1"], slot["t2"]) if slot else (self.tmp[1], self.tmp[2])
        xs = 0 if src_c0 is None else src_c0
        xgt = xg
        xg = xg.t[:, xs:xs + n]
        S.op("pe", lambda e: e.matmul(ps_b[:np_, :n], k["RP"][:np_, :np_], xg[:np_, :n], start=True, stop=True),
             r=[k["RP"], xgt], w=[ps_b])
        S.op("dve", lambda e: e.tensor_tensor(out=t1[:np_, :n], in0=xg[:np_, :n], in1=cs[:np_, 0, :n], op=ALU.mult),
             r=[xgt, cs], w=[t1])
        S.op("dve", lambda e: e.tensor_tensor(out=t2[:np_, :n], in0=ps_b[:np_, :n], in1=cs[:np_, 1, :n], op=ALU.mult),
             r=[ps_b, cs], w=[t2])
        if scale is None:
            S.op("pool", lambda e: e.tensor_tensor(out=dst[:np_, c0:c0 + n], in0=t1[:np_, :n], in1=t2[:np_, :n],
                                                   op=ALU.add), r=[t1, t2], w=[dst])
        else:
            S.op("pool", lambda e: e.tensor_tensor(out=t1[:np_, :n], in0=t1[:np_, :n], in1=t2[:np_, :n],
                                                   op=ALU.add), r=[t1, t2], w=[t1])
            S.op("act", lambda e: e.activation(out=dst[:np_, c0:c0 + n], in_=t1[:np_, :n], func=AF.Copy,
                                               scale=float(scale)), r=[t1], w=[dst])

    def attention(self, stage=9, nqb=None):
        S, k, d = self.S, self.k, self.d
        self.gen = "attn"
        qn, kn, vt = self.R[0], self.R[1], self.R[2]
        anp = S.sbuf("anp", [128, 4], F32)
        S.dma("sp", anp[:], d["anp"], w=[anp])
        S.op("dve", lambda e: e.tensor_scalar(out=anp[:, 1:2], in0=anp[:, 1:2], scalar1=8.0, scalar2=None,
                                              op0=ALU.mult), r=[anp], w=[anp])
        es = S.sbuf("esink", [128, 2], F32)
        S.op("act", lambda e: e.activation(out=es[:], in_=anp[:, 2:4], func=AF.Exp), r=[anp], w=[es])
        esink = S.sbuf("esink_t", [64, 256], F32)
        ones_f = S.sbuf("ones_f", [64, 128], F32)
        S.op("pool", lambda e: e.memset(ones_f[:], 1.0), w=[ones_f])
        for h in range(2):
            S.op("dve", lambda e: e.tensor_scalar(out=esink[:, h * 128:(h + 1) * 128], in0=ones_f[:],
                                                  scalar1=es[0:64, h:h + 1], scalar2=None, op0=ALU.mult),
                 r=[ones_f, es], w=[esink])
        vv = vt.t[:, 0:NKT * 64].rearrange("p (i d) -> p i d", d=64)
        S.dma("sp", vt[:, 0:NKT * 64], d["av"], w=[vt])
        blocks = [(i * 512, 512, True) for i in range(T // 512)] + [(T, L, False)]
        self.att_out_toks = [qn, kn, vt]
        if stage < 1:
            return
        for bi, (c0, n, roped) in enumerate(blocks):
            self.norm_rope_block(d["aqT"], qn, c0, n, anp[:, 0:1], anp, roped, (bi % 2) * 2)
            self.norm_rope_block(d["akT"], kn, c0, n, anp[:, 1:2], anp, roped, (bi % 2) * 2 + 1)
        S.barrier()
        PT = [S.sbuf("PT%d" % i, [128, 2, 5, 128], BF16) for i in range(2)]
        ost = [S.sbuf("oast%d" % i, [64, 2, 512], BF16) for i in range(2)]
        rec = S.sbuf("arec", [64, 256], F32)
        oview = d["oaT"].rearrange("(h d) t -> d h t", h=2)
        if stage < 2:
            return
        for qi, qb in enumerate(range(NQB + 2) if nqb is None else nqb):
            if qb < NQB:
                chunks = ([(qb - 1, 0)] if qb > 0 else []) + [(qb, None)] + ([(qb + 1, 1)] if qb < NQB - 1 else []) \
                    + [(NQB, None), (NQB + 1, None)]
            else:
                chunks = [(NQB, None), (NQB + 1, None)]
            A = self.ps[0:2] if qi % 2 == 0 else self.ps[2:4]
            Bk = self.ps[4:6]
            pv = self.ps[6 + qi % 2]
            pt = PT[qi % 2]
            qc = slice(qb * 128, (qb + 1) * 128)
            nch = len(chunks)
            for h in range(2):
                for bank, sl in ((A[h], range(0, min(nch, 4))), (Bk[h], range(4, nch))):
                    mms = []
                    for ci in sl:
                        kt, mi = chunks[ci]
                        o_ = bank[:, (ci % 4) * 128:(ci % 4 + 1) * 128]
                        mms.append((o_, kn[h * 64:(h + 1) * 64, kt * 128:(kt + 1) * 128],
                                    qn[h * 64:(h + 1) * 64, qc], [kn, qn]))
                        if mi is not None:
                            mms.append((o_, k["IDN"][:], k["MB"][:, mi, 0:128], [k["IDN"], k["MB"]]))
                    for i_, (o_, l_, r_, rd) in enumerate(mms):
                        S.op("pe", lambda e: e.matmul(o_, l_, r_, start=(i_ == 0), stop=(i_ == len(mms) - 1),
                                                      skip_group_check=True),
                             r=rd, w=[bank], signal=(i_ == len(mms) - 1))
            self.att_out_toks = list(self.ps)
            if stage < 3:
                continue
            for h in range(2):
                na = min(nch, 4)
                S.op("act", lambda e: e.activation(out=pt[:, h, 0:na, :],
                                                   in_=A[h][:, 0:na * 128].rearrange("p (c n) -> p c n", n=128),
                                                   func=AF.Exp), r=[A[h]], w=[pt])
                if nch > 4:
                    S.op("act", lambda e: e.activation(out=pt[:, h, 4, :], in_=Bk[h][:, 0:128], func=AF.Exp),
                         r=[Bk[h]], w=[pt])
            self.att_out_toks = list(self.ps) + PT
            if stage < 4:
                continue
            for ci, (kt, mi) in enumerate(chunks):
                S.op("pe", lambda e: e.matmul(pv[0:64, 0:256], vv[:, kt, :], pt[:, :, ci, :],
                                              start=(ci == 0), stop=(ci == nch - 1)),
                     r=[vt, pt], w=[pv], signal=False)
            for ci, (kt, mi) in enumerate(chunks):
                S.op("pe", lambda e: e.matmul(pv[0:64, 256:512], self.C.ones_bf[:, 0:64], pt[:, :, ci, :],
                                              start=(ci == 0), stop=(ci == nch - 1)),
                     r=[self.C.ones_bf, pt], w=[pv], signal=(ci == nch - 1))
            if stage < 5:
                continue
            st = ost[(qb // 4) % 2]
            so = (qb % 4) * 128
            S.op("dve", lambda e: e.tensor_tensor(out=rec[:], in0=pv[0:64, 256:512], in1=esink[:], op=ALU.add),
                 r=[pv, esink], w=[rec])
            S.op("dve", lambda e: e.reciprocal(out=rec[:], in_=rec[:]), r=[rec], w=[rec])
            S.op("dve", lambda e: e.tensor_tensor(out=st[:, :, so:so + 128],
                                                  in0=pv[0:64, 0:256].rearrange("p (h n) -> p h n", h=2),
                                                  in1=rec[:, :].rearrange("p (h n) -> p h n", h=2), op=ALU.mult),
                 r=[pv, rec], w=[st])
            if qb % 4 == 3 or qb == NQB + 1:
                g0 = (qb // 4) * 512
                wdt = so + 128
                if stage >= 6:
                    S.dma("sp", oview[:, :, g0:g0 + wdt], st[:, :, 0:wdt], r=[st])
            self.att_out_toks = ost + list(self.ps) + PT


    def scan_setup(self):
        S, d, nc = self.S, self.d, self.nc
        if hasattr(self, "scan_ready"):
            return
        self.scan_ready = True
        di = lambda n, sh, dt: nc.dram_tensor(n, list(sh), dt, kind="ExternalInput").ap()
        do = lambda n, sh, dt: nc.dram_tensor(n, list(sh), dt, kind="ExternalOutput").ap()
        d.update({
            "gqT": di("gqT", [64, TB], BF16), "gkT": di("gkT", [64, TB], BF16), "gaT": di("gaT", [32, TB], BF16),
            "gk": di("gk", [128, NKT * 64], BF16), "gv": di("gv", [128, NKT * 128], BF16),
            "wg": di("wg", [33, 128], F32),
            "rqT": di("rqT", [64, TB], BF16), "rkT": di("rkT", [64, TB], BF16), "rv": di("rv", [128, NKT * 128], BF16),
            "rld": di("rld", [128, 2], F32),
            "ogT": do("ogT", [128, TB], BF16), "orT": do("orT", [128, TB], BF16),
        })
        k = self.k
        k["SM"] = S.sbuf("k_SM", [128, 2, 512], BF16)
        S.dma("sp", k["SM"][:], d["SM"], w=[k["SM"]])
        k["TRI"] = S.sbuf("k_TRI", [128, 4, 128], BF16)
        S.dma("sp", k["TRI"][:], d["TRI"], w=[k["TRI"]])
        k["IDX"] = S.sbuf("k_IDX", [64, 2, 512], F32)
        S.dma("sp", k["IDX"][:], d["IDX"], w=[k["IDX"]])
        k["PIDX"] = S.sbuf("k_PIDX", [128, 2], F32)
        S.dma("sp", k["PIDX"][:], d["PIDX"], w=[k["PIDX"]])
        self.onef = S.sbuf("c_onef", [128, 1], F32)
        S.op("pool", lambda e: e.memset(self.onef[:], 1.0), w=[self.onef])
        self.acc = S.sbuf("sc_acc", [128, TB], F32)
        self.Sall = [S.sbuf("sc_Sall%d" % i, [64, 9, 128], F32) for i in range(3)]
        self.Sbf = [S.sbuf("sc_Sbf%d" % i, [64, 8, 128], BF16) for i in range(2)]
        self.qin = [S.sbuf("sc_qin%d" % i, [64, 512], BF16) for i in range(2)]
        self.kin = [S.sbuf("sc_kin%d" % i, [64, 512], BF16) for i in range(2)]
        self.kend = [S.sbuf("sc_kend%d" % i, [128, 4, 64], BF16) for i in range(2)]
        self.attm = [S.sbuf("sc_attm%d" % i, [128, 512], BF16) for i in range(2)]
        self.Eq = [S.sbuf("sc_Eq%d" % i, [64, 512], F32) for i in range(2)]
        self.Ek = S.sbuf("sc_Ek", [64, 512], F32)
        self.Eend = S.sbuf("sc_Eend", [128, 4, 64], F32)
        self.sp = S.sbuf("sc_sp", [128, 4, 128], F32)
        self.gaf = self.tmp[2]
        self.gi = 0

    def scan_groups(self, z):
        lat = [(g * 512, 512) for g in range(T // 512)]
        return [(T, L)] + (lat if z == 0 else lat[::-1])

    def chunk_front(self, z, c0, n, qin, kin, kend, kend_tok, vtm, dec_fn, dec_toks, outb):
        S, k = self.S, self.k
        A = self.ps[0]
        Bk, Ck = self.ds_banks[self.gi % 2]
        nt = n // 128
        nchunk = 2 * nt
        t0 = c0 // 128
        gi = self.gi
        self.gi += 1
        attm = self.attm[gi % 2]
        Sall, Snext = self.Sall[gi % 3], self.Sall[(gi + 1) % 3]
        vv = vtm.t[:, 0:NKT * 128].rearrange("p (i d) -> p i d", d=128)
        for p in range(nt):
            S.op("pe", lambda e: e.matmul(A[:, p * 128:(p + 1) * 128], kin[:, p * 128:(p + 1) * 128],
                                          qin[:, p * 128:(p + 1) * 128], start=(p == 0), stop=(p == nt - 1),
                                          skip_group_check=True), r=[kin, qin], w=[A], signal=(p == nt - 1))
        S.op("dve", lambda e: e.tensor_tensor(out=attm[:, :n], in0=A[:, :n], in1=k["SM"][:, z, :n], op=ALU.mult),
             r=[A, k["SM"]], w=[attm])
        for cc in range(nchunk):
            p, hf = cc // 2, cc % 2
            bank = Bk if hf == 0 else Ck
            S.op("pe", lambda e: e.matmul(bank[0:64, p * 128:(p + 1) * 128], kend[hf * 64:(hf + 1) * 64, p, :],
                                          vv[hf * 64:(hf + 1) * 64, t0 + p, :], start=(p == 0), stop=(p == nt - 1),
                                          skip_group_check=True), r=[kend_tok, vtm], w=[bank], signal=(p == nt - 1))
        order = list(range(nchunk)) if z == 0 else list(range(nchunk - 1, -1, -1))
        pos = {}
        for i, cc in enumerate(order):
            pos[cc] = i
            p, hf = cc // 2, cc % 2
            bank = Bk if hf == 0 else Ck
            last = (i == nchunk - 1)
            o_ = Snext[:, 0, :] if last else Sall[:, i + 1, :]
            S.op("dve", lambda e: e.scalar_tensor_tensor(out=o_, in0=Sall[:, i, :], scalar=dec_fn(cc),
                                                         in1=bank[0:64, p * 128:(p + 1) * 128],
                                                         op0=ALU.mult, op1=ALU.add),
                 r=[Sall, bank] + dec_toks, w=[Snext if last else Sall])
        return dict(z=z, c0=c0, n=n, nt=nt, nchunk=nchunk, t0=t0, gi=gi, attm=attm, Sall=Sall, qin=qin, vtm=vtm,
                    vv=vv, pos=pos, outb=outb)

    def chunk_back(self, c):
        if c is None:
            return
        S = self.S
        Dk = self.ps[3]
        z, c0, n, nt, nchunk, t0, gi = c["z"], c["c0"], c["n"], c["nt"], c["nchunk"], c["t0"], c["gi"]
        attm, Sall, qin, vtm, vv, pos, outb = c["attm"], c["Sall"], c["qin"], c["vtm"], c["vv"], c["pos"], c["outb"]
        Sbf = self.Sbf[gi % 2]
        S.op("act", lambda e: e.activation(out=Sbf[:, 0:nchunk, :], in_=Sall[:, 0:nchunk, :], func=AF.Copy),
             r=[Sall], w=[Sbf])
        nmm = nt * 3
        mi = 0
        for p in range(nt):
            S.op("pe", lambda e: e.matmul(Dk[:, p * 128:(p + 1) * 128], vv[:, t0 + p, :], attm[:, p * 128:(p + 1) * 128],
                                          start=(mi == 0), stop=False, skip_group_check=True),
                 r=[vtm, attm], w=[Dk], signal=False)
            mi += 1
            for hf in range(2):
                cc = 2 * p + hf
                S.op("pe", lambda e: e.matmul(Dk[:, cc * 64:(cc + 1) * 64], Sbf[:, pos[cc], :],
                                              qin[:, cc * 64:(cc + 1) * 64], start=False, stop=(mi == nmm - 1),
                                              skip_group_check=True),
                     r=[Sbf, qin], w=[Dk], signal=(mi == nmm - 1))
                mi += 1
        if z == 0:
            S.op("act", lambda e: e.activation(out=self.acc[:, c0:c0 + n], in_=Dk[:, :n], func=AF.Copy),
                 r=[Dk], w=[self.acc])
        else:
            S.op("dve", lambda e: e.tensor_tensor(out=outb[:, c0:c0 + n], in0=Dk[:, :n], in1=self.acc[:, c0:c0 + n],
                                                  op=ALU.add), r=[Dk, self.acc], w=[outb])

    def scan_init_state(self):
        S = self.S
        Sall = self.Sall[self.gi % 3]
        S.op("pool", lambda e: e.memset(Sall[:, 0, :], 0.0), w=[Sall])

    def gla(self):
        S, k, d = self.S, self.k, self.d
        self.scan_setup()
        self.gen = "gla"
        S.barrier()
        qT, kT, vtm, ktm, gaT, outb = self.R[0], self.R[1], self.R[2], self.R[3], self.R[4], self.R[5]
        S.dma("sp", qT[0:64, :], d["gqT"], w=[qT])
        S.dma("sp", kT[0:64, :], d["gkT"], w=[kT])
        S.dma("sp", gaT[0:32, :], d["gaT"], w=[gaT])
        S.dma("sp", vtm[:, 0:NKT * 128], d["gv"], w=[vtm])
        S.dma("sp", ktm[:, 0:NKT * 64], d["gk"], w=[ktm])
        kk = ktm.t[:, 0:NKT * 64].rearrange("p (i d) -> p i d", d=64)
        S.op("pool", lambda e: e.memset(gaT[32:33, :], 1.0), w=[gaT])
        wgb = S.sbuf("gla_wgb", [33, 128], BF16)
        S.dma("pool", wgb[:], d["wg"], w=[wgb])
        trib = k["TRI"]
        spb = self.tbf[2].t[:, :].rearrange("p (i c) -> p i c", c=128)
        spt = self.tbf[2]
        E, Fk, Gk = self.ps[4], self.ps[5], self.ps[4]
        self.ds_banks = [(self.ps[1], self.ps[2]), (self.ps[6], self.ps[7])]
        ex = self.tmp[3]
        pend = None
        for z in range(2):
            self.scan_init_state()
            for (c0, n) in self.scan_groups(z):
                nt = n // 128
                t0 = c0 // 128
                gi = self.gi
                qin, kin, kend, Eq = self.qin[gi % 2], self.kin[gi % 2], self.kend[gi % 2], self.Eq[gi % 2]
                for p in range(nt):
                    S.op("pe", lambda e: e.matmul(E[:, p * 128:(p + 1) * 128], gaT[0:33, c0 + p * 128:c0 + (p + 1) * 128],
                                                  wgb[:, :], start=(p == 0), stop=(p == nt - 1), skip_group_check=True),
                         r=[gaT, wgb], w=[E], signal=(p == nt - 1))
                S.op("act", lambda e: e.activation(out=ex[:, :n], in_=E[:, :n], func=AF.Exp, scale=-1.0),
                     r=[E], w=[ex])
                S.op("act", lambda e: e.activation(out=spb[:, 0:nt, :], in_=ex[:, :n].rearrange("p (i c) -> p i c", c=128),
                                                   func=AF.Ln, bias=self.onef[:, 0:1], scale=1.0),
                     r=[ex, self.onef], w=[spt])
                for p in range(nt):
                    S.op("pe", lambda e: e.matmul(Fk[0:64, p * 128:(p + 1) * 128], spb[:, p, z * 64:(z + 1) * 64],
                                                  trib[:, z, :], start=(p == 0), stop=(p == nt - 1),
                                                  skip_group_check=True),
                         r=[spt, trib], w=[Fk], signal=(p == nt - 1))
                S.op("act", lambda e: e.activation(out=Eq[:, :n], in_=Fk[0:64, :n], func=AF.Exp), r=[Fk], w=[Eq])
                S.op("act", lambda e: e.activation(out=self.Ek[:, :n], in_=Fk[0:64, :n], func=AF.Exp, scale=-1.0),
                     r=[Fk], w=[self.Ek])
                S.op("dve", lambda e: e.scalar_tensor_tensor(out=qin[:, :n], in0=qT[0:64, c0:c0 + n], scalar=0.125,
                                                             in1=Eq[:, :n], op0=ALU.mult, op1=ALU.mult),
                     r=[qT, Eq], w=[qin])
                S.op("dve", lambda e: e.tensor_tensor(out=kin[:, :n], in0=kT[0:64, c0:c0 + n], in1=self.Ek[:, :n],
                                                      op=ALU.mult), r=[kT, self.Ek], w=[kin])
                for p in range(nt):
                    S.op("pe", lambda e: e.matmul(Gk[:, p * 64:(p + 1) * 64], trib[:, 2 + z, :],
                                                  spb[:, p, z * 64:(z + 1) * 64], start=(p == 0), stop=(p == nt - 1),
                                                  skip_group_check=True),
                         r=[spt, trib], w=[Gk], signal=(p == nt - 1))
                S.op("act", lambda e: e.activation(out=self.Eend[:, 0:nt, :],
                                                   in_=Gk[:, 0:nt * 64].rearrange("p (i c) -> p i c", c=64),
                                                   func=AF.Exp), r=[Gk], w=[self.Eend])
                S.op("dve", lambda e: e.tensor_tensor(out=kend[:, 0:nt, :], in0=kk[:, t0:t0 + nt, :],
                                                      in1=self.Eend[:, 0:nt, :], op=ALU.mult),
                     r=[ktm, self.Eend], w=[kend])
                col = (lambda cc: cc * 64 + 63) if z == 0 else (lambda cc: cc * 64)
                cur = self.chunk_front(z, c0, n, qin, kin, kend, kend, vtm,
                                       lambda cc, Eq=Eq, col=col: Eq[:, col(cc):col(cc) + 1], [Eq], outb)
                self.chunk_back(pend)
                pend = cur
        self.chunk_back(pend)
        S.dma("sp", d["ogT"], outb[:, :], r=[outb])
        self.gla_out_toks = [outb]

    def ret(self):
        S, k, d = self.S, self.k, self.d
        self.scan_setup()
        self.gen = "ret"
        qT, kT, vtm, outb = self.R[0], self.R[1], self.R[2], self.R[5]
        S.dma("sp", qT[0:64, :], d["rqT"], w=[qT])
        S.dma("sp", kT[0:64, :], d["rkT"], w=[kT])
        S.dma("sp", vtm[:, 0:NKT * 128], d["rv"], w=[vtm])
        rld = S.sbuf("ret_rld", [128, 2], F32)
        S.dma("sp", rld[:], d["rld"], w=[rld])
        nlg = S.sbuf("ret_nlg", [128, 2], F32)
        lg = S.sbuf("ret_lg", [128, 2], F32)
        S.op("act", lambda e: e.activation(out=nlg[:], in_=rld[:], func=AF.Exp), r=[rld], w=[nlg])
        S.op("dve", lambda e: e.tensor_scalar(out=lg[:], in0=nlg[:], scalar1=-1.0, scalar2=None, op0=ALU.mult),
             r=[nlg], w=[lg])
        EqR = self.Eq
        spv = Tile(self.sp.t[0:64, :, :].rearrange("p a b -> p (a b)"), "spv")
        spv.b = self.sp.b
        EkR = [self.Ek, spv]
        EendR = S.sbuf("ret_EendR", [128, 2], F32)
        decR = S.sbuf("ret_decR", [64, 2], F32)
        for z in range(2):
            S.op("act", lambda e: e.activation(out=EqR[z][:], in_=k["IDX"][:, z, :], func=AF.Exp,
                                               scale=lg[0:64, z:z + 1]), r=[k["IDX"], lg], w=[EqR[z]])
            S.op("act", lambda e: e.activation(out=EkR[z][:], in_=k["IDX"][:, z, :], func=AF.Exp,
                                               scale=nlg[0:64, z:z + 1]), r=[k["IDX"], nlg], w=[EkR[z]])
            S.op("act", lambda e: e.activation(out=EendR[:, z:z + 1], in_=k["PIDX"][:, z:z + 1], func=AF.Exp,
                                               scale=lg[:, z:z + 1]), r=[k["PIDX"], lg], w=[EendR])
        S.op("act", lambda e: e.activation(out=decR[:], in_=lg[0:64, :], func=AF.Exp, scale=64.0), r=[lg], w=[decR])
        Hk = self.ps[7]
        self.ds_banks = [(self.ps[1], self.ps[2]), (self.ps[4], self.ps[5])]
        rq_r, rk_r = self.tbf[0], self.tbf[1]
        pend = None
        for z in range(2):
            self.scan_init_state()
            for (c0, n) in self.scan_groups(z):
                nt = n // 128
                gi = self.gi
                qin, kin, kend = self.qin[gi % 2], self.kin[gi % 2], self.kend[gi % 2]
                if z == 0:
                    if c0 < T:
                        self.rope_block(qT, qT, c0, n, 64, src_c0=c0)
                        self.rope_block(kT, kT, c0, n, 64, scale=0.125, src_c0=c0)
                    else:
                        S.op("act", lambda e: e.activation(out=kT[0:64, c0:c0 + n], in_=kT[0:64, c0:c0 + n],
                                                           func=AF.Copy, scale=0.125), r=[kT], w=[kT])
                S.op("dve", lambda e: e.tensor_tensor(out=qin[:, :n], in0=qT[0:64, c0:c0 + n], in1=EqR[z][:, :n],
                                                      op=ALU.mult), r=[qT, EqR[z]], w=[qin])
                S.op("dve", lambda e: e.tensor_tensor(out=kin[:, :n], in0=kT[0:64, c0:c0 + n], in1=EkR[z][:, :n],
                                                      op=ALU.mult), r=[kT, EkR[z]], w=[kin])
                for p in range(nt):
                    S.op("pe", lambda e: e.matmul(Hk[:, p * 64:(p + 1) * 64], kT[0:64, c0 + p * 128:c0 + (p + 1) * 128],
                                                  k["IDN"][0:64, 0:64], start=(p == 0), stop=(p == nt - 1),
                                                  skip_group_check=True),
                         r=[kT, k["IDN"]], w=[Hk], signal=(p == nt - 1))
                S.op("dve", lambda e: e.tensor_scalar(out=kend[:, 0:nt, :],
                                                      in0=Hk[:, 0:nt * 64].rearrange("p (i c) -> p i c", c=64),
                                                      scalar1=EendR[:, z:z + 1], scalar2=None, op0=ALU.mult),
                     r=[Hk, EendR], w=[kend])
                cur = self.chunk_front(z, c0, n, qin, kin, kend, kend, vtm, lambda cc, z=z: decR[:, z:z + 1],
                                       [decR], outb)
                self.chunk_back(pend)
                pend = cur
        self.chunk_back(pend)
        S.dma("sp", d["orT"], outb[:, :], r=[outb])
        self.ret_out_toks = [outb]


def build_p2(which=("attn", "gla", "ret"), stage=9, nqb=None):
    P = P2()
    outs = []
    if "attn" in which:
        P.attention(stage, nqb)
        outs += P.att_out_toks
    if "gla" in which:
        P.gla()
        outs += P.gla_out_toks
    if "ret" in which:
        P.ret()
        outs += P.ret_out_toks
    P.S.finish(outs, "sp")
    return P.nc, P.S


def tm_layout(a):
    n = a.shape[1]
    return np.ascontiguousarray(a.reshape(-1, 128, n).transpose(1, 0, 2).reshape(128, -1))


_PROG = {}


def _prog(name):
    if name not in _PROG:
        if name == "p1":
            _PROG[name] = build_p1()[0]
        elif name == "p2":
            _PROG[name] = build_p2()[0]
        else:
            _PROG[name] = build_p3()[0]
    return _PROG[name]


def _run(name, maps):
    res = run_bass_kernel_spmd(_prog(name), maps, core_ids=list(range(NCORE)))
    return res.results


def p2_inmaps(inp, l, zfm_b, ztm_b, consts, cos, sin):
    maps = []
    gw, gb_ = inp["gla_gate_w"][l], inp["gla_gate_b"][l]
    for c in range(NCORE):
        b, hh = c // 4, c % 4
        kvh = hh // 2
        zf, zt = zfm_b[b], ztm_b[b]

        def fm(name, o, n):
            return np.ascontiguousarray(zf[FMO[name] + o:FMO[name] + o + n])

        def tm(name, o, n):
            return tm_layout(zt[:, TMO[name] + o:TMO[name] + o + n])
        ak = fm("ak", kvh * 64, 64)
        anp = np.zeros((128, 4), np.float32)
        anp[:, 0] = np.tile(inp["attn_q_norm_g"][l], 2)
        anp[:, 1] = np.tile(inp["attn_k_norm_g"][l], 2)
        anp[:, 2] = inp["attn_sink"][l][2 * hh]
        anp[:, 3] = inp["attn_sink"][l][2 * hh + 1]
        wg = np.zeros((33, 128), np.float32)
        wg[0:16, 0:64] = gw[0][:, hh * 64:(hh + 1) * 64]
        wg[16:32, 64:128] = gw[1][:, hh * 64:(hh + 1) * 64]
        wg[32, 0:64] = gb_[0, hh * 64:(hh + 1) * 64]
        wg[32, 64:128] = gb_[1, hh * 64:(hh + 1) * 64]
        m = {
            "aqT": fm("aq", 2 * hh * 64, 128), "akT": np.ascontiguousarray(np.concatenate([ak, ak], 0)),
            "av": tm("av", kvh * 64, 64), "cosT": cos, "sinT": sin, "anp": anp,
            "gqT": fm("gq", hh * 64, 64), "gkT": fm("gk", hh * 64, 64), "gaT": fm("ga", 0, 32),
            "gk": tm("gk", hh * 64, 64), "gv": tm("gv", hh * 128, 128), "wg": wg,
            "rqT": fm("rq", hh * 64, 64), "rkT": fm("rk", hh * 64, 64), "rv": tm("rv", hh * 128, 128),
            "rld": np.ascontiguousarray(np.broadcast_to(inp["ret_log_decay"][l][:, hh], (128, 2))).astype(np.float32),
        }
        m.update(consts)
        maps.append(m)
    return maps


def kernel(**inp):
    inp = {k: np.asarray(v) for k, v in inp.items()}
    x = inp["x"].astype(np.float32)
    ctx = inp["ctx"].astype(np.float32)
    consts = p2_consts_np()
    cos, sin = rope_tables_np()
    for l in range(DEPTH):
        r1 = _run("p1", p1_inmaps(inp, l, x, ctx))
        zfm_b, ztm_b = [], []
        for b in range(B):
            zf = [np.asarray(r1[b * 4 + s]["zfm"]) for s in range(4)]
            zt = [np.asarray(r1[b * 4 + s]["ztm"]) for s in range(4)]
            zfm_b.append(np.concatenate([z[:, :TS] for z in zf] + [z[:, TS:] for z in zf], axis=1))
            ztm_b.append(np.concatenate([z[:TS] for z in zt] + [z[TS:] for z in zt], axis=0))
        modT_list = [np.asarray(r1[c]["modT"]) for c in range(NCORE)]
        del r1
        r2 = _run("p2", p2_inmaps(inp, l, zfm_b, ztm_b, consts, cos, sin))
        o_b = []
        for b in range(B):
            oa = np.concatenate([np.asarray(r2[b * 4 + hh]["oaT"]) for hh in range(4)], axis=0)
            og = np.concatenate([np.asarray(r2[b * 4 + hh]["ogT"]) for hh in range(4)], axis=0)
            orr = np.concatenate([np.asarray(r2[b * 4 + hh]["orT"]) for hh in range(4)], axis=0)
            o_b.append(np.ascontiguousarray(np.concatenate([oa, og, orr], axis=0).T))
        del r2
        o_all = np.stack(o_b, 0)
        g_all = np.stack([np.ascontiguousarray(
            np.concatenate([zfm_b[b][FMO["gr"]:FMO["gr"] + 512], zfm_b[b][FMO["rg"]:FMO["rg"] + 512],
                            zfm_b[b][FMO["mg"]:FMO["mg"] + 3072]], axis=0).T) for b in range(B)], 0)
        del zfm_b, ztm_b
        r3 = _run("p3", p3_inmaps(inp, l, x, ctx, modT_list, o_all[:, :T], o_all[:, T:], g_all[:, :T], g_all[:, T:]))
        xn = np.empty_like(x)
        cn = np.empty_like(ctx)
        for c in range(NCORE):
            b, seg = c // 4, c % 4
            xo = np.asarray(r3[c]["xo"])
            xn[b, seg * TS:(seg + 1) * TS] = xo[:, :TS].T
            cn[b, seg * LS:(seg + 1) * LS] = xo[:, TS:].T
        x, ctx = xn, cn
    return x.astype(np.float32)
```
